# Optimizing a Trainium2 kernel written in Bass

```python
import math
import jax, jax.numpy as jnp
from jax import lax
import numpy as np

D_MODEL = 2048
BATCH = 4
SEQ = 2048
DEPTH = 1
DEC_BATCH = 128
DEC_SEQ = 8
PAST_LEN = 16384
PAGE_SIZE = 128

D_A = D_MODEL // 2
CHUNK = 128
SGU_GROUPS = 8
SGU_HEAD = D_A // SGU_GROUPS
D_B = D_MODEL // 2
CONV_WIDTH = 31
N_MEM = 256
N_MEM_HEADS = 4
D_C = D_MODEL // 2
MEM_HEAD_DIM = D_C // N_MEM_HEADS
N_BRANCH = 3
D_FF = 4 * D_MODEL
N_IN = 2 * D_A + 2 * D_B + D_C + N_BRANCH * D_MODEL
SPLITS = [D_A, 2 * D_A, 2 * D_A + D_B, 2 * D_A + 2 * D_B, 2 * D_A + 2 * D_B + D_C]
EPS = 1e-6

kernel_name = "gated_parallel_sgu_conformer_memattn_decoder_step"


def rms_norm(x, g):
    xf = x.astype(jnp.float32)
    y = xf * lax.rsqrt(jnp.mean(xf * xf, axis=-1, keepdims=True) + EPS)
    return (y * g.astype(jnp.float32)).astype(x.dtype)


def layer_norm(x, g, b):
    xf = x.astype(jnp.float32)
    mu = jnp.mean(xf, axis=-1, keepdims=True)
    xc = xf - mu
    y = xc * lax.rsqrt(jnp.mean(xc * xc, axis=-1, keepdims=True) + EPS)
    return (y * g.astype(jnp.float32) + b.astype(jnp.float32)).astype(x.dtype)


def mem_kv(mem, mem_norm_g, w_mem_kv):
    b = mem.shape[0]
    kv = rms_norm(mem, mem_norm_g) @ w_mem_kv
    k, v = jnp.split(kv, 2, axis=-1)
    return (k.reshape(b, N_MEM, N_MEM_HEADS, MEM_HEAD_DIM),
            v.reshape(b, N_MEM, N_MEM_HEADS, MEM_HEAD_DIM))


def causal_dwconv(c_ext, conv_w, conv_b):
    out = lax.conv_general_dilated(
        c_ext, conv_w[:, None, :].astype(c_ext.dtype), window_strides=(1,), padding='VALID',
        dimension_numbers=('NWC', 'WIO', 'NWC'), feature_group_count=D_B)
    return out + conv_b


def layer(x, mem_k, mem_v, conv_hist,
          attn_norm_g, w_in, b_gate, sgu_ln_g, sgu_ln_b, sgu_w, sgu_b, w_a_out,
          conv_w, conv_b, conv_ln_g, conv_ln_b, w_b_out, w_c_out, w_o,
          mlp_norm_g, w_up, w_down):
    bsz, s, _ = x.shape
    h = rms_norm(x, attn_norm_g)
    z = h @ w_in
    u, v, glu_a, glu_b, q, gates = jnp.split(z, SPLITS, axis=-1)

    u = jax.nn.gelu(u)
    v = layer_norm(jax.nn.gelu(v), sgu_ln_g, sgu_ln_b)
    n = min(s, CHUNK)
    vc = v.reshape(bsz, s // n, n, SGU_GROUPS, SGU_HEAD)
    w_s = sgu_w[:, :n, :n] * jnp.tril(jnp.ones((n, n), dtype=sgu_w.dtype))
    mixed = jnp.einsum('gts,bcsgh->bctgh', w_s, vc) + sgu_b[:, :n].T[None, None, :, :, None]
    y_a = (u * mixed.reshape(bsz, s, D_A)) @ w_a_out

    c = glu_a * jax.nn.sigmoid(glu_b)
    c_ext = jnp.concatenate([conv_hist.astype(c.dtype), c], axis=1)
    dc = causal_dwconv(c_ext, conv_w, conv_b)
    y_b = jax.nn.silu(layer_norm(dc, conv_ln_g, conv_ln_b)) @ w_b_out
    new_hist = c_ext[:, -(CONV_WIDTH - 1):]

    qh = q.reshape(bsz, s, N_MEM_HEADS, MEM_HEAD_DIM)
    scores = jnp.einsum('bshd,bmhd->bhsm', qh, mem_k).astype(jnp.float32) / math.sqrt(MEM_HEAD_DIM)
    probs = jax.nn.softmax(scores, axis=-1).astype(mem_v.dtype)
    o = jnp.einsum('bhsm,bmhd->bshd', probs, mem_v).reshape(bsz, s, D_C)
    y_c = o @ w_c_out

    g = jax.nn.sigmoid(gates + b_gate).reshape(bsz, s, N_BRANCH, D_MODEL)
    merged = g[:, :, 0] * y_a + g[:, :, 1] * y_b + g[:, :, 2] * y_c
    x = x + merged @ w_o

    h2 = rms_norm(x, mlp_norm_g)
    x = x + jnp.square(jax.nn.relu(h2 @ w_up)) @ w_down
    return x, new_hist, v


def setup_inputs(seed: int = 0) -> dict:
    key = jax.random.key(seed)
    ks = jax.random.split(key, 32)
    f32 = jnp.float32

    def nrm(k, shape, scale):
        return jax.random.normal(k, shape, f32) * scale

    return {
        "x_prompt": nrm(ks[0], (BATCH, SEQ, D_MODEL), 1.0),
        "x_sample": nrm(ks[1], (DEC_BATCH, DEC_SEQ, D_MODEL), 1.0),
        "mem_prompt": nrm(ks[2], (BATCH, N_MEM, D_MODEL), 1.0),
        "cache_mem_k": nrm(ks[3], (DEPTH, DEC_BATCH, N_MEM, N_MEM_HEADS, MEM_HEAD_DIM), 1.0),
        "cache_mem_v": nrm(ks[4], (DEPTH, DEC_BATCH, N_MEM, N_MEM_HEADS, MEM_HEAD_DIM), 1.0),
        "state_conv": nrm(ks[5], (DEPTH, DEC_BATCH, CONV_WIDTH - 1, D_B), 0.5),
        "attn_norm_g": 1.0 + nrm(ks[6], (DEPTH, D_MODEL), 0.02),
        "w_in": nrm(ks[7], (DEPTH, D_MODEL, N_IN), D_MODEL ** -0.5),
        "b_gate": nrm(ks[8], (DEPTH, N_BRANCH * D_MODEL), 0.02),
        "sgu_ln_g": 1.0 + nrm(ks[9], (DEPTH, D_A), 0.02),
        "sgu_ln_b": nrm(ks[10], (DEPTH, D_A), 0.02),
        "sgu_w": nrm(ks[11], (DEPTH, SGU_GROUPS, CHUNK, CHUNK), CHUNK ** -0.5),
        "sgu_b": 1.0 + nrm(ks[12], (DEPTH, SGU_GROUPS, CHUNK), 0.1),
        "w_a_out": nrm(ks[13], (DEPTH, D_A, D_MODEL), D_A ** -0.5),
        "conv_w": nrm(ks[14], (DEPTH, CONV_WIDTH, D_B), CONV_WIDTH ** -0.5),
        "conv_b": nrm(ks[15], (DEPTH, D_B), 0.02),
        "conv_ln_g": 1.0 + nrm(ks[16], (DEPTH, D_B), 0.02),
        "conv_ln_b": nrm(ks[17], (DEPTH, D_B), 0.02),
        "w_b_out": nrm(ks[18], (DEPTH, D_B, D_MODEL), D_B ** -0.5),
        "mem_norm_g": 1.0 + nrm(ks[19], (DEPTH, D_MODEL), 0.02),
        "w_mem_kv": nrm(ks[20], (DEPTH, D_MODEL, 2 * D_C), D_MODEL ** -0.5),
        "w_c_out": nrm(ks[21], (DEPTH, D_C, D_MODEL), D_C ** -0.5),
        "w_o": nrm(ks[22], (DEPTH, D_MODEL, D_MODEL), D_MODEL ** -0.5),
        "mlp_norm_g": 1.0 + nrm(ks[23], (DEPTH, D_MODEL), 0.02),
        "w_up": nrm(ks[24], (DEPTH, D_MODEL, D_FF), D_MODEL ** -0.5),
        "w_down": nrm(ks[25], (DEPTH, D_FF, D_MODEL), D_FF ** -0.5),
        "final_norm_g": 1.0 + nrm(ks[26], (D_MODEL,), 0.02),
    }


def reference(x_prompt, x_sample, mem_prompt, cache_mem_k, cache_mem_v, state_conv,
              attn_norm_g, w_in, b_gate, sgu_ln_g, sgu_ln_b, sgu_w, sgu_b, w_a_out,
              conv_w, conv_b, conv_ln_g, conv_ln_b, w_b_out, mem_norm_g, w_mem_kv,
              w_c_out, w_o, mlp_norm_g, w_up, w_down, final_norm_g):
    xp, xs = x_prompt, x_sample
    mk_p, mv_p, hist_p, hist_s, chunk_v_s = [], [], [], [], []
    for l in range(DEPTH):
        lw = (attn_norm_g[l], w_in[l], b_gate[l], sgu_ln_g[l], sgu_ln_b[l], sgu_w[l], sgu_b[l],
              w_a_out[l], conv_w[l], conv_b[l], conv_ln_g[l], conv_ln_b[l], w_b_out[l],
              w_c_out[l], w_o[l], mlp_norm_g[l], w_up[l], w_down[l])
        k_p, v_p = mem_kv(mem_prompt, mem_norm_g[l], w_mem_kv[l])
        zero_hist = jnp.zeros((xp.shape[0], CONV_WIDTH - 1, D_B), xp.dtype)
        xp, hp, _ = layer(xp, k_p, v_p, zero_hist, *lw)
        xs, hs, vs = layer(xs, cache_mem_k[l], cache_mem_v[l], state_conv[l], *lw)
        mk_p.append(k_p)
        mv_p.append(v_p)
        hist_p.append(hp)
        hist_s.append(hs)
        chunk_v_s.append(vs)
    y_prompt = rms_norm(xp, final_norm_g)
    y_sample = rms_norm(xs, final_norm_g)
    new_mem_k_prompt = jnp.stack(mk_p, axis=0)
    new_mem_v_prompt = jnp.stack(mv_p, axis=0)
    new_conv_state_prompt = jnp.stack(hist_p, axis=0)
    new_conv_state_sample = jnp.stack(hist_s, axis=0)
    new_chunk_v_sample = jnp.stack(chunk_v_s, axis=0)
    return (y_prompt, y_sample, new_mem_k_prompt, new_mem_v_prompt,
            new_conv_state_prompt, new_conv_state_sample, new_chunk_v_sample)
```

```python
import numpy as np
from contextlib import ExitStack
import concourse.bass as bass
import concourse.mybir as mybir
from concourse.bass_utils import run_bass_kernel_spmd

F32 = mybir.dt.float32
BF16 = mybir.dt.bfloat16
AF = mybir.ActivationFunctionType
ALU = mybir.AluOpType
AX = mybir.AxisListType

D = 2048
NIN = 11264
TP = 512
TS = 64
T = TP + TS
H = 32
EPS = 1e-6
NSLOT = 3
SAME_SYNC = False


class Sched:
    def __init__(self, nc, es):
        self.nc = nc
        self.engs = {"pe": nc.tensor, "act": nc.scalar, "dve": nc.vector,
                     "pool": nc.gpsimd, "sp": nc.sync}
        self.names = list(self.engs)
        self.sem = {e: es.enter_context(nc.semaphore("s_" + e)) for e in self.names}
        self.cnt = {e: 0 for e in self.names}
        self.q = {e: [] for e in self.names}
        self.known = {e: {} for e in self.names}
        self.snap = {e: [None] for e in self.names}
        self.dsem = {}
        self.dsnap = {}
        self.last_w = {}
        self.readers = {}
        self.pending = {e: [] for e in self.names}
        self.es = es
        self.nwaits = 0

    def _learn(self, e, ev):
        kn = self.known[e]
        k = (ev[0], ev[1])
        if kn.get(k, 0) < ev[2]:
            kn[k] = ev[2]
        other = self.snap[ev[1]][ev[2]] if ev[0] == "e" else self.dsnap[(ev[1], ev[2])]
        for kk, v in other.items():
            if kn.get(kk, 0) < v:
                kn[kk] = v

    def _deps(self, e, reads, writes, is_dma, use_pending=True, sreads=()):
        deps = {}
        sdeps = {}
        for k in sreads:
            w = self.last_w.get(k)
            if w and w[0] == "e" and w[1] == e:
                sdeps[(w[0], w[1])] = max(sdeps.get((w[0], w[1]), 0), w[2])

        def add(ev):
            k = (ev[0], ev[1])
            if deps.get(k, 0) < ev[2]:
                deps[k] = ev[2]
        if use_pending:
            for ev in self.pending[e]:
                add(ev)
            self.pending[e] = []
        for k in reads:
            w = self.last_w.get(k)
            if w:
                add(w)
        for k in writes:
            w = self.last_w.get(k)
            if w:
                add(w)
            for rk, rv in self.readers.get(k, {}).items():
                add((rk[0], rk[1], rv))
        waits = []
        for k_, v_ in sdeps.items():
            if self.known[e].get(k_, 0) < v_:
                waits.append((k_[0], k_[1], v_))
        for (ty, nm), v in deps.items():
            if ty == "e" and nm == e and not is_dma and (e == "pe" or not SAME_SYNC):
                continue
            if self.known[e].get((ty, nm), 0) >= v:
                continue
            waits.append((ty, nm, v))
        for ev in waits:
            self._learn(e, ev)
        self.nwaits += len(waits)
        return waits

    def _record(self, ev, reads, writes):
        for k in writes:
            self.last_w[k] = ev
            self.readers[k] = {}
        for k in reads:
            r = self.readers.setdefault(k, {})
            kk = (ev[0], ev[1])
            if r.get(kk, 0) < ev[2]:
                r[kk] = ev[2]

    def op(self, e, fn, reads=(), writes=(), sreads=()):
        reads = list(reads) + list(sreads)
        bankr = [k for k in reads if isinstance(k, tuple) and k[0] == "bank"]
        if bankr:
            reads = [k for k in reads if not (isinstance(k, tuple) and k[0] == "bank")]
            writes = list(writes) + bankr
        waits = self._deps(e, reads, writes, False, sreads=sreads)
        self.cnt[e] += 1
        n = self.cnt[e]
        s = dict(self.known[e])
        s[("e", e)] = max(s.get(("e", e), 0), n - 1)
        self.snap[e].append(s)
        self._record(("e", e, n), reads, writes)
        self.q[e].append((waits, fn, None))

    def dma(self, qn, fn, semname, reads=(), writes=(), use_pending=True):
        waits = self._deps(qn, reads, writes, True, use_pending)
        if semname not in self.dsem:
            self.dsem[semname] = [self.es.enter_context(self.nc.semaphore("d_" + semname)), 0]
        h = self.dsem[semname]
        h[1] += 16
        ev = ("d", semname, h[1])
        self.dsnap[(semname, h[1])] = dict(self.known[qn])
        self._record(ev, reads, writes)
        self.q[qn].append((waits, fn, h[0]))

    def barrier(self):
        evs = [("e", f, self.cnt[f]) for f in ("pe", "act", "dve", "pool") if self.cnt[f] > 0]
        evs += [("d", nm, h[1]) for nm, h in self.dsem.items() if nm not in ("w0a", "w1a", "w2a", "w0b", "w1b", "w2b")]
        for e in self.names:
            self.pending[e] = list(evs)

    def emit(self, block):
        def run(e):
            eng = self.engs[e]
            for waits, fn, dsem in self.q[e]:
                for (ty, nm, v) in waits:
                    eng.wait_ge(self.sem[nm] if ty == "e" else self.dsem[nm][0], v)
                ins = fn()
                if dsem is None:
                    ins.then_inc(self.sem[e], 1)
                else:
                    ins.then_inc(dsem, 16)
            if e == "sp":
                for f in ("pe", "act", "dve", "pool"):
                    if self.cnt[f]:
                        eng.wait_ge(self.sem[f], self.cnt[f])
                for nm, h in self.dsem.items():
                    eng.wait_ge(h[0], h[1])

        @block.tensor
        def _(x):
            run("pe")

        @block.scalar
        def _(x):
            run("act")

        @block.vector
        def _(x):
            run("dve")

        @block.gpsimd
        def _(x):
            run("pool")

        @block.sync
        def _(x):
            run("sp")


def build_program(kstop=99, io_names=None):
    nc = bass.Bass("TRN2", target_bir_lowering=False)
    es = ExitStack()
    es.enter_context(nc.allow_non_contiguous_dma(reason="small strided parameter loads"))

    IN_SHAPES = {
        "xp": [1024, D], "xh": [H, D], "xs": [128, D], "mem": [256, D],
        "ck": [16, 256, 1024], "cv": [16, 256, 1024], "scs": [16, 30, 1024],
        "attn_g": [16, 128], "mlp_g": [16, 128], "mem_g": [16, 128], "w_in": [D, NIN], "b_gate": [48, 128],
        "sgu_ln_g": [1024], "sgu_ln_b": [1024], "sgu_w": [8, 128, 128], "sgu_b": [8, 128],
        "w_a": [1024, D], "conv_w": [31, 1024], "conv_b": [8, 128], "conv_ln_g": [8, 128], "conv_ln_b": [8, 128],
        "w_b": [1024, D], "w_kv": [D, D], "w_c": [1024, D], "w_o": [D, D], "w_up": [D, 4 * D],
        "w_down": [4 * D, D], "fin_g": [D],
    }
    OUT_SHAPES = {"yp": [1024, D], "ys": [128, D], "mk": [256, 1024], "mv": [256, 1024],
                  "csp": [30, 1024], "css": [16, 30, 1024], "cvs": [128, 1024]}

    class _IO:
        def __init__(self):
            self._t = {}

        def __getattr__(self, name):
            t = self.__dict__["_t"]
            if name not in t:
                if name in IN_SHAPES:
                    t[name] = nc.dram_tensor(name, IN_SHAPES[name], F32, kind="ExternalInput").ap()
                elif name in OUT_SHAPES:
                    t[name] = nc.dram_tensor(name, OUT_SHAPES[name], F32, kind="ExternalOutput").ap()
                else:
                    raise AttributeError(name)
            return t[name]
    I = _IO()
    if io_names is None and kstop >= 99:
        io_names = list(IN_SHAPES) + list(OUT_SHAPES)
    for n_ in list(IN_SHAPES) + list(OUT_SHAPES):
        if io_names is not None and n_ in io_names:
            getattr(I, n_)

    def sb(name, shape, dt=F32):
        return es.enter_context(nc.sbuf_tensor(name, shape, dt))

    def ps(name, shape, dt=F32):
        return es.enter_context(nc.psum_tensor(name, shape, dt))

    hT = sb("hT", [128, 16, H + T], BF16)
    xres = sb("xres", [128, 5, D])
    wring = [sb("wr%d" % i, [128, 16, 256], BF16) for i in range(NSLOT)]
    qo = sb("qo", [128, 16, T], BF16)
    uT = sb("uT", [128, 8, T], BF16)
    cT = sb("cT", [128, 8, H + TP])
    cS = sb("cS", [128, 8, 8, 38])
    cbT = sb("cbT", [128, 8, T], BF16)
    mT = sb("mT", [128, 16, T], BF16)
    KT = sb("KT", [128, 8, 256], BF16)
    Vp = sb("Vp", [128, 2, 1024], BF16)
    bc = sb("bc", [128, D])
    xsb = [sb("xsb%d" % i, [128, D], BF16) for i in range(2)]
    ident_bf = sb("ident_bf", [128, 128], BF16)
    ident_f = sb("ident_f", [128, 128])
    trimask = sb("trimask", [128, 128])
    pp = sb("pp", [128, 128])
    cw = sb("cw", [128, 8, 32])
    WsT = sb("WsT", [128, 8, 128], BF16)
    WsTs = sb("WsTs", [64, 8, 64], BF16)
    biasbc = sb("biasbc", [128, 8, 128])
    maskq = sb("maskq", [128, 8, 64], BF16)
    epsb = sb("epsb", [128, 1])
    stats = sb("stats", [128, 1024])
    rlb = [sb("rl%d" % i, [128, T]) for i in range(2)]

    W0_ = ps("W0", [128, 1024]); W1_ = ps("W1", [128, 1024])
    TBw = ps("TBw", [128, 2048], BF16)
    W3_ = ps("W3", [128, 1024])
    W = [W0_, W1_, None, W3_]

    sc = Sched(nc, es)
    V, S, G, PE, SP = nc.vector, nc.scalar, nc.gpsimd, nc.tensor, nc.sync

    def bk(i):
        return [("bank", i)]

    def bank(i):
        assert i in (0, 1, 2, 3, 6, 7)
        return W[i // 2][:, (i % 2) * 512:(i % 2) * 512 + 512]

    TB = TBw[:, 0:1024]
    W3 = W[3]
    mT_f = mT[:, :, :].rearrange("p c t -> p (c t)").bitcast(F32).rearrange("p (c t) -> p c t", c=8)

    def cells(name, chunks, tiles):
        return [(name, c, t) for c in chunks for t in tiles]

    ALLT = range(5)
    stat_ctr = [0]

    def stat(n=16):
        i = stat_ctr[0] % 32
        stat_ctr[0] += 1
        return stats[:, i * 32:i * 32 + 32], ("stat", i)

    scr_off = [0]
    scr_gen = [0]
    xres_flat = xres[:, :, :].rearrange("p a b -> p (a b)")

    def scr_reset():
        import os as _os
        if not _os.environ.get("KNOB_NOBAR"):
            sc.barrier()
        scr_off[0] = 0
        scr_gen[0] += 1

    def scratch(nelem, dt=F32):
        n32 = nelem if dt == F32 else (nelem + 1) // 2
        n32 = (n32 + 7) // 8 * 8
        o = scr_off[0]
        assert o + n32 <= 5 * D, "scratch overflow"
        scr_off[0] += n32
        ap = xres_flat[:, o:o + n32]
        if dt != F32:
            ap = ap.bitcast(dt)[:, 0:nelem]
        else:
            ap = ap[:, 0:nelem]
        return ap, ("scr", scr_gen[0], o)

    def mm_group(mms, reads, writes):
        def fn():
            ins = None
            for (o, l, r, st, sp_) in mms:
                ins = PE.matmul(o, lhsT=l, rhs=r, start=st, stop=sp_)
            return ins
        sc.op("pe", fn, reads, writes)

    def tr_group(trs, reads, writes):
        def fn():
            ins = None
            for (o, i_, idn) in trs:
                ins = PE.transpose(out=o, in_=i_, identity=idn)
            o, i_, idn = trs[-1]
            if i_.dtype == F32:
                ins = PE.transpose(out=o, in_=i_, identity=idn)
            return ins
        sc.op("pe", fn, reads, writes)

    wcnt = [0]

    def wblock(wdram, r0, kcn, c0, ncol=256):
        s = wcnt[0] % NSLOT
        wcnt[0] += 1
        slot = wring[s]
        if ncol != 256:
            assert kcn * ncol <= 16 * 256
            slot = slot[:, :, :].rearrange("p a b -> p (a b)")[:, 0:kcn * ncol].rearrange("p (k n) -> p k n", k=kcn)
        src = wdram[r0:r0 + 128 * kcn, c0:c0 + ncol].rearrange("(k p) n -> p k n", p=128)
        hk = max(1, kcn // 2)
        keys = []
        for hi, (k0, k1) in enumerate([(0, hk), (hk, kcn)]):
            if k1 <= k0:
                continue
            key = ("wslot", s, hi)
            keys.append(key)
            sc.dma("pool", (lambda k0=k0, k1=k1: G.dma_start(out=slot[:, k0:k1, 0:ncol], in_=src[:, k0:k1, :])),
                   "w%d%s" % (s, "ab"[hi]), writes=[key], use_pending=False)
        return slot, keys

    acc_ctr = [0]
    ss_ctr = [0]

    def proj_fm(slot, skey, kcn, nchunks, rhs_t, rhs_keys, segs, epilogue):
        for cl in range(nchunks):
            outs = []
            for grp in ("p", "small"):
                mms = []
                wrs = []
                for (sn, c0, n) in segs:
                    if (sn == "p") != (grp == "p"):
                        continue
                    if sn == "p":
                        b = acc_ctr[0] % 3
                        acc_ctr[0] += 1
                        o = bank(b)[:, 0:n]
                        wr = bk(b)
                    else:
                        off = 0 if sn == "s" else 64
                        o = bank(3)[:, off:off + n]
                        wr = bk(3)
                    outs.append((sn, o, wr))
                    wrs += wr
                    for k in range(kcn):
                        mms.append((o, slot[:, k, cl * 128:(cl + 1) * 128], rhs_t[:, k, c0:c0 + n],
                                    k == 0, k == kcn - 1))
                if mms:
                    mm_group(mms, list(skey) + list(rhs_keys), list(dict.fromkeys(wrs)))
            for (sn, o, wr) in outs:
                epilogue(cl, sn, o, wr)

    tm_ctr = [0]

    def proj_tm(slot, skey, kcn, lhs_t, lhs_keys_fn, col0, tiles, epilogue, ncol=256):
        par = tm_ctr[0] % 2
        tm_ctr[0] += 1
        for (ti, tc0, rows) in tiles:
            bi = ti if ti < 4 else 6
            o = bank(bi)[0:rows, 0:ncol]
            wr = bk(bi)
            mms = [(o, lhs_t[:, k, tc0:tc0 + rows], slot[:, k, 0:ncol], k == 0, k == kcn - 1)
                   for k in range(kcn)]
            mm_group(mms, list(skey) + lhs_keys_fn(ti), wr)
            epilogue(ti, rows, o, wr)

    def proj_tm512(wdram, r0, c0, lhs_t, lhs_keys_fn, tiles, epilogue):
        blocks = [wblock(wdram, r0 + 1024 * hb, 8, c0, ncol=512) for hb in range(2)]
        outs = {}
        for hb, (slot, skey) in enumerate(blocks):
            for (ti, tc0, rows) in tiles:
                bi = ti if ti < 4 else 6
                o = bank(bi)[0:rows, 0:512]
                outs[ti] = (o, bk(bi), rows)
                mms = [(o, lhs_t[:, 8 * hb + k, tc0:tc0 + rows], slot[:, k, 0:512], hb == 0 and k == 0,
                        hb == 1 and k == 7) for k in range(8)]
                mm_group(mms, list(skey) + lhs_keys_fn(ti), bk(bi))
        for (ti, tc0, rows) in tiles:
            o, wr, rows = outs[ti]
            epilogue(ti, rows, o, wr)

    def wide_keys(i):
        return bk(2 * i) + bk(2 * i + 1)

    def setup():
        sc.op("pool", lambda: G.memset(ident_bf[:], 1.0), writes=["ident_bf"], sreads=["ident_bf"])
        sc.op("pool", lambda: G.affine_select(out=ident_bf[:], in_=ident_bf[:], pattern=[[-1, 128]],
                                              compare_op=ALU.is_equal, fill=0.0, base=0, channel_multiplier=1),
              writes=["ident_bf"], sreads=["ident_bf"])
        sc.op("pool", lambda: G.memset(ident_f[:], 1.0), writes=["ident_f"], sreads=["ident_f"])
        sc.op("pool", lambda: G.affine_select(out=ident_f[:], in_=ident_f[:], pattern=[[-1, 128]],
                                              compare_op=ALU.is_equal, fill=0.0, base=0, channel_multiplier=1),
              writes=["ident_f"], sreads=["ident_f"])
        sc.op("pool", lambda: G.memset(trimask[:], 1.0), writes=["trimask"], sreads=["trimask"])
        sc.op("pool", lambda: G.affine_select(out=trimask[:], in_=trimask[:], pattern=[[1, 128]],
                                              compare_op=ALU.is_ge, fill=0.0, base=0, channel_multiplier=-1),
              writes=["trimask"], sreads=["trimask"])
        sc.op("pool", lambda: G.memset(epsb[:], EPS), writes=["epsb"])
        sc.op("pool", lambda: G.memset(stats[:], 0.0), writes=[("stat", i) for i in range(64)])
        sc.op("pool", lambda: G.memset(maskq[:], 0.0), writes=["maskq"], sreads=["maskq"])
        for b in range(8):
            sc.op("pool", (lambda b=b: G.memset(maskq[:, b, 8 * b:8 * b + 8], 1.0)), writes=["maskq"], sreads=["maskq"])

        prm_t, _ = scratch(128)
        rows = [(I.attn_g, 0, 16), (I.mlp_g, 16, 16), (I.mem_g, 32, 16), (I.b_gate, 48, 48),
                (I.conv_b, 96, 8), (I.conv_ln_g, 104, 8), (I.conv_ln_b, 112, 8)]
        for i, (src, r0, n) in enumerate(rows):
            sc.dma("sp", (lambda src=src, r0=r0, n=n: SP.dma_start(out=prm_t[r0:r0 + n, :], in_=src[:, :])),
                   "prm%d" % i, writes=["prm"])
        tr_group([(W3[:, 0:120], prm_t[0:120, :], ident_f[0:120, 0:120])], ["prm", "ident_f"], wide_keys(3))
        sc.op("dve", lambda: V.tensor_copy(out=pp[:, 0:120], in_=W3[:, 0:120]), wide_keys(3), ["pp"])
        cwst, _ = scratch(1024)
        sc.dma("sp", lambda: SP.dma_start(out=cwst[0:31, :], in_=I.conv_w[:, :]), "cwst", writes=["cwst"])
        tr_group([(W3[:, j * 32:j * 32 + 31], cwst[0:31, j * 128:(j + 1) * 128], ident_f[0:31, 0:31])
                  for j in range(8)], ["cwst", "ident_f"], wide_keys(3))
        sc.op("dve", lambda: V.tensor_copy(out=cw[:, :, 0:31],
                                           in_=W3[:, 0:256].rearrange("p (j t) -> p j t", j=8)[:, :, 0:31]),
              wide_keys(3), ["cw"])
        wnat, _ = scratch(1024)
        wn3 = wnat.rearrange("p (g s) -> p g s", g=8)
        sc.dma("sp", lambda: SP.dma_start(out=wn3, in_=I.sgu_w.rearrange("g t s -> t g s")), "wnat", writes=["wnat"])
        tr_group([(W3[:, g * 128:(g + 1) * 128], wn3[:, g, :], ident_f[:, :]) for g in range(8)],
                 ["wnat", "ident_f"], wide_keys(3))
        sc.op("dve", lambda: V.tensor_tensor(out=WsT[:, :, :], in0=W3[:, :].rearrange("p (g t) -> p g t", g=8),
                                             in1=trimask[:, :].unsqueeze(1).to_broadcast([128, 8, 128]), op=ALU.mult),
              wide_keys(3) + ["trimask"], ["WsT"])
        a32, _ = scratch(64)
        abf, _ = scratch(64, BF16)
        ebf, _ = scratch(64, BF16)
        mblk, _ = scratch(64)
        for g in range(8):
            sc.dma("sp", (lambda g=g: SP.dma_start(out=a32[0:8, g * 8:(g + 1) * 8],
                                                   in_=I.sgu_w[g, 0:8, 0:8].rearrange("t s -> s t"))),
                   "wssg%d" % g, writes=[("a32", g)])
        sc.op("dve", lambda: V.tensor_copy(out=abf[0:8, :], in_=a32[0:8, :]), [("a32", g) for g in range(8)], ["abf"])
        sc.op("pool", lambda: G.memset(ebf[0:8, :], 1.0), writes=["ebf"], sreads=["ebf"])
        sc.op("pool", lambda: G.affine_select(out=ebf[0:8, :].rearrange("p (b j) -> p b j", b=8),
                                              in_=ebf[0:8, :].rearrange("p (b j) -> p b j", b=8),
                                              pattern=[[0, 8], [1, 8]], compare_op=ALU.is_equal, fill=0.0,
                                              base=0, channel_multiplier=-1), writes=["ebf"], sreads=["ebf"])
        sc.op("pool", lambda: G.memset(mblk[0:64, :], 1.0), writes=["mblk"], sreads=["mblk"])
        sc.op("pool", lambda: G.affine_select(out=mblk[0:64, :].rearrange("p (b t) -> p b t", b=8),
                                              in_=mblk[0:64, :].rearrange("p (b t) -> p b t", b=8),
                                              pattern=[[8, 8], [1, 8]], compare_op=ALU.is_ge, fill=0.0,
                                              base=0, channel_multiplier=-1), writes=["mblk"], sreads=["mblk"])
        sc.op("pool", lambda: G.affine_select(out=mblk[0:64, :].rearrange("p (b t) -> p b t", b=8),
                                              in_=mblk[0:64, :].rearrange("p (b t) -> p b t", b=8),
                                              pattern=[[-8, 8], [0, 8]], compare_op=ALU.is_ge, fill=0.0,
                                              base=0, channel_multiplier=1), writes=["mblk"], sreads=["mblk"])
        mm_group([(W3[0:64, 0:64], ebf[0:8, 0:64], abf[0:8, 0:64], True, True)], ["ebf", "abf"], wide_keys(3))
        sc.op("dve", lambda: V.tensor_tensor(
            out=WsTs[:, :, :].rearrange("p g (b t) -> p g b t", b=8),
            in0=W3[0:64, 0:64].rearrange("p (g t) -> p g t", g=8).unsqueeze(2).to_broadcast([64, 8, 8, 8]),
            in1=mblk[0:64, :].rearrange("p (b t) -> p b t", b=8).unsqueeze(1).to_broadcast([64, 8, 8, 8]),
            op=ALU.mult), wide_keys(3) + ["mblk"], ["WsTs"])
        sc.dma("sp", lambda: SP.dma_start(out=biasbc[:, :, :], in_=I.sgu_b.partition_broadcast(128)), "biasbc",
               writes=["biasbc"])

    xsb_ctr = [0]

    def norm_T(src, src_keys, rows, gcol, dst_fn, dst_keys):
        i = xsb_ctr[0] % 2
        xsb_ctr[0] += 1
        xb = xsb[i]
        xk = ("xsb", i)
        st, sk = stat()
        sc.op("dve", lambda: V.memset(st[:, 0:4], 0.0), writes=[sk])
        sc.op("act", lambda: S.activation(out=xb[0:rows, :], in_=src, func=AF.Square,
                                          accum_out=st[0:rows, 0:1]), list(src_keys), [sk, xk])
        sc.op("act", lambda: S.activation(out=st[0:rows, 1:2], in_=st[0:rows, 0:1], func=AF.Sqrt,
                                          scale=1.0 / D, bias=epsb[0:rows, 0:1]), ["epsb"], [sk], sreads=[sk])
        sc.op("dve", lambda: V.reciprocal(out=st[0:rows, 2:3], in_=st[0:rows, 1:2]), [sk], [sk])
        sc.op("dve", lambda: V.tensor_scalar_mul(out=xb[0:rows, :], in0=src, scalar1=st[0:rows, 2:3]),
              list(src_keys), [xk], sreads=[sk])
        tr_group([(TBw[:, k * 128:k * 128 + rows], xb[0:rows, k * 128:(k + 1) * 128], ident_bf[0:rows, 0:rows])
                  for k in range(16)], [xk, "ident_bf"], bk(4) + bk(5))
        src_ps = TBw[:, :].rearrange("p (k t) -> p k t", k=16)
        dst = dst_fn()
        dk = list(dst_keys)
        for k in range(16):
            if k < 8:
                sc.op("act", (lambda k=k: S.activation(out=dst[:, k, :], in_=src_ps[:, k, 0:rows], func=AF.Copy,
                                                       scale=pp[:, gcol + k:gcol + k + 1])),
                      bk(4) + ["pp"], dk)
            else:
                sc.op("dve", (lambda k=k: V.tensor_scalar_mul(out=dst[:, k, :], in0=src_ps[:, k, 0:rows],
                                                              scalar1=pp[:, gcol + k:gcol + k + 1])),
                      bk(5) + ["pp"], dk)

    def ln_stats(src0, src1, rows, src_keys):
        st, sk = stat()

        sk2 = (sk, "b")
        sc.op("dve", lambda: V.bn_stats(out=st[0:rows, 0:6], in_=src0), list(src_keys), [sk])
        sc.op("dve", lambda: V.bn_stats(out=st[0:rows, 16:22], in_=src1), list(src_keys), [sk2])
        sc.op("dve", lambda: V.bn_aggr(out=st[0:rows, 6:8], in_=st[0:rows, 0:6]), [], [sk], sreads=[sk])
        sc.op("dve", lambda: V.bn_aggr(out=st[0:rows, 8:10], in_=st[0:rows, 16:22]), [], [sk2], sreads=[sk2])
        sc.op("dve", lambda: V.tensor_tensor(out=st[0:rows, 10:12], in0=st[0:rows, 6:8], in1=st[0:rows, 8:10],
                                             op=ALU.add), [], [sk], sreads=[sk, sk2])
        sc.op("dve", lambda: V.tensor_scalar_mul(out=st[0:rows, 12:14], in0=st[0:rows, 10:12], scalar1=0.5),
              [], [sk], sreads=[sk])
        sc.op("dve", lambda: V.tensor_tensor(out=st[0:rows, 10:11], in0=st[0:rows, 6:7], in1=st[0:rows, 8:9],
                                             op=ALU.subtract), [], [sk], sreads=[sk])
        sc.op("dve", lambda: V.tensor_tensor(out=st[0:rows, 11:12], in0=st[0:rows, 10:11], in1=st[0:rows, 10:11],
                                             op=ALU.mult), [], [sk], sreads=[sk])
        sc.op("dve", lambda: V.scalar_tensor_tensor(out=st[0:rows, 13:14], in0=st[0:rows, 11:12], scalar=0.25,
                                                    in1=st[0:rows, 13:14], op0=ALU.mult, op1=ALU.add),
              [], [sk], sreads=[sk])
        sc.op("act", lambda: S.activation(out=st[0:rows, 14:15], in_=st[0:rows, 13:14], func=AF.Sqrt,
                                          scale=1.0, bias=epsb[0:rows, 0:1]), [sk, "epsb"], [sk])
        sc.op("dve", lambda: V.reciprocal(out=st[0:rows, 15:16], in_=st[0:rows, 14:15]), [sk], [sk])
        return st, sk

    def mem_kv():
        scr_reset()
        for mt in range(2):
            xst, xk = scratch(D)
            sc.dma("sp", (lambda mt=mt, xst=xst: SP.dma_start(out=xst[:, :], in_=I.mem[mt * 128:(mt + 1) * 128, :])),
                   "xst%d" % mt, writes=[xk])
            norm_T(xst[:, :], [xk], 128, 32, (lambda mt=mt: hT[:, :, mt * 128:(mt + 1) * 128]),
                   cells("hT", range(16), [mt]))
        if kstop <= 0.3:
            return
        kvst = [scratch(512) for _ in range(2)]
        for jb in range(8):
            slot, skey = wblock(I.w_kv, 0, 16, jb * 256)
            if kstop <= 0.5:
                sc.op("dve", lambda: V.tensor_copy(out=KT[:, 0, 0:256], in_=slot[:, 0, 0:256]), list(skey), ["KT"])
                return
            if jb < 4:
                def epi(cl, sn, o, okey, jb=jb):
                    c = 2 * jb + cl
                    sc.op("act", lambda: S.activation(out=KT[:, c, :], in_=o, func=AF.Copy), okey, ["KT"])
                proj_fm(slot, skey, 16, 2, hT, cells("hT", range(16), [0, 1]), [("p", 0, 256)], epi)
            if kstop <= 0.6:
                return
            ks, kk = kvst[jb % 2]
            ks3 = ks.rearrange("p (m c) -> p m c", m=2)

            def epi_tm(ti, rows, o, okeys, jb=jb, ks3=ks3, kk=kk):
                if jb < 4:
                    sc.op("dve", lambda: V.tensor_copy(out=ks3[:, ti, :], in_=o), okeys, [(kk, ti)])
                else:
                    sc.op("dve", lambda: V.tensor_copy(out=ks3[:, ti, :], in_=o), okeys, [(kk, ti)])
                    sc.op("act", lambda: S.activation(out=Vp[:, ti, (jb - 4) * 256:(jb - 3) * 256], in_=o,
                                                      func=AF.Copy), okeys, ["Vp"])
            import os as _os
            kd = _os.environ.get("KDBG", "")
            if kd == "noepi":
                epi_tm = lambda ti, rows, o, okeys: None
            proj_tm(slot, skey, 16, hT, lambda ti: cells("hT", range(16), [ti]), 0,
                    [(0, 0, 128)] if kd == "t0" else [(0, 0, 128), (1, 128, 128)], epi_tm)
            if kstop <= 0.7:
                return
            dst = I.mk if jb < 4 else I.mv
            c0 = (jb % 4) * 256
            import os as _os
            kdbg = _os.environ.get("KDBG", "")
            for m_ in range(2):
                if kdbg == "actq":
                    sc.dma("act", (lambda dst=dst, c0=c0, ks3=ks3, m_=m_: S.dma_start(
                        out=dst[m_ * 128:(m_ + 1) * 128, c0:c0 + 256], in_=ks3[:, m_, :])),
                        "kvst%d_%d" % (jb % 2, m_), reads=[(kk, m_)])
                elif kdbg == "pp":
                    sc.dma("sp", (lambda dst=dst, c0=c0, m_=m_: SP.dma_start(
                        out=dst[m_ * 128:(m_ + 1) * 128, c0:c0 + 128], in_=pp[:, :])),
                        "kvst%d_%d" % (jb % 2, m_), reads=["pp"])
                elif kdbg == "nodep":
                    sc.dma("sp", (lambda dst=dst, c0=c0, ks3=ks3, m_=m_: SP.dma_start(
                        out=dst[m_ * 128:(m_ + 1) * 128, c0:c0 + 256], in_=ks3[:, m_, :])),
                        "kvst%d_%d" % (jb % 2, m_), reads=[])
                else:
                    sc.dma("sp", (lambda dst=dst, c0=c0, ks3=ks3, m_=m_: SP.dma_start(
                        out=dst[m_ * 128:(m_ + 1) * 128, c0:c0 + 256], in_=ks3[:, m_, :])),
                        "kvst%d_%d" % (jb % 2, m_), reads=[(kk, m_)])
            if kstop <= 0.8:
                return

    def run_pass(p):
        pc0 = HO = H
        sc0 = H + TP
        hsegs = [("p", HO, TP), ("s", sc0, TS)]
        hkeys_all = cells("hT", range(16), range(5))

        scr_reset()
        xstage = [scratch(D), scratch(D)]
        tl = [("h", (I.xh[:, :] if p == 0 else I.xp[480:512, :]), 32, 0, 5)]
        for i in range(4):
            r0 = p * 512 + i * 128
            tl.append(("p", I.xp[r0:r0 + 128, :], 128, HO + 128 * i, i))
        tl.append(("s", I.xs[p * 64:(p + 1) * 64, :], 64, sc0, 4))
        for n, (kind, src, rows, col, ti) in enumerate(tl):
            xst, xk = xstage[n % 2]
            sc.dma("sp", (lambda xst=xst, src=src, rows=rows: SP.dma_start(out=xst[0:rows, :], in_=src)),
                   "xst%d" % (n % 2), writes=[xk])
            norm_T(xst[0:rows, :], [xk], rows, 0, (lambda col=col, rows=rows: hT[:, :, col:col + rows]),
                   cells("hT", range(16), [ti]))

        if kstop <= 2 + 10 * p:
            return
        scr_reset()
        sg, sgk = scratch(2 * (H + T))
        sg3 = sg.rearrange("p (c t) -> p c t", c=2)
        scst, scstk = scratch(1024)
        gsegs = [("h", 0, H), ("p", HO, TP), ("s", sc0, TS)]
        import os as _os
        for g4 in range(2):
            if _os.environ.get("KDBG", "") in ("g0", "g0l") and g4 == 1:
                break
            s0 = p * 8 + g4 * 4
            sc.dma("sp", (lambda s0=s0: SP.dma_start(out=scst[0:120, :],
                                                     in_=I.scs[s0:s0 + 4].rearrange("b r f -> (b r) f"))),
                   "scst", writes=[scstk])
            if _os.environ.get("KDBG", "") == "g0l":
                break
            tr_group([(W3[:, j * 128:j * 128 + 120], scst[0:120, j * 128:(j + 1) * 128], ident_f[0:120, 0:120])
                      for j in range(8)], [scstk, "ident_f"], wide_keys(3))
            import os as _os
            for bb in range(4):
                if _os.environ.get("KDBG", "") == "nocsh":
                    break
                sc.dma("sp", (lambda s0=s0, bb=bb: SP.dma_start(out=I.css[s0 + bb, 0:22, :],
                                                               in_=scst[bb * 30 + 8:bb * 30 + 30, :])),
                       "csh%d" % bb, reads=[scstk])
            if _os.environ.get("KDBG", "") == "nocp":
                continue
            sc.op("dve", (lambda g4=g4: V.tensor_copy(
                out=cS[:, :, 4 * g4:4 * g4 + 4, 0:30],
                in_=W3[:, :].rearrange("p (j t) -> p j t", j=8)[:, :, 0:120].rearrange("p j (b r) -> p j b r", b=4))),
                wide_keys(3), cells("cS", range(8), [0]))
        for jb in range(4):
            slot, skey = wblock(I.w_in, 0, 16, 3072 + jb * 256)

            def epi_b(cl, sn, o, okey):
                lo = {"h": 0, "p": H, "s": H + TP}[sn]
                n = {"h": H, "p": TP, "s": TS}[sn]
                sc.op("act", lambda: S.activation(out=sg3[:, cl, lo:lo + n], in_=o, func=AF.Sigmoid),
                      okey, [(sgk, cl, sn)])
            proj_fm(slot, skey, 16, 2, hT, cells("hT", range(16), range(6)), gsegs, epi_b)
            slot, skey = wblock(I.w_in, 0, 16, 2048 + jb * 256)

            def epi_a(cl, sn, o, okey, jb=jb):
                j = 2 * jb + cl
                if sn == "h":
                    sc.op("dve", lambda: V.tensor_tensor(out=cT[:, j, 0:H], in0=o, in1=sg3[:, cl, 0:H], op=ALU.mult),
                          okey + [(sgk, cl, sn)], cells("cT", [j], [0]))
                elif sn == "p":
                    sc.op("dve", lambda: V.tensor_tensor(out=cT[:, j, H:H + TP], in0=o, in1=sg3[:, cl, H:H + TP],
                                                         op=ALU.mult),
                          okey + [(sgk, cl, sn)], cells("cT", [j], [0]))
                else:
                    sc.op("dve", lambda: V.tensor_tensor(
                        out=cS[:, j, :, 30:38], in0=o.rearrange("p (b t) -> p b t", b=8),
                        in1=sg3[:, cl, H + TP:H + T].rearrange("p (b t) -> p b t", b=8), op=ALU.mult),
                        okey + [(sgk, cl, sn)], cells("cS", [j], [0]))
            proj_fm(slot, skey, 16, 2, hT, cells("hT", range(16), range(6)), gsegs, epi_a)

        def conv_chunk(j):
            def conv_p():
                dc = mT_f[:, j, 0:TP]
                ins = V.tensor_scalar(out=dc, in0=cT[:, j, 2:2 + TP], scalar1=cw[:, j, 0:1],
                                      scalar2=pp[:, 96 + j:97 + j], op0=ALU.mult, op1=ALU.add)
                for t in range(1, 31):
                    ins = V.scalar_tensor_tensor(out=dc, in0=cT[:, j, 2 + t:2 + t + TP],
                                                 scalar=cw[:, j, t:t + 1], in1=dc, op0=ALU.mult, op1=ALU.add)
                return ins
            sc.op("dve", conv_p, cells("cT", [j], [0]) + ["cw", "pp"], cells("mT", [2 * j, 2 * j + 1], range(4)))

            dcs = mT_f[:, j, TP:T].rearrange("p (b t) -> p b t", b=8)
            ck = cells("mT", [2 * j, 2 * j + 1], [4])
            sc.op("dve", lambda: V.tensor_scalar(out=dcs, in0=cS[:, j, :, 0:8], scalar1=cw[:, j, 0:1],
                                                 scalar2=pp[:, 96 + j:97 + j], op0=ALU.mult, op1=ALU.add),
                  cells("cS", [j], [0]) + ["cw", "pp"], ck)
            for t in range(1, 31):
                sc.op("dve", (lambda t=t: V.scalar_tensor_tensor(out=dcs, in0=cS[:, j, :, t:t + 8],
                                                                 scalar=cw[:, j, t:t + 1], in1=dcs,
                                                                 op0=ALU.mult, op1=ALU.add)),
                      cells("cS", [j], [0]) + ["cw", "pp"], ck, sreads=ck)

        scr_reset()
        for jb in range(4):
            slot, skey = wblock(I.w_in, 0, 16, 4096 + jb * 256)

            def epi(cl, sn, o, okey, jb=jb):
                c = 2 * jb + cl
                if sn == "p":
                    sc.op("act", lambda: S.activation(out=qo[:, c, 0:TP], in_=o, func=AF.Copy), okey,
                          cells("qo", [c], range(4)))
                else:
                    sc.op("act", lambda: S.activation(out=qo[:, c, TP:T], in_=o, func=AF.Copy), okey,
                          cells("qo", [c], [4]))
            proj_fm(slot, skey, 16, 2, hT, hkeys_all, hsegs, epi)
            if jb in (1, 3):
                conv_chunk(jb // 2)

        p32, p32k = scratch(1024)
        pbf, pbfk = scratch(1024, BF16)
        pT, pTk = scratch(1024, BF16)

        def softmax(rows, heads, wk):
            st, sk = stat()
            for h in range(4):
                sc.op("dve", (lambda h=h: V.tensor_reduce(out=st[0:rows, h:h + 1], in_=heads[h],
                                                          axis=AX.X, op=ALU.max)), wk, [sk])
            sc.op("dve", lambda: V.tensor_scalar_mul(out=st[0:rows, 4:8], in0=st[0:rows, 0:4], scalar1=-1.0 / 16),
                  [], [sk], sreads=[sk])
            sc.op("dve", lambda: V.memset(st[0:rows, 8:12], 0.0), [], [sk], sreads=[sk])
            for h in range(4):
                sc.op("act", (lambda h=h: S.activation(out=p32[0:rows, h * 256:(h + 1) * 256],
                                                       in_=heads[h], func=AF.Exp,
                                                       bias=st[0:rows, 4 + h:5 + h], scale=1.0 / 16,
                                                       accum_out=st[0:rows, 8 + h:9 + h])),
                      wk + [sk], [sk, p32k])
            sc.op("dve", lambda: V.reciprocal(out=st[0:rows, 12:16], in_=st[0:rows, 8:12]), [sk], [sk])
            for h in range(4):
                sc.op("dve", (lambda h=h: V.tensor_scalar_mul(out=pbf[0:rows, h * 256:(h + 1) * 256],
                                                              in0=p32[0:rows, h * 256:(h + 1) * 256],
                                                              scalar1=st[0:rows, 12 + h:13 + h])),
                      [p32k], [pbfk], sreads=[sk])
            tr_group([(TB[:, j * 128:j * 128 + rows], pbf[0:rows, j * 128:(j + 1) * 128],
                       ident_bf[0:rows, 0:rows]) for j in range(8)], [pbfk, "ident_bf"], bk(4))
            sc.op("act", lambda: S.activation(
                out=pT.rearrange("p (j t) -> p j t", j=8)[:, :, 0:rows],
                in_=TB.rearrange("p (j t) -> p j t", j=8)[:, :, 0:rows], func=AF.Copy), bk(4), [pTk])

        for i in range(4):
            cols = slice(128 * i, 128 * i + 128)
            mms = []
            for h in range(4):
                for dc in range(2):
                    c = 2 * h + dc
                    mms.append((W3[:, h * 256:(h + 1) * 256], qo[:, c, cols], KT[:, c, :], dc == 0, dc == 1))
            mm_group(mms, cells("qo", range(8), [i]) + ["KT"], wide_keys(3))
            softmax(128, [W3[:, h * 256:(h + 1) * 256] for h in range(4)], wide_keys(3))
            mms = []
            for h in range(4):
                for dc in range(2):
                    c = 2 * h + dc
                    for mc in range(2):
                        mms.append((W[1][:, c * 128:(c + 1) * 128],
                                    Vp[:, mc, h * 256 + dc * 128:h * 256 + dc * 128 + 128],
                                    pT[:, (2 * h + mc) * 128:(2 * h + mc) * 128 + 128], mc == 0, mc == 1))
            mm_group(mms, [pTk, "Vp"], wide_keys(1))
            sc.op("dve", (lambda cols=cols: V.tensor_copy(out=qo[:, 8:16, cols],
                                                         in_=W[1][:, :].rearrange("p (c t) -> p c t", c=8))),
                  wide_keys(1), cells("qo", range(8, 16), [i]))

        shb = [2, 3, 6, 7]
        shk = bk(2) + bk(3) + bk(6) + bk(7)
        Kst = scratch(2048)
        KsT = [scratch(2048, BF16) for _ in range(2)]
        Vs = [scratch(2048, BF16) for _ in range(2)]
        qmb = [scratch(512, BF16) for _ in range(2)]
        for b in range(8):
            seq = p * 8 + b
            kst, kstk = Kst
            kst3 = kst.rearrange("p (m f) -> p m f", m=2)
            sc.dma("sp", (lambda seq=seq, kst3=kst3: SP.dma_start(
                out=kst3, in_=I.ck[seq].rearrange("(m p) f -> p m f", p=128))), "kst", writes=[kstk])
            kt, ktk = KsT[b % 2]
            kt3 = kt.rearrange("p (c m) -> p c m", c=8)
            for half in range(2):
                trs = []
                Wk = W[0]
                wkk = wide_keys(0)
                for cc in range(4):
                    c = half * 4 + cc
                    for mc in range(2):
                        trs.append((Wk[:, cc * 256 + mc * 128:cc * 256 + mc * 128 + 128],
                                    kst3[:, mc, c * 128:(c + 1) * 128], ident_f[:, :]))
                tr_group(trs, [kstk, "ident_f"], wkk)
                if half == 0:
                    sc.op("act", (lambda kt3=kt3: S.activation(
                        out=kt3[:, 0:4, :], in_=W[0][:, :].rearrange("p (c m) -> p c m", c=4), func=AF.Copy)),
                        wkk, [(ktk, 0)])
                else:
                    sc.op("dve", (lambda kt3=kt3: V.tensor_copy(
                        out=kt3[:, 4:8, :], in_=W[0][:, :].rearrange("p (c m) -> p c m", c=4))),
                        wkk, [(ktk, 1)])
            qm, qmk = qmb[b % 2]
            qm3 = qm.rearrange("p (c t) -> p c t", c=8)
            sc.op("dve", (lambda qm3=qm3, b=b: V.tensor_tensor(
                out=qm3, in0=qo[:, 0:8, TP:T],
                in1=maskq[:, b, :].unsqueeze(1).to_broadcast([128, 8, 64]), op=ALU.mult)),
                cells("qo", range(8), [4]) + ["maskq"], [qmk])
            mms = []
            for h in range(4):
                for dc in range(2):
                    c = 2 * h + dc
                    mms.append((bank(shb[h])[0:64, 0:256], qm3[:, c, :], kt3[:, c, :],
                                b == 0 and dc == 0, b == 7 and dc == 1))
            mm_group(mms, [qmk, (ktk, 0), (ktk, 1)], shk)
        softmax(64, [bank(shb[h])[0:64, 0:256] for h in range(4)], shk)
        pT3 = pT.rearrange("p (j t) -> p j t", j=8)
        for b in range(8):
            seq = p * 8 + b
            vs_, vsk = Vs[b % 2]
            vs3 = vs_.rearrange("p (m f) -> p m f", m=2)
            sc.dma("pool", (lambda seq=seq, vs3=vs3: G.dma_start(
                out=vs3, in_=I.cv[seq].rearrange("(m p) f -> p m f", p=128))), "vs%d" % (b % 2), writes=[vsk])
            mms = []
            for h in range(4):
                for dc in range(2):
                    c = 2 * h + dc
                    for mc in range(2):
                        mms.append((bank(0)[:, c * 64 + 8 * b:c * 64 + 8 * b + 8],
                                    vs3[:, mc, h * 256 + dc * 128:h * 256 + dc * 128 + 128],
                                    pT3[:, 2 * h + mc, 8 * b:8 * b + 8], mc == 0, mc == 1))
            mm_group(mms, [vsk, pTk], bk(0))
        sc.op("dve", lambda: V.tensor_copy(out=qo[:, 8:16, TP:T],
                                           in_=bank(0).rearrange("p (c t) -> p c t", c=8)),
              bk(0), cells("qo", range(8, 16), [4]))

        if kstop <= 3 + 10 * p:
            return
        scr_reset()
        for jb in range(4):
            slot, skey = wblock(I.w_in, 0, 16, jb * 256)

            def epi(cl, sn, o, okey, jb=jb):
                c = 2 * jb + cl
                if sn == "p":
                    sc.op("act", lambda: S.activation(out=uT[:, c, 0:TP], in_=o, func=AF.Gelu_apprx_tanh),
                          okey, cells("uT", [c], range(4)))
                else:
                    sc.op("act", lambda: S.activation(out=uT[:, c, TP:T], in_=o, func=AF.Gelu_apprx_tanh),
                          okey, cells("uT", [c], [4]))
            proj_fm(slot, skey, 16, 2, hT, hkeys_all, hsegs, epi)
            if jb in (1, 3):
                conv_chunk(2 + jb // 2)
        sc.dma("sp", lambda: SP.dma_start(out=bc[:, 0:1024], in_=I.sgu_ln_g.partition_broadcast(128)), "bc0",
               writes=["bc"])
        sc.dma("sp", lambda: SP.dma_start(out=bc[:, 1024:2048], in_=I.sgu_ln_b.partition_broadcast(128)), "bc1",
               writes=["bc"])
        gv = [scratch(1024) for _ in range(5)]
        vb = [scratch(1024, BF16) for _ in range(2)]
        tmpm, tmpk = scratch(1024)
        vtiles = [(i, HO + 128 * i, 128) for i in range(4)] + [(4, sc0, 64)]
        for jb in range(4):
            slot, skey = wblock(I.w_in, 0, 16, 1024 + jb * 256)

            def epi_tm(ti, rows, o, okeys, jb=jb):
                g_, gk = gv[ti]
                sc.op("act", lambda: S.activation(out=g_[0:rows, jb * 256:(jb + 1) * 256], in_=o,
                                                  func=AF.Gelu_apprx_tanh), okeys, [(gk, jb)])
            proj_tm(slot, skey, 16, hT, lambda ti: cells("hT", range(16), [ti]), 0, vtiles, epi_tm)
            conv_chunk(4 + jb)
        def sgu_tile(ti, tc0, rows):
            g_, gk = gv[ti]
            gkeys = [(gk, j) for j in range(4)]
            st, sk = ln_stats(g_[0:rows, 0:512], g_[0:rows, 512:1024], rows, gkeys)
            sc.op("dve", lambda: V.tensor_scalar(out=g_[0:rows, :], in0=g_[0:rows, :], scalar1=st[0:rows, 12:13],
                                                 scalar2=st[0:rows, 15:16], op0=ALU.subtract, op1=ALU.mult),
                  gkeys, gkeys, sreads=[sk])
            sc.op("dve", lambda: V.tensor_tensor(out=g_[0:rows, :], in0=g_[0:rows, :], in1=bc[0:rows, 0:1024],
                                                 op=ALU.mult), gkeys + ["bc"], gkeys)
            sc.op("dve", lambda: V.tensor_tensor(out=g_[0:rows, :], in0=g_[0:rows, :], in1=bc[0:rows, 1024:2048],
                                                 op=ALU.add), gkeys + ["bc"], gkeys)
            v_, vk = vb[ti % 2]
            sc.op("act", lambda: S.activation(out=v_[0:rows, :], in_=g_[0:rows, :], func=AF.Copy), gkeys, [vk])
            if ti == 4:
                sc.dma("sp", lambda: SP.dma_start(out=I.cvs[p * 64:(p + 1) * 64, :], in_=g_[0:64, :]), "cvs",
                       reads=gkeys)
                mms = [(W3[:, g * 64:(g + 1) * 64], v_[0:64, g * 128:(g + 1) * 128], WsTs[0:64, g, :], True, True)
                       for g in range(8)]
                mm_group(mms, [vk, "WsTs"], wide_keys(3))
                sc.op("dve", lambda: V.tensor_tensor(
                    out=tmpm[:, 0:512].rearrange("p (g b t) -> p g b t", g=8, b=8),
                    in0=W3[:, 0:512].rearrange("p (g b t) -> p g b t", g=8, b=8),
                    in1=biasbc[:, :, 0:8].unsqueeze(2).to_broadcast([128, 8, 8, 8]), op=ALU.add),
                    wide_keys(3) + ["biasbc"], [tmpk])
                sc.op("dve", lambda: V.tensor_tensor(out=uT[:, :, TP:T], in0=uT[:, :, TP:T],
                                                     in1=tmpm[:, 0:512].rearrange("p (g t) -> p g t", g=8),
                                                     op=ALU.mult),
                      [tmpk] + cells("uT", range(8), [4]), cells("uT", range(8), [4]))
            else:
                mms = [(W3[:, g * 128:(g + 1) * 128], v_[:, g * 128:(g + 1) * 128], WsT[:, g, :], True, True)
                       for g in range(8)]
                mm_group(mms, [vk, "WsT"], wide_keys(3))
                sc.op("dve", lambda: V.tensor_tensor(out=tmpm.rearrange("p (g t) -> p g t", g=8),
                                                     in0=W3[:, :].rearrange("p (g t) -> p g t", g=8),
                                                     in1=biasbc[:, :, :], op=ALU.add),
                      wide_keys(3) + ["biasbc"], [tmpk])
                cs_ = slice(128 * ti, 128 * ti + 128)
                sc.op("dve", (lambda cs_=cs_: V.tensor_tensor(out=uT[:, :, cs_], in0=uT[:, :, cs_],
                                                             in1=tmpm.rearrange("p (g t) -> p g t", g=8),
                                                             op=ALU.mult)),
                      [tmpk] + cells("uT", range(8), [ti]), cells("uT", range(8), [ti]))

        for (ti, tc0, rows) in vtiles:
            sgu_tile(ti, tc0, rows)
        if kstop <= 4 + 10 * p:
            return

        scr_reset()
        nb = [scratch(1024, BF16) for _ in range(2)]
        cso, csok = scratch(1024)
        cnew, cnk = scratch(512)
        sc.op("dve", lambda: V.tensor_copy(out=cnew.rearrange("p (j b t) -> p j b t", j=8, b=8),
                                           in_=cS[:, :, :, 30:38]), cells("cS", range(8), [0]), [cnk])
        tr_group([(W3[0:64, j * 128:(j + 1) * 128], cnew[:, j * 64:(j + 1) * 64], ident_f[:, :]) for j in range(8)],
                 [cnk, "ident_f"], wide_keys(3))
        sc.op("act", lambda: S.activation(out=cso[0:64, :], in_=W3[0:64, :], func=AF.Copy), wide_keys(3), [csok])
        for b in range(8):
            sc.dma("sp", (lambda b=b: SP.dma_start(out=I.css[p * 8 + b, 22:30, :], in_=cso[8 * b:8 * b + 8, :])),
                   "cso", reads=[csok])
        if p == 1:
            csp_s, cspk = scratch(1024)
            tr_group([(W3[0:32, j * 128:(j + 1) * 128], cT[:, j, TP:TP + H], ident_f[:, :]) for j in range(8)],
                     cells("cT", range(8), [0]) + ["ident_f"], wide_keys(3))
            sc.op("act", lambda: S.activation(out=csp_s[0:32, :], in_=W3[0:32, :], func=AF.Copy),
                  wide_keys(3), [cspk])
            sc.dma("sp", lambda: SP.dma_start(out=I.csp[:, :], in_=csp_s[2:32, :]), "csp", reads=[cspk])
        def ln_tile(ti, tcol, rows):
            tr_group([(W3[0:rows, j * 128:(j + 1) * 128], mT_f[:, j, tcol:tcol + rows], ident_f[:, :])
                      for j in range(8)], cells("mT", range(16), [ti]) + ["ident_f"], wide_keys(3))
            st, sk = ln_stats(W3[0:rows, 0:512], W3[0:rows, 512:1024], rows, wide_keys(3))
            n_, nk = nb[ti % 2]
            sc.op("dve", lambda: V.tensor_scalar(out=n_[0:rows, :], in0=W3[0:rows, :], scalar1=st[0:rows, 12:13],
                                                 scalar2=st[0:rows, 15:16], op0=ALU.subtract, op1=ALU.mult),
                  wide_keys(3), [nk], sreads=[sk])
            tr_group([(TB[:, j * 128:j * 128 + rows], n_[0:rows, j * 128:(j + 1) * 128], ident_bf[0:rows, 0:rows])
                      for j in range(8)], [nk, "ident_bf"], bk(4))
            for j in range(8):
                sc.op("act", (lambda j=j: S.activation(out=cbT[:, j, tcol:tcol + rows],
                                                       in_=TB[:, j * 128:j * 128 + rows], func=AF.Silu,
                                                       scale=pp[:, 104 + j:105 + j], bias=pp[:, 112 + j:113 + j])),
                      bk(4) + ["pp"], cells("cbT", [j], [ti]))

        for (ti, tcol, rows) in [(i, 128 * i, 128) for i in range(4)] + [(4, TP, 64)]:
            ln_tile(ti, tcol, rows)
        if kstop <= 5 + 10 * p:
            return

        scr_reset()
        sgg = [scratch(2 * T) for _ in range(2)]
        macc, mak = scratch(2 * T)
        tmpg, tgk = scratch(2 * T)
        macc3 = macc.rearrange("p (c t) -> p c t", c=2)
        tmpg3 = tmpg.rearrange("p (c t) -> p c t", c=2)
        branches = [(I.w_a, uT, "uT", range(8)), (I.w_b, cbT, "cbT", range(8)), (I.w_c, qo, "qo", range(8, 16))]
        psegs = [("p", 0, TP), ("s", TP, TS)]
        for jg in range(8):
            for br in range(3):
                slot, skey = wblock(I.w_in, 0, 16, 5120 + br * 2048 + jg * 256)
                sg_, sgk_ = sgg[br % 2]
                sgx = sg_.rearrange("p (c t) -> p c t", c=2)

                def epi_g(cl, sn, o, okey, br=br, jg=jg, sgx=sgx, sgk_=sgk_):
                    lo, n = (0, TP) if sn == "p" else (TP, TS)
                    col = 48 + br * 16 + 2 * jg + cl
                    sc.op("act", lambda: S.activation(out=sgx[:, cl, lo:lo + n], in_=o, func=AF.Sigmoid,
                                                      bias=pp[:, col:col + 1], scale=1.0),
                          okey + ["pp"], [(sgk_, cl, sn)])
                proj_fm(slot, skey, 16, 2, hT, hkeys_all, hsegs, epi_g)
                wsrc, act_t, aname, achunks = branches[br]
                slot, skey = wblock(wsrc, 0, 8, jg * 256)
                rhs_t = act_t if br < 2 else qo[:, 8:16, :]

                def epi_y(cl, sn, o, okey, br=br, jg=jg, sgx=sgx, sgk_=sgk_):
                    lo, n = (0, TP) if sn == "p" else (TP, TS)
                    c = 2 * jg + cl
                    tl_ = range(4) if sn == "p" else [4]
                    if br == 0:
                        sc.op("dve", lambda: V.tensor_tensor(out=macc3[:, cl, lo:lo + n], in0=o,
                                                             in1=sgx[:, cl, lo:lo + n], op=ALU.mult),
                              okey + [(sgk_, cl, sn)], [(mak, cl, sn)])
                    else:
                        sc.op("dve", lambda: V.tensor_tensor(out=tmpg3[:, cl, lo:lo + n], in0=o,
                                                             in1=sgx[:, cl, lo:lo + n], op=ALU.mult),
                              okey + [(sgk_, cl, sn)], [(tgk, cl, sn)])
                        if br == 1:
                            sc.op("dve", lambda: V.tensor_tensor(out=macc3[:, cl, lo:lo + n],
                                                                 in0=macc3[:, cl, lo:lo + n],
                                                                 in1=tmpg3[:, cl, lo:lo + n], op=ALU.add),
                                  [(mak, cl, sn), (tgk, cl, sn)], [(mak, cl, sn)])
                        else:
                            sc.op("dve", lambda: V.tensor_tensor(out=mT[:, c, lo:lo + n],
                                                                 in0=macc3[:, cl, lo:lo + n],
                                                                 in1=tmpg3[:, cl, lo:lo + n], op=ALU.add),
                                  [(mak, cl, sn), (tgk, cl, sn)], cells("mT", [c], tl_))
                proj_fm(slot, skey, 8, 2, rhs_t, cells(aname, achunks, range(5)), psegs, epi_y)

        if kstop <= 6 + 10 * p:
            return
        scr_reset()
        xt_list = [(i, 128 * i, 128) for i in range(4)] + [(4, TP, 64)]
        for (ti, tcol, rows) in xt_list:
            src = I.xp[p * 512 + ti * 128:p * 512 + ti * 128 + 128, :] if ti < 4 else I.xs[p * 64:(p + 1) * 64, :]
            sc.dma("sp", (lambda ti=ti, rows=rows, src=src: SP.dma_start(out=xres[0:rows, ti, :], in_=src)),
                   "xres%d" % ti, writes=[("xres", ti, n) for n in range(8)])
        for m_ in range(4):
            def epi_o(ti, rows, o, okeys, m_=m_):
                xk_ = [("xres", ti, 2 * m_), ("xres", ti, 2 * m_ + 1)]
                sc.op("dve", lambda: V.tensor_tensor(out=xres[0:rows, ti, m_ * 512:(m_ + 1) * 512],
                                                     in0=xres[0:rows, ti, m_ * 512:(m_ + 1) * 512], in1=o,
                                                     op=ALU.add),
                      okeys + xk_, xk_)
            proj_tm512(I.w_o, 0, m_ * 512, mT, lambda ti: cells("mT", range(16), [ti]), xt_list, epi_o)

        for (ti, tcol, rows) in xt_list:
            norm_T(xres[0:rows, ti, :], [("xres", ti, n) for n in range(8)], rows, 16,
                   (lambda tcol=tcol, rows=rows: hT[:, :, H + tcol:H + tcol + rows]), cells("hT", range(16), [ti]))

        if kstop <= 7 + 10 * p:
            return
        rl_ctr = [0]
        for r in range(4):
            for jb in range(8):
                slot, skey = wblock(I.w_up, 0, 16, r * 2048 + jb * 256)

                def epi_u(cl, sn, o, okey, jb=jb):
                    c = 2 * jb + cl
                    lo, n = (0, TP) if sn == "p" else (TP, TS)
                    tl_ = range(4) if sn == "p" else [4]
                    ri = rl_ctr[0] % 2
                    rl_ctr[0] += 1
                    rbuf = rlb[ri]
                    rk = [("rl", ri)]
                    sc.op("act", lambda: S.activation(out=rbuf[:, lo:lo + n], in_=o, func=AF.Relu), okey, rk)
                    sc.op("dve", lambda: V.tensor_tensor(out=qo[:, c, lo:lo + n], in0=rbuf[:, lo:lo + n],
                                                         in1=rbuf[:, lo:lo + n], op=ALU.mult),
                          rk, cells("qo", [c], tl_))
                proj_fm(slot, skey, 16, 2, hT, hkeys_all, hsegs, epi_u)
            for m_ in range(4):
                def epi_d(ti, rows, o, okeys, m_=m_):
                    xk_ = [("xres", ti, 2 * m_), ("xres", ti, 2 * m_ + 1)]
                    sc.op("dve", lambda: V.tensor_tensor(out=xres[0:rows, ti, m_ * 512:(m_ + 1) * 512],
                                                         in0=xres[0:rows, ti, m_ * 512:(m_ + 1) * 512], in1=o,
                                                         op=ALU.add),
                          okeys + xk_, xk_)
                proj_tm512(I.w_down, r * 2048, m_ * 512, qo, lambda ti: cells("qo", range(16), [ti]), xt_list, epi_d)

        sc.dma("sp", lambda: SP.dma_start(out=bc[:, :], in_=I.fin_g.partition_broadcast(128)), "bc0", writes=["bc"])
        def fin_tile(ti, tcol, rows):
            xk = [("xres", ti, n) for n in range(8)]
            i = xsb_ctr[0] % 2
            xsb_ctr[0] += 1
            xb = xsb[i]
            st, sk = stat()
            sc.op("dve", lambda: V.memset(st[:, 0:4], 0.0), writes=[sk])
            sc.op("act", lambda: S.activation(out=xb[0:rows, :], in_=xres[0:rows, ti, :], func=AF.Square,
                                              accum_out=st[0:rows, 0:1]), xk, [sk, ("xsb", i)])
            sc.op("act", lambda: S.activation(out=st[0:rows, 1:2], in_=st[0:rows, 0:1], func=AF.Sqrt,
                                              scale=1.0 / D, bias=epsb[0:rows, 0:1]), ["epsb"], [sk], sreads=[sk])
            sc.op("dve", lambda: V.reciprocal(out=st[0:rows, 2:3], in_=st[0:rows, 1:2]), [sk], [sk])
            sc.op("dve", lambda: V.scalar_tensor_tensor(out=xres[0:rows, ti, :], in0=xres[0:rows, ti, :],
                                                        scalar=st[0:rows, 2:3], in1=bc[0:rows, :],
                                                        op0=ALU.mult, op1=ALU.mult), xk + ["bc"], xk, sreads=[sk])
            dst = I.yp[p * 512 + ti * 128:p * 512 + ti * 128 + 128, :] if ti < 4 else I.ys[p * 64:(p + 1) * 64, :]
            sc.dma("sp", (lambda dst=dst, ti=ti, rows=rows: SP.dma_start(out=dst, in_=xres[0:rows, ti, :])),
                   "xres%d" % ti, reads=xk)
        for (ti, tcol, rows) in xt_list:
            fin_tile(ti, tcol, rows)

    with es:
        setup()
        if kstop > 0:
            mem_kv()
        for p in range(2):
            if kstop >= 2 + 10 * p:
                run_pass(p)
        if kstop < 99:
            dbg = nc.dram_tensor("dbg", [128, 2048], F32, kind="ExternalOutput").ap()
            dbt = bc
            sc.barrier()
            import os as _os
            sc.op("dve", lambda: V.memset(dbt[:, 0:2048], 0.0), [], ["dbt"])
        if kstop < 99 and not _os.environ.get("KNODBG"):
            sc.op("dve", lambda: V.tensor_copy(out=dbt[:, 0:64], in_=stats[:, 0:64]), [], ["dbt"])
            sc.op("dve", lambda: V.tensor_copy(out=dbt[:, 64:192], in_=hT[:, 0, 0:128]), [], ["dbt"])
            if kstop >= 3:
                sc.op("dve", lambda: V.tensor_copy(out=dbt[:, 1024:1264].rearrange("p (b r) -> p b r", b=8),
                                                   in_=cS[:, 3, :, 0:30]), [], ["dbt"])
                sc.op("dve", lambda: V.tensor_copy(out=dbt[:, 1264:1504].rearrange("p (b r) -> p b r", b=8),
                                                   in_=cS[:, 7, :, 0:30]), [], ["dbt"])
            sc.op("dve", lambda: V.tensor_copy(out=dbt[:, 192:320], in_=wring[0][:, 0, 0:128]), [], ["dbt"])
            sc.op("dve", lambda: V.tensor_copy(out=dbt[:, 320:448], in_=KT[:, 0, 0:128]), [], ["dbt"])
            sc.op("dve", lambda: V.tensor_copy(out=dbt[:, 448:576], in_=xres_flat[:, 4096:4096 + 128]), [], ["dbt"])
            sc.op("dve", lambda: V.tensor_copy(out=dbt[:, 576:696], in_=pp[:, 0:120]), [], ["dbt"])
            sc.op("dve", lambda: V.tensor_copy(out=dbt[:, 704:832], in_=xsb[1][:, 0:128]), [], ["dbt"])
            sc.op("dve", lambda: V.tensor_copy(out=dbt[:, 192:320], in_=ident_bf[:, 0:128]), [], ["dbt"])
            sc.op("dve", lambda: V.tensor_copy(out=dbt[:, 320:448], in_=TBw[:, 0:128]), [], ["dbt"])
            sc.op("dve", lambda: V.tensor_copy(out=dbt[:, 832:864], in_=cw[:, 1, 0:32]), [], ["dbt"])
            sc.op("dve", lambda: V.tensor_copy(out=dbt[0:64, 864:928], in_=WsTs[:, 1, 0:64]), [], ["dbt"])
            sc.op("dve", lambda: V.tensor_copy(out=dbt[:, 928:1024], in_=ident_f[:, 0:96]), [], ["dbt"])
        if kstop < 99:
            sc.barrier()
            sc.dma("sp", lambda: SP.dma_start(out=dbg[:, :], in_=dbt[:, 0:2048]), "dbg")
        block = es.enter_context(nc.Block())
        sc.emit(block)
    nc._io_names = (list(I._t), )
    return nc, sc


_CACHE = {}


def kernel(x_prompt, x_sample, mem_prompt, cache_mem_k, cache_mem_v, state_conv,
           attn_norm_g, w_in, b_gate, sgu_ln_g, sgu_ln_b, sgu_w, sgu_b, w_a_out,
           conv_w, conv_b, conv_ln_g, conv_ln_b, w_b_out, mem_norm_g, w_mem_kv,
           w_c_out, w_o, mlp_norm_g, w_up, w_down, final_norm_g):
    f = lambda a: np.ascontiguousarray(np.asarray(a, dtype=np.float32))
    if "nc" not in _CACHE:
        _CACHE["nc"] = build_program()[0]
    nc = _CACHE["nc"]
    x_prompt = f(x_prompt); x_sample = f(x_sample); mem_prompt = f(mem_prompt)
    ckf = f(cache_mem_k)[0].reshape(128, 256, 1024)
    cvf = f(cache_mem_v)[0].reshape(128, 256, 1024)
    scf = f(state_conv)[0]
    shared = {
        "attn_g": f(attn_norm_g)[0].reshape(16, 128), "mlp_g": f(mlp_norm_g)[0].reshape(16, 128),
        "mem_g": f(mem_norm_g)[0].reshape(16, 128), "w_in": f(w_in)[0], "b_gate": f(b_gate)[0].reshape(48, 128),
        "sgu_ln_g": f(sgu_ln_g)[0], "sgu_ln_b": f(sgu_ln_b)[0], "sgu_w": f(sgu_w)[0], "sgu_b": f(sgu_b)[0],
        "w_a": f(w_a_out)[0], "conv_w": f(conv_w)[0], "conv_b": f(conv_b)[0].reshape(8, 128),
        "conv_ln_g": f(conv_ln_g)[0].reshape(8, 128), "conv_ln_b": f(conv_ln_b)[0].reshape(8, 128),
        "w_b": f(w_b_out)[0], "w_kv": f(w_mem_kv)[0], "w_c": f(w_c_out)[0], "w_o": f(w_o)[0],
        "w_up": f(w_up)[0], "w_down": f(w_down)[0], "fin_g": f(final_norm_g),
    }
    in_maps = []
    for c in range(8):
        b, half = c // 2, c % 2
        m = dict(shared)
        m["xp"] = np.ascontiguousarray(x_prompt[b, half * 1024:(half + 1) * 1024])
        m["xh"] = (np.ascontiguousarray(x_prompt[b, 1024 - H:1024]) if half == 1
                   else np.zeros((H, D), np.float32))
        m["xs"] = np.ascontiguousarray(x_sample[c * 16:(c + 1) * 16].reshape(128, D))
        m["mem"] = np.ascontiguousarray(mem_prompt[b])
        m["ck"] = np.ascontiguousarray(ckf[c * 16:(c + 1) * 16])
        m["cv"] = np.ascontiguousarray(cvf[c * 16:(c + 1) * 16])
        m["scs"] = np.ascontiguousarray(scf[c * 16:(c + 1) * 16])
        in_maps.append(m)
    res = run_bass_kernel_spmd(nc, in_maps, core_ids=list(range(8)))
    R = res.results
    y_prompt = np.zeros((4, 2048, D), np.float32)
    y_sample = np.zeros((128, 8, D), np.float32)
    nk = np.zeros((1, 4, 256, 4, 256), np.float32)
    nv = np.zeros((1, 4, 256, 4, 256), np.float32)
    ncp = np.zeros((1, 4, 30, 1024), np.float32)
    ncs = np.zeros((1, 128, 30, 1024), np.float32)
    ncv = np.zeros((1, 128, 8, 1024), np.float32)
    for c in range(8):
        b, half = c // 2, c % 2
        y_prompt[b, half * 1024:(half + 1) * 1024] = R[c]["yp"]
        y_sample[c * 16:(c + 1) * 16] = R[c]["ys"].reshape(16, 8, D)
        if half == 0:
            nk[0, b] = R[c]["mk"].reshape(256, 4, 256)
            nv[0, b] = R[c]["mv"].reshape(256, 4, 256)
        else:
            ncp[0, b] = R[c]["csp"]
        ncs[0, c * 16:(c + 1) * 16] = R[c]["css"]
        ncv[0, c * 16:(c + 1) * 16] = R[c]["cvs"].reshape(16, 8, 1024)
    return (y_prompt, y_sample, nk, nv, ncp, ncs, ncv)
```

```python
import numpy as np
from contextlib import ExitStack
import concourse.bass as bass
import concourse.mybir as mybir
from concourse.bass_utils import run_bass_kernel_spmd

F32 = mybir.dt.float32
BF16 = mybir.dt.bfloat16
AF = mybir.ActivationFunctionType
ALU = mybir.AluOpType
AX = mybir.AxisListType

D = 2048
NIN = 11264
TP = 512
TS = 64
T = TP + TS
H = 32
EPS = 1e-6
NSLOT = 3
SAME_SYNC = False


class Sched:
    def __init__(self, nc, es):
        self.nc = nc
        self.engs = {"pe": nc.tensor, "act": nc.scalar, "dve": nc.vector,
                     "pool": nc.gpsimd, "sp": nc.sync}
        self.names = list(self.engs)
        self.sem = {e: es.enter_context(nc.semaphore("s_" + e)) for e in self.names}
        self.cnt = {e: 0 for e in self.names}
        self.q = {e: [] for e in self.names}
        self.known = {e: {} for e in self.names}
        self.snap = {e: [None] for e in self.names}
        self.dsem = {}
        self.dsnap = {}
        self.last_w = {}
        self.readers = {}
        self.pending = {e: [] for e in self.names}
        self.es = es
        self.nwaits = 0

    def _learn(self, e, ev):
        kn = self.known[e]
        k = (ev[0], ev[1])
        if kn.get(k, 0) < ev[2]:
            kn[k] = ev[2]
        other = self.snap[ev[1]][ev[2]] if ev[0] == "e" else self.dsnap[(ev[1], ev[2])]
        for kk, v in other.items():
            if kn.get(kk, 0) < v:
                kn[kk] = v

    def _deps(self, e, reads, writes, is_dma, use_pending=True, sreads=()):
        deps = {}
        sdeps = {}
        for k in sreads:
            w = self.last_w.get(k)
            if w and w[0] == "e" and w[1] == e:
                sdeps[(w[0], w[1])] = max(sdeps.get((w[0], w[1]), 0), w[2])

        def add(ev):
            k = (ev[0], ev[1])
            if deps.get(k, 0) < ev[2]:
                deps[k] = ev[2]
        if use_pending:
            for ev in self.pending[e]:
                add(ev)
            self.pending[e] = []
        for k in reads:
            w = self.last_w.get(k)
            if w:
                add(w)
        for k in writes:
            w = self.last_w.get(k)
            if w:
                add(w)
            for rk, rv in self.readers.get(k, {}).items():
                add((rk[0], rk[1], rv))
        waits = []
        for k_, v_ in sdeps.items():
            if self.known[e].get(k_, 0) < v_:
                waits.append((k_[0], k_[1], v_))
        for (ty, nm), v in deps.items():
            if ty == "e" and nm == e and not is_dma and (e == "pe" or not SAME_SYNC):
                continue
            if self.known[e].get((ty, nm), 0) >= v:
                continue
            waits.append((ty, nm, v))
        for ev in waits:
            self._learn(e, ev)
        self.nwaits += len(waits)
        return waits

    def _record(self, ev, reads, writes):
        for k in writes:
            self.last_w[k] = ev
            self.readers[k] = {}
        for k in reads:
            r = self.readers.setdefault(k, {})
            kk = (ev[0], ev[1])
            if r.get(kk, 0) < ev[2]:
                r[kk] = ev[2]

    def op(self, e, fn, reads=(), writes=(), sreads=()):
        reads = list(reads) + list(sreads)
        bankr = [k for k in reads if isinstance(k, tuple) and k[0] == "bank"]
        if bankr:
            reads = [k for k in reads if not (isinstance(k, tuple) and k[0] == "bank")]
            writes = list(writes) + bankr
        waits = self._deps(e, reads, writes, False, sreads=sreads)
        self.cnt[e] += 1
        n = self.cnt[e]
        s = dict(self.known[e])
        s[("e", e)] = max(s.get(("e", e), 0), n - 1)
        self.snap[e].append(s)
        self._record(("e", e, n), reads, writes)
        self.q[e].append((waits, fn, None))

    def dma(self, qn, fn, semname, reads=(), writes=(), use_pending=True):
        waits = self._deps(qn, reads, writes, True, use_pending)
        if semname not in self.dsem:
            self.dsem[semname] = [self.es.enter_context(self.nc.semaphore("d_" + semname)), 0]
        h = self.dsem[semname]
        h[1] += 16
        ev = ("d", semname, h[1])
        self.dsnap[(semname, h[1])] = dict(self.known[qn])
        self._record(ev, reads, writes)
        self.q[qn].append((waits, fn, h[0]))

    def barrier(self):
        evs = [("e", f, self.cnt[f]) for f in ("pe", "act", "dve", "pool") if self.cnt[f] > 0]
        evs += [("d", nm, h[1]) for nm, h in self.dsem.items() if nm not in ("w0a", "w1a", "w2a", "w0b", "w1b", "w2b")]
        for e in self.names:
            self.pending[e] = list(evs)

    def emit(self, block):
        def run(e):
            eng = self.engs[e]
            for waits, fn, dsem in self.q[e]:
                for (ty, nm, v) in waits:
                    eng.wait_ge(self.sem[nm] if ty == "e" else self.dsem[nm][0], v)
                ins = fn()
                if dsem is None:
                    ins.then_inc(self.sem[e], 1)
                else:
                    ins.then_inc(dsem, 16)
            if e == "sp":
                for f in ("pe", "act", "dve", "pool"):
                    if self.cnt[f]:
                        eng.wait_ge(self.sem[f], self.cnt[f])
                for nm, h in self.dsem.items():
                    eng.wait_ge(h[0], h[1])

        @block.tensor
        def _(x):
            run("pe")

        @block.scalar
        def _(x):
            run("act")

        @block.vector
        def _(x):
            run("dve")

        @block.gpsimd
        def _(x):
            run("pool")

        @block.sync
        def _(x):
            run("sp")


def build_program(kstop=99, io_names=None):
    nc = bass.Bass("TRN2", target_bir_lowering=False)
    es = ExitStack()
    es.enter_context(nc.allow_non_contiguous_dma(reason="small strided parameter loads"))

    IN_SHAPES = {
        "xp": [1024, D], "xh": [H, D], "xs": [128, D], "mem": [256, D],
        "ck": [16, 256, 1024], "cv": [16, 256, 1024], "scs": [16, 30, 1024],
        "attn_g": [16, 128], "mlp_g": [16, 128], "mem_g": [16, 128], "w_in": [D, NIN], "b_gate": [48, 128],
        "sgu_ln_g": [1024], "sgu_ln_b": [1024], "sgu_w": [8, 128, 128], "sgu_b": [8, 128],
        "w_a": [1024, D], "conv_w": [31, 1024], "conv_b": [8, 128], "conv_ln_g": [8, 128], "conv_ln_b": [8, 128],
        "w_b": [1024, D], "w_kv": [D, D], "w_c": [1024, D], "w_o": [D, D], "w_up": [D, 4 * D],
        "w_down": [4 * D, D], "fin_g": [D],
    }
    OUT_SHAPES = {"yp": [1024, D], "ys": [128, D], "mk": [256, 1024], "mv": [256, 1024],
                  "csp": [30, 1024], "css": [16, 30, 1024], "cvs": [128, 1024]}

    class _IO:
        def __init__(self):
            self._t = {}

        def __getattr__(self, name):
            t = self.__dict__["_t"]
            if name not in t:
                if name in IN_SHAPES:
                    t[name] = nc.dram_tensor(name, IN_SHAPES[name], F32, kind="ExternalInput").ap()
                elif name in OUT_SHAPES:
                    t[name] = nc.dram_tensor(name, OUT_SHAPES[name], F32, kind="ExternalOutput").ap()
                else:
                    raise AttributeError(name)
            return t[name]
    I = _IO()
    if io_names is None and kstop >= 99:
        io_names = list(IN_SHAPES) + list(OUT_SHAPES)
    for n_ in list(IN_SHAPES) + list(OUT_SHAPES):
        if io_names is not None and n_ in io_names:
            getattr(I, n_)

    def sb(name, shape, dt=F32):
        return es.enter_context(nc.sbuf_tensor(name, shape, dt))

    def ps(name, shape, dt=F32):
        return es.enter_context(nc.psum_tensor(name, shape, dt))

    hT = sb("hT", [128, 16, H + T], BF16)
    xres = sb("xres", [128, 5, D])
    wring = [sb("wr%d" % i, [128, 16, 256], BF16) for i in range(NSLOT)]
    qo = sb("qo", [128, 16, T], BF16)
    uT = sb("uT", [128, 8, T], BF16)
    cT = sb("cT", [128, 8, H + TP])
    cS = sb("cS", [128, 8, 8, 38])
    cbT = sb("cbT", [128, 8, T], BF16)
    mT = sb("mT", [128, 16, T], BF16)
    KT = sb("KT", [128, 8, 256], BF16)
    Vp = sb("Vp", [128, 2, 1024], BF16)
    bc = sb("bc", [128, D])
    xsb = [sb("xsb%d" % i, [128, D], BF16) for i in range(2)]
    ident_bf = sb("ident_bf", [128, 128], BF16)
    ident_f = sb("ident_f", [128, 128])
    trimask = sb("trimask", [128, 128])
    pp = sb("pp", [128, 128])
    cw = sb("cw", [128, 8, 32])
    WsT = sb("WsT", [128, 8, 128], BF16)
    WsTs = sb("WsTs", [64, 8, 64], BF16)
    biasbc = sb("biasbc", [128, 8, 128])
    maskq = sb("maskq", [128, 8, 64], BF16)
    epsb = sb("epsb", [128, 1])
    stats = sb("stats", [128, 1024])
    rlb = [sb("rl%d" % i, [128, T]) for i in range(2)]

    W0_ = ps("W0", [128, 1024]); W1_ = ps("W1", [128, 1024])
    TBw = ps("TBw", [128, 2048], BF16)
    W3_ = ps("W3", [128, 1024])
    W = [W0_, W1_, None, W3_]

    sc = Sched(nc, es)
    V, S, G, PE, SP = nc.vector, nc.scalar, nc.gpsimd, nc.tensor, nc.sync

    def bk(i):
        return [("bank", i)]

    def bank(i):
        assert i in (0, 1, 2, 3, 6, 7)
        return W[i // 2][:, (i % 2) * 512:(i % 2) * 512 + 512]

    TB = TBw[:, 0:1024]
    W3 = W[3]
    mT_f = mT[:, :, :].rearrange("p c t -> p (c t)").bitcast(F32).rearrange("p (c t) -> p c t", c=8)

    def cells(name, chunks, tiles):
        return [(name, c, t) for c in chunks for t in tiles]

    ALLT = range(5)
    stat_ctr = [0]

    def stat(n=16):
        i = stat_ctr[0] % 32
        stat_ctr[0] += 1
        return stats[:, i * 32:i * 32 + 32], ("stat", i)

    scr_off = [0]
    scr_gen = [0]
    xres_flat = xres[:, :, :].rearrange("p a b -> p (a b)")

    def scr_reset():
        import os as _os
        if not _os.environ.get("KNOB_NOBAR"):
            sc.barrier()
        scr_off[0] = 0
        scr_gen[0] += 1

    def scratch(nelem, dt=F32):
        n32 = nelem if dt == F32 else (nelem + 1) // 2
        n32 = (n32 + 7) // 8 * 8
        o = scr_off[0]
        assert o + n32 <= 5 * D, "scratch overflow"
        scr_off[0] += n32
        ap = xres_flat[:, o:o + n32]
        if dt != F32:
            ap = ap.bitcast(dt)[:, 0:nelem]
        else:
            ap = ap[:, 0:nelem]
        return ap, ("scr", scr_gen[0], o)

    def mm_group(mms, reads, writes):
        def fn():
            ins = None
            for (o, l, r, st, sp_) in mms:
                ins = PE.matmul(o, lhsT=l, rhs=r, start=st, stop=sp_)
            return ins
        sc.op("pe", fn, reads, writes)

    def tr_group(trs, reads, writes):
        def fn():
            ins = None
            for (o, i_, idn) in trs:
                ins = PE.transpose(out=o, in_=i_, identity=idn)
            o, i_, idn = trs[-1]
            if i_.dtype == F32:
                ins = PE.transpose(out=o, in_=i_, identity=idn)
            return ins
        sc.op("pe", fn, reads, writes)

    wcnt = [0]

    def wblock(wdram, r0, kcn, c0, ncol=256):
        s = wcnt[0] % NSLOT
        wcnt[0] += 1
        slot = wring[s]
        if ncol != 256:
            assert kcn * ncol <= 16 * 256
            slot = slot[:, :, :].rearrange("p a b -> p (a b)")[:, 0:kcn * ncol].rearrange("p (k n) -> p k n", k=kcn)
        src = wdram[r0:r0 + 128 * kcn, c0:c0 + ncol].rearrange("(k p) n -> p k n", p=128)
        hk = max(1, kcn // 2)
        keys = []
        for hi, (k0, k1) in enumerate([(0, hk), (hk, kcn)]):
            if k1 <= k0:
                continue
            key = ("wslot", s, hi)
            keys.append(key)
            sc.dma("pool", (lambda k0=k0, k1=k1: G.dma_start(out=slot[:, k0:k1, 0:ncol], in_=src[:, k0:k1, :])),
                   "w%d%s" % (s, "ab"[hi]), writes=[key], use_pending=False)
        return slot, keys

    acc_ctr = [0]
    ss_ctr = [0]

    def proj_fm(slot, skey, kcn, nchunks, rhs_t, rhs_keys, segs, epilogue):
        for cl in range(nchunks):
            outs = []
            for grp in ("p", "small"):
                mms = []
                wrs = []
                for (sn, c0, n) in segs:
                    if (sn == "p") != (grp == "p"):
                        continue
                    if sn == "p":
                        b = acc_ctr[0] % 3
                        acc_ctr[0] += 1
                        o = bank(b)[:, 0:n]
                        wr = bk(b)
                    else:
                        off = 0 if sn == "s" else 64
                        o = bank(3)[:, off:off + n]
                        wr = bk(3)
                    outs.append((sn, o, wr))
                    wrs += wr
                    for k in range(kcn):
                        mms.append((o, slot[:, k, cl * 128:(cl + 1) * 128], rhs_t[:, k, c0:c0 + n],
                                    k == 0, k == kcn - 1))
                if mms:
                    mm_group(mms, list(skey) + list(rhs_keys), list(dict.fromkeys(wrs)))
            for (sn, o, wr) in outs:
                epilogue(cl, sn, o, wr)

    tm_ctr = [0]

    def proj_tm(slot, skey, kcn, lhs_t, lhs_keys_fn, col0, tiles, epilogue, ncol=256):
        par = tm_ctr[0] % 2
        tm_ctr[0] += 1
        for (ti, tc0, rows) in tiles:
            bi = ti if ti < 4 else 6
            o = bank(bi)[0:rows, 0:ncol]
            wr = bk(bi)
            mms = [(o, lhs_t[:, k, tc0:tc0 + rows], slot[:, k, 0:ncol], k == 0, k == kcn - 1)
                   for k in range(kcn)]
            mm_group(mms, list(skey) + lhs_keys_fn(ti), wr)
            epilogue(ti, rows, o, wr)

    def proj_tm512(wdram, r0, c0, lhs_t, lhs_keys_fn, tiles, epilogue):
        blocks = [wblock(wdram, r0 + 1024 * hb, 8, c0, ncol=512) for hb in range(2)]
        outs = {}
        for hb, (slot, skey) in enumerate(blocks):
            for (ti, tc0, rows) in tiles:
                bi = ti if ti < 4 else 6
                o = bank(bi)[0:rows, 0:512]
                outs[ti] = (o, bk(bi), rows)
                mms = [(o, lhs_t[:, 8 * hb + k, tc0:tc0 + rows], slot[:, k, 0:512], hb == 0 and k == 0,
                        hb == 1 and k == 7) for k in range(8)]
                mm_group(mms, list(skey) + lhs_keys_fn(ti), bk(bi))
        for (ti, tc0, rows) in tiles:
            o, wr, rows = outs[ti]
            epilogue(ti, rows, o, wr)

    def wide_keys(i):
        return bk(2 * i) + bk(2 * i + 1)

    def setup():
        sc.op("pool", lambda: G.memset(ident_bf[:], 1.0), writes=["ident_bf"], sreads=["ident_bf"])
        sc.op("pool", lambda: G.affine_select(out=ident_bf[:], in_=ident_bf[:], pattern=[[-1, 128]],
                                              compare_op=ALU.is_equal, fill=0.0, base=0, channel_multiplier=1),
              writes=["ident_bf"], sreads=["ident_bf"])
        sc.op("pool", lambda: G.memset(ident_f[:], 1.0), writes=["ident_f"], sreads=["ident_f"])
        sc.op("pool", lambda: G.affine_select(out=ident_f[:], in_=ident_f[:], pattern=[[-1, 128]],
                                              compare_op=ALU.is_equal, fill=0.0, base=0, channel_multiplier=1),
              writes=["ident_f"], sreads=["ident_f"])
        sc.op("pool", lambda: G.memset(trimask[:], 1.0), writes=["trimask"], sreads=["trimask"])
        sc.op("pool", lambda: G.affine_select(out=trimask[:], in_=trimask[:], pattern=[[1, 128]],
                                              compare_op=ALU.is_ge, fill=0.0, base=0, channel_multiplier=-1),
              writes=["trimask"], sreads=["trimask"])
        sc.op("pool", lambda: G.memset(epsb[:], EPS), writes=["epsb"])
        sc.op("pool", lambda: G.memset(stats[:], 0.0), writes=[("stat", i) for i in range(64)])
        sc.op("pool", lambda: G.memset(maskq[:], 0.0), writes=["maskq"], sreads=["maskq"])
        for b in range(8):
            sc.op("pool", (lambda b=b: G.memset(maskq[:, b, 8 * b:8 * b + 8], 1.0)), writes=["maskq"], sreads=["maskq"])

        prm_t, _ = scratch(128)
        rows = [(I.attn_g, 0, 16), (I.mlp_g, 16, 16), (I.mem_g, 32, 16), (I.b_gate, 48, 48),
                (I.conv_b, 96, 8), (I.conv_ln_g, 104, 8), (I.conv_ln_b, 112, 8)]
        for i, (src, r0, n) in enumerate(rows):
            sc.dma("sp", (lambda src=src, r0=r0, n=n: SP.dma_start(out=prm_t[r0:r0 + n, :], in_=src[:, :])),
                   "prm%d" % i, writes=["prm"])
        tr_group([(W3[:, 0:120], prm_t[0:120, :], ident_f[0:120, 0:120])], ["prm", "ident_f"], wide_keys(3))
        sc.op("dve", lambda: V.tensor_copy(out=pp[:, 0:120], in_=W3[:, 0:120]), wide_keys(3), ["pp"])
        cwst, _ = scratch(1024)
        sc.dma("sp", lambda: SP.dma_start(out=cwst[0:31, :], in_=I.conv_w[:, :]), "cwst", writes=["cwst"])
        tr_group([(W3[:, j * 32:j * 32 + 31], cwst[0:31, j * 128:(j + 1) * 128], ident_f[0:31, 0:31])
                  for j in range(8)], ["cwst", "ident_f"], wide_keys(3))
        sc.op("dve", lambda: V.tensor_copy(out=cw[:, :, 0:31],
                                           in_=W3[:, 0:256].rearrange("p (j t) -> p j t", j=8)[:, :, 0:31]),
              wide_keys(3), ["cw"])
        wnat, _ = scratch(1024)
        wn3 = wnat.rearrange("p (g s) -> p g s", g=8)
        sc.dma("sp", lambda: SP.dma_start(out=wn3, in_=I.sgu_w.rearrange("g t s -> t g s")), "wnat", writes=["wnat"])
        tr_group([(W3[:, g * 128:(g + 1) * 128], wn3[:, g, :], ident_f[:, :]) for g in range(8)],
                 ["wnat", "ident_f"], wide_keys(3))
        sc.op("dve", lambda: V.tensor_tensor(out=WsT[:, :, :], in0=W3[:, :].rearrange("p (g t) -> p g t", g=8),
                                             in1=trimask[:, :].unsqueeze(1).to_broadcast([128, 8, 128]), op=ALU.mult),
              wide_keys(3) + ["trimask"], ["WsT"])
        a32, _ = scratch(64)
        abf, _ = scratch(64, BF16)
        ebf, _ = scratch(64, BF16)
        mblk, _ = scratch(64)
        for g in range(8):
            sc.dma("sp", (lambda g=g: SP.dma_start(out=a32[0:8, g * 8:(g + 1) * 8],
                                                   in_=I.sgu_w[g, 0:8, 0:8].rearrange("t s -> s t"))),
                   "wssg%d" % g, writes=[("a32", g)])
        sc.op("dve", lambda: V.tensor_copy(out=abf[0:8, :], in_=a32[0:8, :]), [("a32", g) for g in range(8)], ["abf"])
        sc.op("pool", lambda: G.memset(ebf[0:8, :], 1.0), writes=["ebf"], sreads=["ebf"])
        sc.op("pool", lambda: G.affine_select(out=ebf[0:8, :].rearrange("p (b j) -> p b j", b=8),
                                              in_=ebf[0:8, :].rearrange("p (b j) -> p b j", b=8),
                                              pattern=[[0, 8], [1, 8]], compare_op=ALU.is_equal, fill=0.0,
                                              base=0, channel_multiplier=-1), writes=["ebf"], sreads=["ebf"])
        sc.op("pool", lambda: G.memset(mblk[0:64, :], 1.0), writes=["mblk"], sreads=["mblk"])
        sc.op("pool", lambda: G.affine_select(out=mblk[0:64, :].rearrange("p (b t) -> p b t", b=8),
                                              in_=mblk[0:64, :].rearrange("p (b t) -> p b t", b=8),
                                              pattern=[[8, 8], [1, 8]], compare_op=ALU.is_ge, fill=0.0,
                                              base=0, channel_multiplier=-1), writes=["mblk"], sreads=["mblk"])
        sc.op("pool", lambda: G.affine_select(out=mblk[0:64, :].rearrange("p (b t) -> p b t", b=8),
                                              in_=mblk[0:64, :].rearrange("p (b t) -> p b t", b=8),
                                              pattern=[[-8, 8], [0, 8]], compare_op=ALU.is_ge, fill=0.0,
                                              base=0, channel_multiplier=1), writes=["mblk"], sreads=["mblk"])
        mm_group([(W3[0:64, 0:64], ebf[0:8, 0:64], abf[0:8, 0:64], True, True)], ["ebf", "abf"], wide_keys(3))
        sc.op("dve", lambda: V.tensor_tensor(
            out=WsTs[:, :, :].rearrange("p g (b t) -> p g b t", b=8),
            in0=W3[0:64, 0:64].rearrange("p (g t) -> p g t", g=8).unsqueeze(2).to_broadcast([64, 8, 8, 8]),
            in1=mblk[0:64, :].rearrange("p (b t) -> p b t", b=8).unsqueeze(1).to_broadcast([64, 8, 8, 8]),
            op=ALU.mult), wide_keys(3) + ["mblk"], ["WsTs"])
        sc.dma("sp", lambda: SP.dma_start(out=biasbc[:, :, :], in_=I.sgu_b.partition_broadcast(128)), "biasbc",
               writes=["biasbc"])

    xsb_ctr = [0]

    def norm_T(src, src_keys, rows, gcol, dst_fn, dst_keys):
        i = xsb_ctr[0] % 2
        xsb_ctr[0] += 1
        xb = xsb[i]
        xk = ("xsb", i)
        st, sk = stat()
        sc.op("dve", lambda: V.memset(st[:, 0:4], 0.0), writes=[sk])
        sc.op("act", lambda: S.activation(out=xb[0:rows, :], in_=src, func=AF.Square,
                                          accum_out=st[0:rows, 0:1]), list(src_keys), [sk, xk])
        sc.op("act", lambda: S.activation(out=st[0:rows, 1:2], in_=st[0:rows, 0:1], func=AF.Sqrt,
                                          scale=1.0 / D, bias=epsb[0:rows, 0:1]), ["epsb"], [sk], sreads=[sk])
        sc.op("dve", lambda: V.reciprocal(out=st[0:rows, 2:3], in_=st[0:rows, 1:2]), [sk], [sk])
        sc.op("dve", lambda: V.tensor_scalar_mul(out=xb[0:rows, :], in0=src, scalar1=st[0:rows, 2:3]),
              list(src_keys), [xk], sreads=[sk])
        tr_group([(TBw[:, k * 128:k * 128 + rows], xb[0:rows, k * 128:(k + 1) * 128], ident_bf[0:rows, 0:rows])
                  for k in range(16)], [xk, "ident_bf"], bk(4) + bk(5))
        src_ps = TBw[:, :].rearrange("p (k t) -> p k t", k=16)
        dst = dst_fn()
        dk = list(dst_keys)
        for k in range(16):
            if k < 8:
                sc.op("act", (lambda k=k: S.activation(out=dst[:, k, :], in_=src_ps[:, k, 0:rows], func=AF.Copy,
                                                       scale=pp[:, gcol + k:gcol + k + 1])),
                      bk(4) + ["pp"], dk)
            else:
                sc.op("dve", (lambda k=k: V.tensor_scalar_mul(out=dst[:, k, :], in0=src_ps[:, k, 0:rows],
                                                              scalar1=pp[:, gcol + k:gcol + k + 1])),
                      bk(5) + ["pp"], dk)

    def ln_stats(src0, src1, rows, src_keys):
        st, sk = stat()

        sk2 = (sk, "b")
        sc.op("dve", lambda: V.bn_stats(out=st[0:rows, 0:6], in_=src0), list(src_keys), [sk])
        sc.op("dve", lambda: V.bn_stats(out=st[0:rows, 16:22], in_=src1), list(src_keys), [sk2])
        sc.op("dve", lambda: V.bn_aggr(out=st[0:rows, 6:8], in_=st[0:rows, 0:6]), [], [sk], sreads=[sk])
        sc.op("dve", lambda: V.bn_aggr(out=st[0:rows, 8:10], in_=st[0:rows, 16:22]), [], [sk2], sreads=[sk2])
        sc.op("dve", lambda: V.tensor_tensor(out=st[0:rows, 10:12], in0=st[0:rows, 6:8], in1=st[0:rows, 8:10],
                                             op=ALU.add), [], [sk], sreads=[sk, sk2])
        sc.op("dve", lambda: V.tensor_scalar_mul(out=st[0:rows, 12:14], in0=st[0:rows, 10:12], scalar1=0.5),
              [], [sk], sreads=[sk])
        sc.op("dve", lambda: V.tensor_tensor(out=st[0:rows, 10:11], in0=st[0:rows, 6:7], in1=st[0:rows, 8:9],
                                             op=ALU.subtract), [], [sk], sreads=[sk])
        sc.op("dve", lambda: V.tensor_tensor(out=st[0:rows, 11:12], in0=st[0:rows, 10:11], in1=st[0:rows, 10:11],
                                             op=ALU.mult), [], [sk], sreads=[sk])
        sc.op("dve", lambda: V.scalar_tensor_tensor(out=st[0:rows, 13:14], in0=st[0:rows, 11:12], scalar=0.25,
                                                    in1=st[0:rows, 13:14], op0=ALU.mult, op1=ALU.add),
              [], [sk], sreads=[sk])
        sc.op("act", lambda: S.activation(out=st[0:rows, 14:15], in_=st[0:rows, 13:14], func=AF.Sqrt,
                                          scale=1.0, bias=epsb[0:rows, 0:1]), [sk, "epsb"], [sk])
        sc.op("dve", lambda: V.reciprocal(out=st[0:rows, 15:16], in_=st[0:rows, 14:15]), [sk], [sk])
        return st, sk

    def mem_kv():
        scr_reset()
        for mt in range(2):
            xst, xk = scratch(D)
            sc.dma("sp", (lambda mt=mt, xst=xst: SP.dma_start(out=xst[:, :], in_=I.mem[mt * 128:(mt + 1) * 128, :])),
                   "xst%d" % mt, writes=[xk])
            norm_T(xst[:, :], [xk], 128, 32, (lambda mt=mt: hT[:, :, mt * 128:(mt + 1) * 128]),
                   cells("hT", range(16), [mt]))
        if kstop <= 0.3:
            return
        kvst = [scratch(512) for _ in range(2)]
        for jb in range(8):
            slot, skey = wblock(I.w_kv, 0, 16, jb * 256)
            if kstop <= 0.5:
                sc.op("dve", lambda: V.tensor_copy(out=KT[:, 0, 0:256], in_=slot[:, 0, 0:256]), list(skey), ["KT"])
                return
            if jb < 4:
                def epi(cl, sn, o, okey, jb=jb):
                    c = 2 * jb + cl
                    sc.op("act", lambda: S.activation(out=KT[:, c, :], in_=o, func=AF.Copy), okey, ["KT"])
                proj_fm(slot, skey, 16, 2, hT, cells("hT", range(16), [0, 1]), [("p", 0, 256)], epi)
            if kstop <= 0.6:
                return
            ks, kk = kvst[jb % 2]
            ks3 = ks.rearrange("p (m c) -> p m c", m=2)

            def epi_tm(ti, rows, o, okeys, jb=jb, ks3=ks3, kk=kk):
                if jb < 4:
                    sc.op("dve", lambda: V.tensor_copy(out=ks3[:, ti, :], in_=o), okeys, [(kk, ti)])
                else:
                    sc.op("dve", lambda: V.tensor_copy(out=ks3[:, ti, :], in_=o), okeys, [(kk, ti)])
                    sc.op("act", lambda: S.activation(out=Vp[:, ti, (jb - 4) * 256:(jb - 3) * 256], in_=o,
                                                      func=AF.Copy), okeys, ["Vp"])
            import os as _os
            kd = _os.environ.get("KDBG", "")
            if kd == "noepi":
                epi_tm = lambda ti, rows, o, okeys: None
            proj_tm(slot, skey, 16, hT, lambda ti: cells("hT", range(16), [ti]), 0,
                    [(0, 0, 128)] if kd == "t0" else [(0, 0, 128), (1, 128, 128)], epi_tm)
            if kstop <= 0.7:
                return
            dst = I.mk if jb < 4 else I.mv
            c0 = (jb % 4) * 256
            import os as _os
            kdbg = _os.environ.get("KDBG", "")
            for m_ in range(2):
                if kdbg == "actq":
                    sc.dma("act", (lambda dst=dst, c0=c0, ks3=ks3, m_=m_: S.dma_start(
                        out=dst[m_ * 128:(m_ + 1) * 128, c0:c0 + 256], in_=ks3[:, m_, :])),
                        "kvst%d_%d" % (jb % 2, m_), reads=[(kk, m_)])
                elif kdbg == "pp":
                    sc.dma("sp", (lambda dst=dst, c0=c0, m_=m_: SP.dma_start(
                        out=dst[m_ * 128:(m_ + 1) * 128, c0:c0 + 128], in_=pp[:, :])),
                        "kvst%d_%d" % (jb % 2, m_), reads=["pp"])
                elif kdbg == "nodep":
                    sc.dma("sp", (lambda dst=dst, c0=c0, ks3=ks3, m_=m_: SP.dma_start(
                        out=dst[m_ * 128:(m_ + 1) * 128, c0:c0 + 256], in_=ks3[:, m_, :])),
                        "kvst%d_%d" % (jb % 2, m_), reads=[])
                else:
                    sc.dma("sp", (lambda dst=dst, c0=c0, ks3=ks3, m_=m_: SP.dma_start(
                        out=dst[m_ * 128:(m_ + 1) * 128, c0:c0 + 256], in_=ks3[:, m_, :])),
                        "kvst%d_%d" % (jb % 2, m_), reads=[(kk, m_)])
            if kstop <= 0.8:
                return

    def run_pass(p):
        pc0 = HO = H
        sc0 = H + TP
        hsegs = [("p", HO, TP), ("s", sc0, TS)]
        hkeys_all = cells("hT", range(16), range(5))

        scr_reset()
        xstage = [scratch(D), scratch(D)]
        tl = [("h", (I.xh[:, :] if p == 0 else I.xp[480:512, :]), 32, 0, 5)]
        for i in range(4):
            r0 = p * 512 + i * 128
            tl.append(("p", I.xp[r0:r0 + 128, :], 128, HO + 128 * i, i))
        tl.append(("s", I.xs[p * 64:(p + 1) * 64, :], 64, sc0, 4))
        for n, (kind, src, rows, col, ti) in enumerate(tl):
            xst, xk = xstage[n % 2]
            sc.dma("sp", (lambda xst=xst, src=src, rows=rows: SP.dma_start(out=xst[0:rows, :], in_=src)),
                   "xst%d" % (n % 2), writes=[xk])
            norm_T(xst[0:rows, :], [xk], rows, 0, (lambda col=col, rows=rows: hT[:, :, col:col + rows]),
                   cells("hT", range(16), [ti]))

        if kstop <= 2 + 10 * p:
            return
        scr_reset()
        sg, sgk = scratch(2 * (H + T))
        sg3 = sg.rearrange("p (c t) -> p c t", c=2)
        scst, scstk = scratch(1024)
        gsegs = [("h", 0, H), ("p", HO, TP), ("s", sc0, TS)]
        import os as _os
        for g4 in range(2):
            if _os.environ.get("KDBG", "") in ("g0", "g0l") and g4 == 1:
                break
            s0 = p * 8 + g4 * 4
            sc.dma("sp", (lambda s0=s0: SP.dma_start(out=scst[0:120, :],
                                                     in_=I.scs[s0:s0 + 4].rearrange("b r f -> (b r) f"))),
                   "scst", writes=[scstk])
            if _os.environ.get("KDBG", "") == "g0l":
                break
            tr_group([(W3[:, j * 128:j * 128 + 120], scst[0:120, j * 128:(j + 1) * 128], ident_f[0:120, 0:120])
                      for j in range(8)], [scstk, "ident_f"], wide_keys(3))
            import os as _os
            for bb in range(4):
                if _os.environ.get("KDBG", "") == "nocsh":
                    break
                sc.dma("sp", (lambda s0=s0, bb=bb: SP.dma_start(out=I.css[s0 + bb, 0:22, :],
                                                               in_=scst[bb * 30 + 8:bb * 30 + 30, :])),
                       "csh%d" % bb, reads=[scstk])
            if _os.environ.get("KDBG", "") == "nocp":
                continue
            sc.op("dve", (lambda g4=g4: V.tensor_copy(
                out=cS[:, :, 4 * g4:4 * g4 + 4, 0:30],
                in_=W3[:, :].rearrange("p (j t) -> p j t", j=8)[:, :, 0:120].rearrange("p j (b r) -> p j b r", b=4))),
                wide_keys(3), cells("cS", range(8), [0]))
        for jb in range(4):
            slot, skey = wblock(I.w_in, 0, 16, 3072 + jb * 256)

            def epi_b(cl, sn, o, okey):
                lo = {"h": 0, "p": H, "s": H + TP}[sn]
                n = {"h": H, "p": TP, "s": TS}[sn]
                sc.op("act", lambda: S.activation(out=sg3[:, cl, lo:lo + n], in_=o, func=AF.Sigmoid),
                      okey, [(sgk, cl, sn)])
            proj_fm(slot, skey, 16, 2, hT, cells("hT", range(16), range(6)), gsegs, epi_b)
            slot, skey = wblock(I.w_in, 0, 16, 2048 + jb * 256)

            def epi_a(cl, sn, o, okey, jb=jb):
                j = 2 * jb + cl
                if sn == "h":
                    sc.op("dve", lambda: V.tensor_tensor(out=cT[:, j, 0:H], in0=o, in1=sg3[:, cl, 0:H], op=ALU.mult),
                          okey + [(sgk, cl, sn)], cells("cT", [j], [0]))
                elif sn == "p":
                    sc.op("dve", lambda: V.tensor_tensor(out=cT[:, j, H:H + TP], in0=o, in1=sg3[:, cl, H:H + TP],
                                                         op=ALU.mult),
                          okey + [(sgk, cl, sn)], cells("cT", [j], [0]))
                else:
                    sc.op("dve", lambda: V.tensor_tensor(
                        out=cS[:, j, :, 30:38], in0=o.rearrange("p (b t) -> p b t", b=8),
                        in1=sg3[:, cl, H + TP:H + T].rearrange("p (b t) -> p b t", b=8), op=ALU.mult),
                        okey + [(sgk, cl, sn)], cells("cS", [j], [0]))
            proj_fm(slot, skey, 16, 2, hT, cells("hT", range(16), range(6)), gsegs, epi_a)

        def conv_chunk(j):
            def conv_p():
                dc = mT_f[:, j, 0:TP]
                ins = V.tensor_scalar(out=dc, in0=cT[:, j, 2:2 + TP], scalar1=cw[:, j, 0:1],
                                      scalar2=pp[:, 96 + j:97 + j], op0=ALU.mult, op1=ALU.add)
                for t in range(1, 31):
                    ins = V.scalar_tensor_tensor(out=dc, in0=cT[:, j, 2 + t:2 + t + TP],
                                                 scalar=cw[:, j, t:t + 1], in1=dc, op0=ALU.mult, op1=ALU.add)
                return ins
            sc.op("dve", conv_p, cells("cT", [j], [0]) + ["cw", "pp"], cells("mT", [2 * j, 2 * j + 1], range(4)))

            dcs = mT_f[:, j, TP:T].rearrange("p (b t) -> p b t", b=8)
            ck = cells("mT", [2 * j, 2 * j + 1], [4])
            sc.op("dve", lambda: V.tensor_scalar(out=dcs, in0=cS[:, j, :, 0:8], scalar1=cw[:, j, 0:1],
                                                 scalar2=pp[:, 96 + j:97 + j], op0=ALU.mult, op1=ALU.add),
                  cells("cS", [j], [0]) + ["cw", "pp"], ck)
            for t in range(1, 31):
                sc.op("dve", (lambda t=t: V.scalar_tensor_tensor(out=dcs, in0=cS[:, j, :, t:t + 8],
                                                                 scalar=cw[:, j, t:t + 1], in1=dcs,
                                                                 op0=ALU.mult, op1=ALU.add)),
                      cells("cS", [j], [0]) + ["cw", "pp"], ck, sreads=ck)

        scr_reset()
        for jb in range(4):
            slot, skey = wblock(I.w_in, 0, 16, 4096 + jb * 256)

            def epi(cl, sn, o, okey, jb=jb):
                c = 2 * jb + cl
                if sn == "p":
                    sc.op("act", lambda: S.activation(out=qo[:, c, 0:TP], in_=o, func=AF.Copy), okey,
                          cells("qo", [c], range(4)))
                else:
                    sc.op("act", lambda: S.activation(out=qo[:, c, TP:T], in_=o, func=AF.Copy), okey,
                          cells("qo", [c], [4]))
            proj_fm(slot, skey, 16, 2, hT, hkeys_all, hsegs, epi)
            if jb in (1, 3):
                conv_chunk(jb // 2)

        p32, p32k = scratch(1024)
        pbf, pbfk = scratch(1024, BF16)
        pT, pTk = scratch(1024, BF16)

        def softmax(rows, heads, wk):
            st, sk = stat()
            for h in range(4):
                sc.op("dve", (lambda h=h: V.tensor_reduce(out=st[0:rows, h:h + 1], in_=heads[h],
                                                          axis=AX.X, op=ALU.max)), wk, [sk])
            sc.op("dve", lambda: V.tensor_scalar_mul(out=st[0:rows, 4:8], in0=st[0:rows, 0:4], scalar1=-1.0 / 16),
                  [], [sk], sreads=[sk])
            sc.op("dve", lambda: V.memset(st[0:rows, 8:12], 0.0), [], [sk], sreads=[sk])
            for h in range(4):
                sc.op("act", (lambda h=h: S.activation(out=p32[0:rows, h * 256:(h + 1) * 256],
                                                       in_=heads[h], func=AF.Exp,
                                                       bias=st[0:rows, 4 + h:5 + h], scale=1.0 / 16,
                                                       accum_out=st[0:rows, 8 + h:9 + h])),
                      wk + [sk], [sk, p32k])
            sc.op("dve", lambda: V.reciprocal(out=st[0:rows, 12:16], in_=st[0:rows, 8:12]), [sk], [sk])
            for h in range(4):
                sc.op("dve", (lambda h=h: V.tensor_scalar_mul(out=pbf[0:rows, h * 256:(h + 1) * 256],
                                                              in0=p32[0:rows, h * 256:(h + 1) * 256],
                                                              scalar1=st[0:rows, 12 + h:13 + h])),
                      [p32k], [pbfk], sreads=[sk])
            tr_group([(TB[:, j * 128:j * 128 + rows], pbf[0:rows, j * 128:(j + 1) * 128],
                       ident_bf[0:rows, 0:rows]) for j in range(8)], [pbfk, "ident_bf"], bk(4))
            sc.op("act", lambda: S.activation(
                out=pT.rearrange("p (j t) -> p j t", j=8)[:, :, 0:rows],
                in_=TB.rearrange("p (j t) -> p j t", j=8)[:, :, 0:rows], func=AF.Copy), bk(4), [pTk])

        for i in range(4):
            cols = slice(128 * i, 128 * i + 128)
            mms = []
            for h in range(4):
                for dc in range(2):
                    c = 2 * h + dc
                    mms.append((W3[:, h * 256:(h + 1) * 256], qo[:, c, cols], KT[:, c, :], dc == 0, dc == 1))
            mm_group(mms, cells("qo", range(8), [i]) + ["KT"], wide_keys(3))
            softmax(128, [W3[:, h * 256:(h + 1) * 256] for h in range(4)], wide_keys(3))
            mms = []
            for h in range(4):
                for dc in range(2):
                    c = 2 * h + dc
                    for mc in range(2):
                        mms.append((W[1][:, c * 128:(c + 1) * 128],
                                    Vp[:, mc, h * 256 + dc * 128:h * 256 + dc * 128 + 128],
                                    pT[:, (2 * h + mc) * 128:(2 * h + mc) * 128 + 128], mc == 0, mc == 1))
            mm_group(mms, [pTk, "Vp"], wide_keys(1))
            sc.op("dve", (lambda cols=cols: V.tensor_copy(out=qo[:, 8:16, cols],
                                                         in_=W[1][:, :].rearrange("p (c t) -> p c t", c=8))),
                  wide_keys(1), cells("qo", range(8, 16), [i]))

        shb = [2, 3, 6, 7]
        shk = bk(2) + bk(3) + bk(6) + bk(7)
        Kst = scratch(2048)
        KsT = [scratch(2048, BF16) for _ in range(2)]
        Vs = [scratch(2048, BF16) for _ in range(2)]
        qmb = [scratch(512, BF16) for _ in range(2)]
        for b in range(8):
            seq = p * 8 + b
            kst, kstk = Kst
            kst3 = kst.rearrange("p (m f) -> p m f", m=2)
            sc.dma("sp", (lambda seq=seq, kst3=kst3: SP.dma_start(
                out=kst3, in_=I.ck[seq].rearrange("(m p) f -> p m f", p=128))), "kst", writes=[kstk])
            kt, ktk = KsT[b % 2]
            kt3 = kt.rearrange("p (c m) -> p c m", c=8)
            for half in range(2):
                trs = []
                Wk = W[0]
                wkk = wide_keys(0)
                for cc in range(4):
                    c = half * 4 + cc
                    for mc in range(2):
                        trs.append((Wk[:, cc * 256 + mc * 128:cc * 256 + mc * 128 + 128],
                                    kst3[:, mc, c * 128:(c + 1) * 128], ident_f[:, :]))
                tr_group(trs, [kstk, "ident_f"], wkk)
                if half == 0:
                    sc.op("act", (lambda kt3=kt3: S.activation(
                        out=kt3[:, 0:4, :], in_=W[0][:, :].rearrange("p (c m) -> p c m", c=4), func=AF.Copy)),
                        wkk, [(ktk, 0)])
                else:
                    sc.op("dve", (lambda kt3=kt3: V.tensor_copy(
                        out=kt3[:, 4:8, :], in_=W[0][:, :].rearrange("p (c m) -> p c m", c=4))),
                        wkk, [(ktk, 1)])
            qm, qmk = qmb[b % 2]
            qm3 = qm.rearrange("p (c t) -> p c t", c=8)
            sc.op("dve", (lambda qm3=qm3, b=b: V.tensor_tensor(
                out=qm3, in0=qo[:, 0:8, TP:T],
                in1=maskq[:, b, :].unsqueeze(1).to_broadcast([128, 8, 64]), op=ALU.mult)),
                cells("qo", range(8), [4]) + ["maskq"], [qmk])
            mms = []
            for h in range(4):
                for dc in range(2):
                    c = 2 * h + dc
                    mms.append((bank(shb[h])[0:64, 0:256], qm3[:, c, :], kt3[:, c, :],
                                b == 0 and dc == 0, b == 7 and dc == 1))
            mm_group(mms, [qmk, (ktk, 0), (ktk, 1)], shk)
        softmax(64, [bank(shb[h])[0:64, 0:256] for h in range(4)], shk)
        pT3 = pT.rearrange("p (j t) -> p j t", j=8)
        for b in range(8):
            seq = p * 8 + b
            vs_, vsk = Vs[b % 2]
            vs3 = vs_.rearrange("p (m f) -> p m f", m=2)
            sc.dma("pool", (lambda seq=seq, vs3=vs3: G.dma_start(
                out=vs3, in_=I.cv[seq].rearrange("(m p) f -> p m f", p=128))), "vs%d" % (b % 2), writes=[vsk])
            mms = []
            for h in range(4):
                for dc in range(2):
                    c = 2 * h + dc
                    for mc in range(2):
                        mms.append((bank(0)[:, c * 64 + 8 * b:c * 64 + 8 * b + 8],
                                    vs3[:, mc, h * 256 + dc * 128:h * 256 + dc * 128 + 128],
                                    pT3[:, 2 * h + mc, 8 * b:8 * b + 8], mc == 0, mc == 1))
            mm_group(mms, [vsk, pTk], bk(0))
        sc.op("dve", lambda: V.tensor_copy(out=qo[:, 8:16, TP:T],
                                           in_=bank(0).rearrange("p (c t) -> p c t", c=8)),
              bk(0), cells("qo", range(8, 16), [4]))

        if kstop <= 3 + 10 * p:
            return
        scr_reset()
        for jb in range(4):
            slot, skey = wblock(I.w_in, 0, 16, jb * 256)

            def epi(cl, sn, o, okey, jb=jb):
                c = 2 * jb + cl
                if sn == "p":
                    sc.op("act", lambda: S.activation(out=uT[:, c, 0:TP], in_=o, func=AF.Gelu_apprx_tanh),
                          okey, cells("uT", [c], range(4)))
                else:
                    sc.op("act", lambda: S.activation(out=uT[:, c, TP:T], in_=o, func=AF.Gelu_apprx_tanh),
                          okey, cells("uT", [c], [4]))
            proj_fm(slot, skey, 16, 2, hT, hkeys_all, hsegs, epi)
            if jb in (1, 3):
                conv_chunk(2 + jb // 2)
        sc.dma("sp", lambda: SP.dma_start(out=bc[:, 0:1024], in_=I.sgu_ln_g.partition_broadcast(128)), "bc0",
               writes=["bc"])
        sc.dma("sp", lambda: SP.dma_start(out=bc[:, 1024:2048], in_=I.sgu_ln_b.partition_broadcast(128)), "bc1",
               writes=["bc"])
        gv = [scratch(1024) for _ in range(5)]
        vb = [scratch(1024, BF16) for _ in range(2)]
        tmpm, tmpk = scratch(1024)
        vtiles = [(i, HO + 128 * i, 128) for i in range(4)] + [(4, sc0, 64)]
        for jb in range(4):
            slot, skey = wblock(I.w_in, 0, 16, 1024 + jb * 256)

            def epi_tm(ti, rows, o, okeys, jb=jb):
                g_, gk = gv[ti]
                sc.op("act", lambda: S.activation(out=g_[0:rows, jb * 256:(jb + 1) * 256], in_=o,
                                                  func=AF.Gelu_apprx_tanh), okeys, [(gk, jb)])
            proj_tm(slot, skey, 16, hT, lambda ti: cells("hT", range(16), [ti]), 0, vtiles, epi_tm)
            conv_chunk(4 + jb)
        def sgu_tile(ti, tc0, rows):
            g_, gk = gv[ti]
            gkeys = [(gk, j) for j in range(4)]
            st, sk = ln_stats(g_[0:rows, 0:512], g_[0:rows, 512:1024], rows, gkeys)
            sc.op("dve", lambda: V.tensor_scalar(out=g_[0:rows, :], in0=g_[0:rows, :], scalar1=st[0:rows, 12:13],
                                                 scalar2=st[0:rows, 15:16], op0=ALU.subtract, op1=ALU.mult),
                  gkeys, gkeys, sreads=[sk])
            sc.op("dve", lambda: V.tensor_tensor(out=g_[0:rows, :], in0=g_[0:rows, :], in1=bc[0:rows, 0:1024],
                                                 op=ALU.mult), gkeys + ["bc"], gkeys)
            sc.op("dve", lambda: V.tensor_tensor(out=g_[0:rows, :], in0=g_[0:rows, :], in1=bc[0:rows, 1024:2048],
                                                 op=ALU.add), gkeys + ["bc"], gkeys)
            v_, vk = vb[ti % 2]
            sc.op("act", lambda: S.activation(out=v_[0:rows, :], in_=g_[0:rows, :], func=AF.Copy), gkeys, [vk])
            if ti == 4:
                sc.dma("sp", lambda: SP.dma_start(out=I.cvs[p * 64:(p + 1) * 64, :], in_=g_[0:64, :]), "cvs",
                       reads=gkeys)
                mms = [(W3[:, g * 64:(g + 1) * 64], v_[0:64, g * 128:(g + 1) * 128], WsTs[0:64, g, :], True, True)
                       for g in range(8)]
                mm_group(mms, [vk, "WsTs"], wide_keys(3))
                sc.op("dve", lambda: V.tensor_tensor(
                    out=tmpm[:, 0:512].rearrange("p (g b t) -> p g b t", g=8, b=8),
                    in0=W3[:, 0:512].rearrange("p (g b t) -> p g b t", g=8, b=8),
                    in1=biasbc[:, :, 0:8].unsqueeze(2).to_broadcast([128, 8, 8, 8]), op=ALU.add),
                    wide_keys(3) + ["biasbc"], [tmpk])
                sc.op("dve", lambda: V.tensor_tensor(out=uT[:, :, TP:T], in0=uT[:, :, TP:T],
                                                     in1=tmpm[:, 0:512].rearrange("p (g t) -> p g t", g=8),
                                                     op=ALU.mult),
                      [tmpk] + cells("uT", range(8), [4]), cells("uT", range(8), [4]))
            else:
                mms = [(W3[:, g * 128:(g + 1) * 128], v_[:, g * 128:(g + 1) * 128], WsT[:, g, :], True, True)
                       for g in range(8)]
                mm_group(mms, [vk, "WsT"], wide_keys(3))
                sc.op("dve", lambda: V.tensor_tensor(out=tmpm.rearrange("p (g t) -> p g t", g=8),
                                                     in0=W3[:, :].rearrange("p (g t) -> p g t", g=8),
                                                     in1=biasbc[:, :, :], op=ALU.add),
                      wide_keys(3) + ["biasbc"], [tmpk])
                cs_ = slice(128 * ti, 128 * ti + 128)
                sc.op("dve", (lambda cs_=cs_: V.tensor_tensor(out=uT[:, :, cs_], in0=uT[:, :, cs_],
                                                             in1=tmpm.rearrange("p (g t) -> p g t", g=8),
                                                             op=ALU.mult)),
                      [tmpk] + cells("uT", range(8), [ti]), cells("uT", range(8), [ti]))

        for (ti, tc0, rows) in vtiles:
            sgu_tile(ti, tc0, rows)
        if kstop <= 4 + 10 * p:
            return

        scr_reset()
        nb = [scratch(1024, BF16) for _ in range(2)]
        cso, csok = scratch(1024)
        cnew, cnk = scratch(512)
        sc.op("dve", lambda: V.tensor_copy(out=cnew.rearrange("p (j b t) -> p j b t", j=8, b=8),
                                           in_=cS[:, :, :, 30:38]), cells("cS", range(8), [0]), [cnk])
        tr_group([(W3[0:64, j * 128:(j + 1) * 128], cnew[:, j * 64:(j + 1) * 64], ident_f[:, :]) for j in range(8)],
                 [cnk, "ident_f"], wide_keys(3))
        sc.op("act", lambda: S.activation(out=cso[0:64, :], in_=W3[0:64, :], func=AF.Copy), wide_keys(3), [csok])
        for b in range(8):
            sc.dma("sp", (lambda b=b: SP.dma_start(out=I.css[p * 8 + b, 22:30, :], in_=cso[8 * b:8 * b + 8, :])),
                   "cso", reads=[csok])
        if p == 1:
            csp_s, cspk = scratch(1024)
            tr_group([(W3[0:32, j * 128:(j + 1) * 128], cT[:, j, TP:TP + H], ident_f[:, :]) for j in range(8)],
                     cells("cT", range(8), [0]) + ["ident_f"], wide_keys(3))
            sc.op("act", lambda: S.activation(out=csp_s[0:32, :], in_=W3[0:32, :], func=AF.Copy),
                  wide_keys(3), [cspk])
            sc.dma("sp", lambda: SP.dma_start(out=I.csp[:, :], in_=csp_s[2:32, :]), "csp", reads=[cspk])
        def ln_tile(ti, tcol, rows):
            tr_group([(W3[0:rows, j * 128:(j + 1) * 128], mT_f[:, j, tcol:tcol + rows], ident_f[:, :])
                      for j in range(8)], cells("mT", range(16), [ti]) + ["ident_f"], wide_keys(3))
            st, sk = ln_stats(W3[0:rows, 0:512], W3[0:rows, 512:1024], rows, wide_keys(3))
            n_, nk = nb[ti % 2]
            sc.op("dve", lambda: V.tensor_scalar(out=n_[0:rows, :], in0=W3[0:rows, :], scalar1=st[0:rows, 12:13],
                                                 scalar2=st[0:rows, 15:16], op0=ALU.subtract, op1=ALU.mult),
                  wide_keys(3), [nk], sreads=[sk])
            tr_group([(TB[:, j * 128:j * 128 + rows], n_[0:rows, j * 128:(j + 1) * 128], ident_bf[0:rows, 0:rows])
                      for j in range(8)], [nk, "ident_bf"], bk(4))
            for j in range(8):
                sc.op("act", (lambda j=j: S.activation(out=cbT[:, j, tcol:tcol + rows],
                                                       in_=TB[:, j * 128:j * 128 + rows], func=AF.Silu,
                                                       scale=pp[:, 104 + j:105 + j], bias=pp[:, 112 + j:113 + j])),
                      bk(4) + ["pp"], cells("cbT", [j], [ti]))

        for (ti, tcol, rows) in [(i, 128 * i, 128) for i in range(4)] + [(4, TP, 64)]:
            ln_tile(ti, tcol, rows)
        if kstop <= 5 + 10 * p:
            return

        scr_reset()
        sgg = [scratch(2 * T) for _ in range(2)]
        macc, mak = scratch(2 * T)
        tmpg, tgk = scratch(2 * T)
        macc3 = macc.rearrange("p (c t) -> p c t", c=2)
        tmpg3 = tmpg.rearrange("p (c t) -> p c t", c=2)
        branches = [(I.w_a, uT, "uT", range(8)), (I.w_b, cbT, "cbT", range(8)), (I.w_c, qo, "qo", range(8, 16))]
        psegs = [("p", 0, TP), ("s", TP, TS)]
        for jg in range(8):
            for br in range(3):
                slot, skey = wblock(I.w_in, 0, 16, 5120 + br * 2048 + jg * 256)
                sg_, sgk_ = sgg[br % 2]
                sgx = sg_.rearrange("p (c t) -> p c t", c=2)

                def epi_g(cl, sn, o, okey, br=br, jg=jg, sgx=sgx, sgk_=sgk_):
                    lo, n = (0, TP) if sn == "p" else (TP, TS)
                    col = 48 + br * 16 + 2 * jg + cl
                    sc.op("act", lambda: S.activation(out=sgx[:, cl, lo:lo + n], in_=o, func=AF.Sigmoid,
                                                      bias=pp[:, col:col + 1], scale=1.0),
                          okey + ["pp"], [(sgk_, cl, sn)])
                proj_fm(slot, skey, 16, 2, hT, hkeys_all, hsegs, epi_g)
                wsrc, act_t, aname, achunks = branches[br]
                slot, skey = wblock(wsrc, 0, 8, jg * 256)
                rhs_t = act_t if br < 2 else qo[:, 8:16, :]

                def epi_y(cl, sn, o, okey, br=br, jg=jg, sgx=sgx, sgk_=sgk_):
                    lo, n = (0, TP) if sn == "p" else (TP, TS)
                    c = 2 * jg + cl
                    tl_ = range(4) if sn == "p" else [4]
                    if br == 0:
                        sc.op("dve", lambda: V.tensor_tensor(out=macc3[:, cl, lo:lo + n], in0=o,
                                                             in1=sgx[:, cl, lo:lo + n], op=ALU.mult),
                              okey + [(sgk_, cl, sn)], [(mak, cl, sn)])
                    else:
                        sc.op("dve", lambda: V.tensor_tensor(out=tmpg3[:, cl, lo:lo + n], in0=o,
                                                             in1=sgx[:, cl, lo:lo + n], op=ALU.mult),
                              okey + [(sgk_, cl, sn)], [(tgk, cl, sn)])
                        if br == 1:
                            sc.op("dve", lambda: V.tensor_tensor(out=macc3[:, cl, lo:lo + n],
                                                                 in0=macc3[:, cl, lo:lo + n],
                                                                 in1=tmpg3[:, cl, lo:lo + n], op=ALU.add),
                                  [(mak, cl, sn)], [(mak, cl, sn)], sreads=[(tgk, cl, sn)])
                        else:
                            sc.op("dve", lambda: V.tensor_tensor(out=mT[:, c, lo:lo + n],
                                                                 in0=macc3[:, cl, lo:lo + n],
                                                                 in1=tmpg3[:, cl, lo:lo + n], op=ALU.add),
                                  [(mak, cl, sn)], cells("mT", [c], tl_), sreads=[(tgk, cl, sn)])
                proj_fm(slot, skey, 8, 2, rhs_t, cells(aname, achunks, range(5)), psegs, epi_y)

        if kstop <= 6 + 10 * p:
            return
        scr_reset()
        xt_list = [(i, 128 * i, 128) for i in range(4)] + [(4, TP, 64)]
        for (ti, tcol, rows) in xt_list:
            src = I.xp[p * 512 + ti * 128:p * 512 + ti * 128 + 128, :] if ti < 4 else I.xs[p * 64:(p + 1) * 64, :]
            sc.dma("sp", (lambda ti=ti, rows=rows, src=src: SP.dma_start(out=xres[0:rows, ti, :], in_=src)),
                   "xres%d" % ti, writes=[("xres", ti, n) for n in range(8)])
        for m_ in range(4):
            def epi_o(ti, rows, o, okeys, m_=m_):
                xk_ = [("xres", ti, 2 * m_), ("xres", ti, 2 * m_ + 1)]
                sc.op("dve", lambda: V.tensor_tensor(out=xres[0:rows, ti, m_ * 512:(m_ + 1) * 512],
                                                     in0=xres[0:rows, ti, m_ * 512:(m_ + 1) * 512], in1=o,
                                                     op=ALU.add),
                      okeys + xk_, xk_)
            proj_tm512(I.w_o, 0, m_ * 512, mT, lambda ti: cells("mT", range(16), [ti]), xt_list, epi_o)

        for (ti, tcol, rows) in xt_list:
            norm_T(xres[0:rows, ti, :], [("xres", ti, n) for n in range(8)], rows, 16,
                   (lambda tcol=tcol, rows=rows: hT[:, :, H + tcol:H + tcol + rows]), cells("hT", range(16), [ti]))

        if kstop <= 7 + 10 * p:
            return
        rl_ctr = [0]
        for r in range(4):
            for jb in range(8):
                slot, skey = wblock(I.w_up, 0, 16, r * 2048 + jb * 256)

                def epi_u(cl, sn, o, okey, jb=jb):
                    c = 2 * jb + cl
                    lo, n = (0, TP) if sn == "p" else (TP, TS)
                    tl_ = range(4) if sn == "p" else [4]
                    ri = rl_ctr[0] % 2
                    rl_ctr[0] += 1
                    rbuf = rlb[ri]
                    rk = [("rl", ri)]
                    sc.op("act", lambda: S.activation(out=rbuf[:, lo:lo + n], in_=o, func=AF.Relu), okey, rk)
                    sc.op("dve", lambda: V.tensor_tensor(out=qo[:, c, lo:lo + n], in0=rbuf[:, lo:lo + n],
                                                         in1=rbuf[:, lo:lo + n], op=ALU.mult),
                          rk, cells("qo", [c], tl_))
                proj_fm(slot, skey, 16, 2, hT, hkeys_all, hsegs, epi_u)
            for m_ in range(4):
                def epi_d(ti, rows, o, okeys, m_=m_):
                    xk_ = [("xres", ti, 2 * m_), ("xres", ti, 2 * m_ + 1)]
                    sc.op("dve", lambda: V.tensor_tensor(out=xres[0:rows, ti, m_ * 512:(m_ + 1) * 512],
                                                         in0=xres[0:rows, ti, m_ * 512:(m_ + 1) * 512], in1=o,
                                                         op=ALU.add),
                          okeys + xk_, xk_)
                proj_tm512(I.w_down, r * 2048, m_ * 512, qo, lambda ti: cells("qo", range(16), [ti]), xt_list, epi_d)

        sc.dma("sp", lambda: SP.dma_start(out=bc[:, :], in_=I.fin_g.partition_broadcast(128)), "bc0", writes=["bc"])
        def fin_tile(ti, tcol, rows):
            xk = [("xres", ti, n) for n in range(8)]
            i = xsb_ctr[0] % 2
            xsb_ctr[0] += 1
            xb = xsb[i]
            st, sk = stat()
            sc.op("dve", lambda: V.memset(st[:, 0:4], 0.0), writes=[sk])
            sc.op("act", lambda: S.activation(out=xb[0:rows, :], in_=xres[0:rows, ti, :], func=AF.Square,
                                              accum_out=st[0:rows, 0:1]), xk, [sk, ("xsb", i)])
            sc.op("act", lambda: S.activation(out=st[0:rows, 1:2], in_=st[0:rows, 0:1], func=AF.Sqrt,
                                              scale=1.0 / D, bias=epsb[0:rows, 0:1]), ["epsb"], [sk], sreads=[sk])
            sc.op("dve", lambda: V.reciprocal(out=st[0:rows, 2:3], in_=st[0:rows, 1:2]), [sk], [sk])
            sc.op("dve", lambda: V.scalar_tensor_tensor(out=xres[0:rows, ti, :], in0=xres[0:rows, ti, :],
                                                        scalar=st[0:rows, 2:3], in1=bc[0:rows, :],
                                                        op0=ALU.mult, op1=ALU.mult), xk + ["bc"], xk, sreads=[sk])
            dst = I.yp[p * 512 + ti * 128:p * 512 + ti * 128 + 128, :] if ti < 4 else I.ys[p * 64:(p + 1) * 64, :]
            sc.dma("sp", (lambda dst=dst, ti=ti, rows=rows: SP.dma_start(out=dst, in_=xres[0:rows, ti, :])),
                   "xres%d" % ti, reads=xk)
        for (ti, tcol, rows) in xt_list:
            fin_tile(ti, tcol, rows)

    with es:
        setup()
        if kstop > 0:
            mem_kv()
        for p in range(2):
            if kstop >= 2 + 10 * p:
                run_pass(p)
        if kstop < 99:
            dbg = nc.dram_tensor("dbg", [128, 2048], F32, kind="ExternalOutput").ap()
            dbt = bc
            sc.barrier()
            import os as _os
            sc.op("dve", lambda: V.memset(dbt[:, 0:2048], 0.0), [], ["dbt"])
        if kstop < 99 and not _os.environ.get("KNODBG"):
            sc.op("dve", lambda: V.tensor_copy(out=dbt[:, 0:64], in_=stats[:, 0:64]), [], ["dbt"])
            sc.op("dve", lambda: V.tensor_copy(out=dbt[:, 64:192], in_=hT[:, 0, 0:128]), [], ["dbt"])
            if kstop >= 3:
                sc.op("dve", lambda: V.tensor_copy(out=dbt[:, 1024:1264].rearrange("p (b r) -> p b r", b=8),
                                                   in_=cS[:, 3, :, 0:30]), [], ["dbt"])
                sc.op("dve", lambda: V.tensor_copy(out=dbt[:, 1264:1504].rearrange("p (b r) -> p b r", b=8),
                                                   in_=cS[:, 7, :, 0:30]), [], ["dbt"])
            sc.op("dve", lambda: V.tensor_copy(out=dbt[:, 192:320], in_=wring[0][:, 0, 0:128]), [], ["dbt"])
            sc.op("dve", lambda: V.tensor_copy(out=dbt[:, 320:448], in_=KT[:, 0, 0:128]), [], ["dbt"])
            sc.op("dve", lambda: V.tensor_copy(out=dbt[:, 448:576], in_=xres_flat[:, 4096:4096 + 128]), [], ["dbt"])
            sc.op("dve", lambda: V.tensor_copy(out=dbt[:, 576:696], in_=pp[:, 0:120]), [], ["dbt"])
            sc.op("dve", lambda: V.tensor_copy(out=dbt[:, 704:832], in_=xsb[1][:, 0:128]), [], ["dbt"])
            sc.op("dve", lambda: V.tensor_copy(out=dbt[:, 192:320], in_=ident_bf[:, 0:128]), [], ["dbt"])
            sc.op("dve", lambda: V.tensor_copy(out=dbt[:, 320:448], in_=TBw[:, 0:128]), [], ["dbt"])
            sc.op("dve", lambda: V.tensor_copy(out=dbt[:, 832:864], in_=cw[:, 1, 0:32]), [], ["dbt"])
            sc.op("dve", lambda: V.tensor_copy(out=dbt[0:64, 864:928], in_=WsTs[:, 1, 0:64]), [], ["dbt"])
            sc.op("dve", lambda: V.tensor_copy(out=dbt[:, 928:1024], in_=ident_f[:, 0:96]), [], ["dbt"])
        if kstop < 99:
            sc.barrier()
            sc.dma("sp", lambda: SP.dma_start(out=dbg[:, :], in_=dbt[:, 0:2048]), "dbg")
        block = es.enter_context(nc.Block())
        sc.emit(block)
    nc._io_names = (list(I._t), )
    return nc, sc


_CACHE = {}


def kernel(x_prompt, x_sample, mem_prompt, cache_mem_k, cache_mem_v, state_conv,
           attn_norm_g, w_in, b_gate, sgu_ln_g, sgu_ln_b, sgu_w, sgu_b, w_a_out,
           conv_w, conv_b, conv_ln_g, conv_ln_b, w_b_out, mem_norm_g, w_mem_kv,
           w_c_out, w_o, mlp_norm_g, w_up, w_down, final_norm_g):
    f = lambda a: np.ascontiguousarray(np.asarray(a, dtype=np.float32))
    if "nc" not in _CACHE:
        _CACHE["nc"] = build_program()[0]
    nc = _CACHE["nc"]
    x_prompt = f(x_prompt); x_sample = f(x_sample); mem_prompt = f(mem_prompt)
    ckf = f(cache_mem_k)[0].reshape(128, 256, 1024)
    cvf = f(cache_mem_v)[0].reshape(128, 256, 1024)
    scf = f(state_conv)[0]
    shared = {
        "attn_g": f(attn_norm_g)[0].reshape(16, 128), "mlp_g": f(mlp_norm_g)[0].reshape(16, 128),
        "mem_g": f(mem_norm_g)[0].reshape(16, 128), "w_in": f(w_in)[0], "b_gate": f(b_gate)[0].reshape(48, 128),
        "sgu_ln_g": f(sgu_ln_g)[0], "sgu_ln_b": f(sgu_ln_b)[0], "sgu_w": f(sgu_w)[0], "sgu_b": f(sgu_b)[0],
        "w_a": f(w_a_out)[0], "conv_w": f(conv_w)[0], "conv_b": f(conv_b)[0].reshape(8, 128),
        "conv_ln_g": f(conv_ln_g)[0].reshape(8, 128), "conv_ln_b": f(conv_ln_b)[0].reshape(8, 128),
        "w_b": f(w_b_out)[0], "w_kv": f(w_mem_kv)[0], "w_c": f(w_c_out)[0], "w_o": f(w_o)[0],
        "w_up": f(w_up)[0], "w_down": f(w_down)[0], "fin_g": f(final_norm_g),
    }
    in_maps = []
    for c in range(8):
        b, half = c // 2, c % 2
        m = dict(shared)
        m["xp"] = np.ascontiguousarray(x_prompt[b, half * 1024:(half + 1) * 1024])
        m["xh"] = (np.ascontiguousarray(x_prompt[b, 1024 - H:1024]) if half == 1
                   else np.zeros((H, D), np.float32))
        m["xs"] = np.ascontiguousarray(x_sample[c * 16:(c + 1) * 16].reshape(128, D))
        m["mem"] = np.ascontiguousarray(mem_prompt[b])
        m["ck"] = np.ascontiguousarray(ckf[c * 16:(c + 1) * 16])
        m["cv"] = np.ascontiguousarray(cvf[c * 16:(c + 1) * 16])
        m["scs"] = np.ascontiguousarray(scf[c * 16:(c + 1) * 16])
        in_maps.append(m)
    res = run_bass_kernel_spmd(nc, in_maps, core_ids=list(range(8)))
    R = res.results
    y_prompt = np.zeros((4, 2048, D), np.float32)
    y_sample = np.zeros((128, 8, D), np.float32)
    nk = np.zeros((1, 4, 256, 4, 256), np.float32)
    nv = np.zeros((1, 4, 256, 4, 256), np.float32)
    ncp = np.zeros((1, 4, 30, 1024), np.float32)
    ncs = np.zeros((1, 128, 30, 1024), np.float32)
    ncv = np.zeros((1, 128, 8, 1024), np.float32)
    for c in range(8):
        b, half = c // 2, c % 2
        y_prompt[b, half * 1024:(half + 1) * 1024] = R[c]["yp"]
        y_sample[c * 16:(c + 1) * 16] = R[c]["ys"].reshape(16, 8, D)
        if half == 0:
            nk[0, b] = R[c]["mk"].reshape(256, 4, 256)
            nv[0, b] = R[c]["mv"].reshape(256, 4, 256)
        else:
            ncp[0, b] = R[c]["csp"]
        ncs[0, c * 16:(c + 1) * 16] = R[c]["css"]
        ncv[0, c * 16:(c + 1) * 16] = R[c]["cvs"].reshape(16, 8, 1024)
    return (y_prompt, y_sample, nk, nv, ncp, ncs, ncv)
```

```python
import numpy as np
from contextlib import ExitStack
import concourse.bass as bass
import concourse.mybir as mybir
from concourse.bass_utils import run_bass_kernel_spmd

F32 = mybir.dt.float32
BF16 = mybir.dt.bfloat16
AF = mybir.ActivationFunctionType
ALU = mybir.AluOpType
AX = mybir.AxisListType

D = 2048
NIN = 11264
TP = 512
TS = 64
T = TP + TS
H = 32
EPS = 1e-6
NSLOT = 3
SAME_SYNC = False


class Sched:
    def __init__(self, nc, es):
        self.nc = nc
        self.engs = {"pe": nc.tensor, "act": nc.scalar, "dve": nc.vector,
                     "pool": nc.gpsimd, "sp": nc.sync}
        self.names = list(self.engs)
        self.sem = {e: es.enter_context(nc.semaphore("s_" + e)) for e in self.names}
        self.cnt = {e: 0 for e in self.names}
        self.q = {e: [] for e in self.names}
        self.known = {e: {} for e in self.names}
        self.snap = {e: [None] for e in self.names}
        self.dsem = {}
        self.dsnap = {}
        self.last_w = {}
        self.readers = {}
        self.pending = {e: [] for e in self.names}
        self.es = es
        self.nwaits = 0

    def _learn(self, e, ev):
        kn = self.known[e]
        k = (ev[0], ev[1])
        if kn.get(k, 0) < ev[2]:
            kn[k] = ev[2]
        other = self.snap[ev[1]][ev[2]] if ev[0] == "e" else self.dsnap[(ev[1], ev[2])]
        for kk, v in other.items():
            if kn.get(kk, 0) < v:
                kn[kk] = v

    def _deps(self, e, reads, writes, is_dma, use_pending=True, sreads=()):
        deps = {}
        sdeps = {}
        for k in sreads:
            w = self.last_w.get(k)
            if w and w[0] == "e" and w[1] == e:
                sdeps[(w[0], w[1])] = max(sdeps.get((w[0], w[1]), 0), w[2])

        def add(ev):
            k = (ev[0], ev[1])
            if deps.get(k, 0) < ev[2]:
                deps[k] = ev[2]
        if use_pending:
            for ev in self.pending[e]:
                add(ev)
            self.pending[e] = []
        for k in reads:
            w = self.last_w.get(k)
            if w:
                add(w)
        for k in writes:
            w = self.last_w.get(k)
            if w:
                add(w)
            for rk, rv in self.readers.get(k, {}).items():
                add((rk[0], rk[1], rv))
        waits = []
        for k_, v_ in sdeps.items():
            if self.known[e].get(k_, 0) < v_:
                waits.append((k_[0], k_[1], v_))
        for (ty, nm), v in deps.items():
            if ty == "e" and nm == e and not is_dma and (e == "pe" or not SAME_SYNC):
                continue
            if self.known[e].get((ty, nm), 0) >= v:
                continue
            waits.append((ty, nm, v))
        for ev in waits:
            self._learn(e, ev)
        self.nwaits += len(waits)
        return waits

    def _record(self, ev, reads, writes):
        for k in writes:
            self.last_w[k] = ev
            self.readers[k] = {}
        for k in reads:
            r = self.readers.setdefault(k, {})
            kk = (ev[0], ev[1])
            if r.get(kk, 0) < ev[2]:
                r[kk] = ev[2]

    def op(self, e, fn, reads=(), writes=(), sreads=()):
        reads = list(reads) + list(sreads)
        bankr = [k for k in reads if isinstance(k, tuple) and k[0] == "bank"]
        if bankr:
            reads = [k for k in reads if not (isinstance(k, tuple) and k[0] == "bank")]
            writes = list(writes) + bankr
        waits = self._deps(e, reads, writes, False, sreads=sreads)
        self.cnt[e] += 1
        n = self.cnt[e]
        s = dict(self.known[e])
        s[("e", e)] = max(s.get(("e", e), 0), n - 1)
        self.snap[e].append(s)
        self._record(("e", e, n), reads, writes)
        self.q[e].append((waits, fn, None))

    def dma(self, qn, fn, semname, reads=(), writes=(), use_pending=True):
        waits = self._deps(qn, reads, writes, True, use_pending)
        if semname not in self.dsem:
            self.dsem[semname] = [self.es.enter_context(self.nc.semaphore("d_" + semname)), 0]
        h = self.dsem[semname]
        h[1] += 16
        ev = ("d", semname, h[1])
        self.dsnap[(semname, h[1])] = dict(self.known[qn])
        self._record(ev, reads, writes)
        self.q[qn].append((waits, fn, h[0]))

    def barrier(self):
        evs = [("e", f, self.cnt[f]) for f in ("pe", "act", "dve", "pool") if self.cnt[f] > 0]
        evs += [("d", nm, h[1]) for nm, h in self.dsem.items() if nm not in ("w0a", "w1a", "w2a", "w0b", "w1b", "w2b")]
        for e in self.names:
            self.pending[e] = list(evs)

    def emit(self, block):
        def run(e):
            eng = self.engs[e]
            for waits, fn, dsem in self.q[e]:
                for (ty, nm, v) in waits:
                    eng.wait_ge(self.sem[nm] if ty == "e" else self.dsem[nm][0], v)
                ins = fn()
                if dsem is None:
                    ins.then_inc(self.sem[e], 1)
                else:
                    ins.then_inc(dsem, 16)
            if e == "sp":
                for f in ("pe", "act", "dve", "pool"):
                    if self.cnt[f]:
                        eng.wait_ge(self.sem[f], self.cnt[f])
                for nm, h in self.dsem.items():
                    eng.wait_ge(h[0], h[1])

        @block.tensor
        def _(x):
            run("pe")

        @block.scalar
        def _(x):
            run("act")

        @block.vector
        def _(x):
            run("dve")

        @block.gpsimd
        def _(x):
            run("pool")

        @block.sync
        def _(x):
            run("sp")


def build_program(kstop=99, io_names=None):
    nc = bass.Bass("TRN2", target_bir_lowering=False)
    es = ExitStack()
    es.enter_context(nc.allow_non_contiguous_dma(reason="small strided parameter loads"))

    IN_SHAPES = {
        "xp": [1024, D], "xh": [H, D], "xs": [128, D], "mem": [256, D],
        "ck": [16, 256, 1024], "cv": [16, 256, 1024], "scs": [16, 30, 1024],
        "attn_g": [16, 128], "mlp_g": [16, 128], "mem_g": [16, 128], "w_in": [D, NIN], "b_gate": [48, 128],
        "sgu_ln_g": [1024], "sgu_ln_b": [1024], "sgu_w": [8, 128, 128], "sgu_b": [8, 128],
        "w_a": [1024, D], "conv_w": [31, 1024], "conv_b": [8, 128], "conv_ln_g": [8, 128], "conv_ln_b": [8, 128],
        "w_b": [1024, D], "w_kv": [D, D], "w_c": [1024, D], "w_o": [D, D], "w_up": [D, 4 * D],
        "w_down": [4 * D, D], "fin_g": [D],
    }
    OUT_SHAPES = {"yp": [1024, D], "ys": [128, D], "mk": [256, 1024], "mv": [256, 1024],
                  "csp": [30, 1024], "css": [16, 30, 1024], "cvs": [128, 1024]}

    class _IO:
        def __init__(self):
            self._t = {}

        def __getattr__(self, name):
            t = self.__dict__["_t"]
            if name not in t:
                if name in IN_SHAPES:
                    t[name] = nc.dram_tensor(name, IN_SHAPES[name], F32, kind="ExternalInput").ap()
                elif name in OUT_SHAPES:
                    t[name] = nc.dram_tensor(name, OUT_SHAPES[name], F32, kind="ExternalOutput").ap()
                else:
                    raise AttributeError(name)
            return t[name]
    I = _IO()
    if io_names is None and kstop >= 99:
        io_names = list(IN_SHAPES) + list(OUT_SHAPES)
    for n_ in list(IN_SHAPES) + list(OUT_SHAPES):
        if io_names is not None and n_ in io_names:
            getattr(I, n_)

    def sb(name, shape, dt=F32):
        return es.enter_context(nc.sbuf_tensor(name, shape, dt))

    def ps(name, shape, dt=F32):
        return es.enter_context(nc.psum_tensor(name, shape, dt))

    hT = sb("hT", [128, 16, H + T], BF16)
    xres = sb("xres", [128, 5, D])
    wring = [sb("wr%d" % i, [128, 16, 256], BF16) for i in range(NSLOT)]
    qo = sb("qo", [128, 16, T], BF16)
    uT = sb("uT", [128, 8, T], BF16)
    cT = sb("cT", [128, 8, H + TP])
    cS = sb("cS", [128, 8, 8, 38])
    cbT = sb("cbT", [128, 8, T], BF16)
    mT = sb("mT", [128, 16, T], BF16)
    KT = sb("KT", [128, 8, 256], BF16)
    Vp = sb("Vp", [128, 2, 1024], BF16)
    bc = sb("bc", [128, D])
    xsb = [sb("xsb%d" % i, [128, D], BF16) for i in range(2)]
    ident_bf = sb("ident_bf", [128, 128], BF16)
    ident_f = sb("ident_f", [128, 128])
    trimask = sb("trimask", [128, 128])
    pp = sb("pp", [128, 128])
    cw = sb("cw", [128, 8, 32])
    WsT = sb("WsT", [128, 8, 128], BF16)
    WsTs = sb("WsTs", [64, 8, 64], BF16)
    biasbc = sb("biasbc", [128, 8, 128])
    maskq = sb("maskq", [128, 8, 64], BF16)
    epsb = sb("epsb", [128, 1])
    stats = sb("stats", [128, 1024])
    rlb = [sb("rl%d" % i, [128, T]) for i in range(2)]

    W0_ = ps("W0", [128, 1024]); W1_ = ps("W1", [128, 1024])
    TBw = ps("TBw", [128, 2048], BF16)
    W3_ = ps("W3", [128, 1024])
    W = [W0_, W1_, None, W3_]

    sc = Sched(nc, es)
    V, S, G, PE, SP = nc.vector, nc.scalar, nc.gpsimd, nc.tensor, nc.sync

    def bk(i):
        return [("bank", i)]

    def bank(i):
        assert i in (0, 1, 2, 3, 6, 7)
        return W[i // 2][:, (i % 2) * 512:(i % 2) * 512 + 512]

    TB = TBw[:, 0:1024]
    W3 = W[3]
    mT_f = mT[:, :, :].rearrange("p c t -> p (c t)").bitcast(F32).rearrange("p (c t) -> p c t", c=8)

    def cells(name, chunks, tiles):
        return [(name, c, t) for c in chunks for t in tiles]

    ALLT = range(5)
    stat_ctr = [0]

    def stat(n=16):
        i = stat_ctr[0] % 32
        stat_ctr[0] += 1
        return stats[:, i * 32:i * 32 + 32], ("stat", i)

    scr_off = [0]
    scr_gen = [0]
    xres_flat = xres[:, :, :].rearrange("p a b -> p (a b)")

    def scr_reset():
        import os as _os
        if not _os.environ.get("KNOB_NOBAR"):
            sc.barrier()
        scr_off[0] = 0
        scr_gen[0] += 1

    def scratch(nelem, dt=F32):
        n32 = nelem if dt == F32 else (nelem + 1) // 2
        n32 = (n32 + 7) // 8 * 8
        o = scr_off[0]
        assert o + n32 <= 5 * D, "scratch overflow"
        scr_off[0] += n32
        ap = xres_flat[:, o:o + n32]
        if dt != F32:
            ap = ap.bitcast(dt)[:, 0:nelem]
        else:
            ap = ap[:, 0:nelem]
        return ap, ("scr", scr_gen[0], o)

    def mm_group(mms, reads, writes):
        def fn():
            ins = None
            for (o, l, r, st, sp_) in mms:
                ins = PE.matmul(o, lhsT=l, rhs=r, start=st, stop=sp_)
            return ins
        sc.op("pe", fn, reads, writes)

    def tr_group(trs, reads, writes):
        def fn():
            ins = None
            for (o, i_, idn) in trs:
                ins = PE.transpose(out=o, in_=i_, identity=idn)
            o, i_, idn = trs[-1]
            if i_.dtype == F32:
                ins = PE.transpose(out=o, in_=i_, identity=idn)
            return ins
        sc.op("pe", fn, reads, writes)

    wcnt = [0]

    def wblock(wdram, r0, kcn, c0, ncol=256):
        s = wcnt[0] % NSLOT
        wcnt[0] += 1
        slot = wring[s]
        if ncol != 256:
            assert kcn * ncol <= 16 * 256
            slot = slot[:, :, :].rearrange("p a b -> p (a b)")[:, 0:kcn * ncol].rearrange("p (k n) -> p k n", k=kcn)
        src = wdram[r0:r0 + 128 * kcn, c0:c0 + ncol].rearrange("(k p) n -> p k n", p=128)
        hk = max(1, kcn // 2)
        keys = []
        for hi, (k0, k1) in enumerate([(0, hk), (hk, kcn)]):
            if k1 <= k0:
                continue
            key = ("wslot", s, hi)
            keys.append(key)
            sc.dma("pool", (lambda k0=k0, k1=k1: G.dma_start(out=slot[:, k0:k1, 0:ncol], in_=src[:, k0:k1, :])),
                   "w%d%s" % (s, "ab"[hi]), writes=[key], use_pending=False)
        return slot, keys

    acc_ctr = [0]
    ss_ctr = [0]

    def proj_fm(slot, skey, kcn, nchunks, rhs_t, rhs_keys, segs, epilogue):
        for cl in range(nchunks):
            outs = []
            for grp in ("p", "small"):
                mms = []
                wrs = []
                for (sn, c0, n) in segs:
                    if (sn == "p") != (grp == "p"):
                        continue
                    if sn == "p":
                        b = acc_ctr[0] % 3
                        acc_ctr[0] += 1
                        o = bank(b)[:, 0:n]
                        wr = bk(b)
                    else:
                        off = 0 if sn == "s" else 64
                        o = bank(3)[:, off:off + n]
                        wr = bk(3)
                    outs.append((sn, o, wr))
                    wrs += wr
                    for k in range(kcn):
                        mms.append((o, slot[:, k, cl * 128:(cl + 1) * 128], rhs_t[:, k, c0:c0 + n],
                                    k == 0, k == kcn - 1))
                if mms:
                    mm_group(mms, list(skey) + list(rhs_keys), list(dict.fromkeys(wrs)))
            for (sn, o, wr) in outs:
                epilogue(cl, sn, o, wr)

    tm_ctr = [0]

    def proj_tm(slot, skey, kcn, lhs_t, lhs_keys_fn, col0, tiles, epilogue, ncol=256):
        par = tm_ctr[0] % 2
        tm_ctr[0] += 1
        for (ti, tc0, rows) in tiles:
            bi = ti if ti < 4 else 6
            o = bank(bi)[0:rows, 0:ncol]
            wr = bk(bi)
            mms = [(o, lhs_t[:, k, tc0:tc0 + rows], slot[:, k, 0:ncol], k == 0, k == kcn - 1)
                   for k in range(kcn)]
            mm_group(mms, list(skey) + lhs_keys_fn(ti), wr)
            epilogue(ti, rows, o, wr)

    def proj_tm512(wdram, r0, c0, lhs_t, lhs_keys_fn, tiles, epilogue):
        blocks = [wblock(wdram, r0 + 1024 * hb, 8, c0, ncol=512) for hb in range(2)]
        outs = {}
        for hb, (slot, skey) in enumerate(blocks):
            for (ti, tc0, rows) in tiles:
                bi = ti if ti < 4 else 6
                o = bank(bi)[0:rows, 0:512]
                outs[ti] = (o, bk(bi), rows)
                mms = [(o, lhs_t[:, 8 * hb + k, tc0:tc0 + rows], slot[:, k, 0:512], hb == 0 and k == 0,
                        hb == 1 and k == 7) for k in range(8)]
                mm_group(mms, list(skey) + lhs_keys_fn(ti), bk(bi))
        for (ti, tc0, rows) in tiles:
            o, wr, rows = outs[ti]
            epilogue(ti, rows, o, wr)

    def wide_keys(i):
        return bk(2 * i) + bk(2 * i + 1)

    def setup():
        sc.op("pool", lambda: G.memset(ident_bf[:], 1.0), writes=["ident_bf"], sreads=["ident_bf"])
        sc.op("pool", lambda: G.affine_select(out=ident_bf[:], in_=ident_bf[:], pattern=[[-1, 128]],
                                              compare_op=ALU.is_equal, fill=0.0, base=0, channel_multiplier=1),
              writes=["ident_bf"], sreads=["ident_bf"])
        sc.op("pool", lambda: G.memset(ident_f[:], 1.0), writes=["ident_f"], sreads=["ident_f"])
        sc.op("pool", lambda: G.affine_select(out=ident_f[:], in_=ident_f[:], pattern=[[-1, 128]],
                                              compare_op=ALU.is_equal, fill=0.0, base=0, channel_multiplier=1),
              writes=["ident_f"], sreads=["ident_f"])
        sc.op("pool", lambda: G.memset(trimask[:], 1.0), writes=["trimask"], sreads=["trimask"])
        sc.op("pool", lambda: G.affine_select(out=trimask[:], in_=trimask[:], pattern=[[1, 128]],
                                              compare_op=ALU.is_ge, fill=0.0, base=0, channel_multiplier=-1),
              writes=["trimask"], sreads=["trimask"])
        sc.op("pool", lambda: G.memset(epsb[:], EPS), writes=["epsb"])
        sc.op("pool", lambda: G.memset(stats[:], 0.0), writes=[("stat", i) for i in range(64)])
        sc.op("pool", lambda: G.memset(maskq[:], 0.0), writes=["maskq"], sreads=["maskq"])
        for b in range(8):
            sc.op("pool", (lambda b=b: G.memset(maskq[:, b, 8 * b:8 * b + 8], 1.0)), writes=["maskq"], sreads=["maskq"])

        prm_t, _ = scratch(128)
        rows = [(I.attn_g, 0, 16), (I.mlp_g, 16, 16), (I.mem_g, 32, 16), (I.b_gate, 48, 48),
                (I.conv_b, 96, 8), (I.conv_ln_g, 104, 8), (I.conv_ln_b, 112, 8)]
        for i, (src, r0, n) in enumerate(rows):
            sc.dma("sp", (lambda src=src, r0=r0, n=n: SP.dma_start(out=prm_t[r0:r0 + n, :], in_=src[:, :])),
                   "prm%d" % i, writes=["prm"])
        tr_group([(W3[:, 0:120], prm_t[0:120, :], ident_f[0:120, 0:120])], ["prm", "ident_f"], wide_keys(3))
        sc.op("dve", lambda: V.tensor_copy(out=pp[:, 0:120], in_=W3[:, 0:120]), wide_keys(3), ["pp"])
        cwst, _ = scratch(1024)
        sc.dma("sp", lambda: SP.dma_start(out=cwst[0:31, :], in_=I.conv_w[:, :]), "cwst", writes=["cwst"])
        tr_group([(W3[:, j * 32:j * 32 + 31], cwst[0:31, j * 128:(j + 1) * 128], ident_f[0:31, 0:31])
                  for j in range(8)], ["cwst", "ident_f"], wide_keys(3))
        sc.op("dve", lambda: V.tensor_copy(out=cw[:, :, 0:31],
                                           in_=W3[:, 0:256].rearrange("p (j t) -> p j t", j=8)[:, :, 0:31]),
              wide_keys(3), ["cw"])
        wnat, _ = scratch(1024)
        wn3 = wnat.rearrange("p (g s) -> p g s", g=8)
        sc.dma("sp", lambda: SP.dma_start(out=wn3, in_=I.sgu_w.rearrange("g t s -> t g s")), "wnat", writes=["wnat"])
        tr_group([(W3[:, g * 128:(g + 1) * 128], wn3[:, g, :], ident_f[:, :]) for g in range(8)],
                 ["wnat", "ident_f"], wide_keys(3))
        sc.op("dve", lambda: V.tensor_tensor(out=WsT[:, :, :], in0=W3[:, :].rearrange("p (g t) -> p g t", g=8),
                                             in1=trimask[:, :].unsqueeze(1).to_broadcast([128, 8, 128]), op=ALU.mult),
              wide_keys(3) + ["trimask"], ["WsT"])
        a32, _ = scratch(64)
        abf, _ = scratch(64, BF16)
        ebf, _ = scratch(64, BF16)
        mblk, _ = scratch(64)
        for g in range(8):
            sc.dma("sp", (lambda g=g: SP.dma_start(out=a32[0:8, g * 8:(g + 1) * 8],
                                                   in_=I.sgu_w[g, 0:8, 0:8].rearrange("t s -> s t"))),
                   "wssg%d" % g, writes=[("a32", g)])
        sc.op("dve", lambda: V.tensor_copy(out=abf[0:8, :], in_=a32[0:8, :]), [("a32", g) for g in range(8)], ["abf"])
        sc.op("pool", lambda: G.memset(ebf[0:8, :], 1.0), writes=["ebf"], sreads=["ebf"])
        sc.op("pool", lambda: G.affine_select(out=ebf[0:8, :].rearrange("p (b j) -> p b j", b=8),
                                              in_=ebf[0:8, :].rearrange("p (b j) -> p b j", b=8),
                                              pattern=[[0, 8], [1, 8]], compare_op=ALU.is_equal, fill=0.0,
                                              base=0, channel_multiplier=-1), writes=["ebf"], sreads=["ebf"])
        sc.op("pool", lambda: G.memset(mblk[0:64, :], 1.0), writes=["mblk"], sreads=["mblk"])
        sc.op("pool", lambda: G.affine_select(out=mblk[0:64, :].rearrange("p (b t) -> p b t", b=8),
                                              in_=mblk[0:64, :].rearrange("p (b t) -> p b t", b=8),
                                              pattern=[[8, 8], [1, 8]], compare_op=ALU.is_ge, fill=0.0,
                                              base=0, channel_multiplier=-1), writes=["mblk"], sreads=["mblk"])
        sc.op("pool", lambda: G.affine_select(out=mblk[0:64, :].rearrange("p (b t) -> p b t", b=8),
                                              in_=mblk[0:64, :].rearrange("p (b t) -> p b t", b=8),
                                              pattern=[[-8, 8], [0, 8]], compare_op=ALU.is_ge, fill=0.0,
                                              base=0, channel_multiplier=1), writes=["mblk"], sreads=["mblk"])
        mm_group([(W3[0:64, 0:64], ebf[0:8, 0:64], abf[0:8, 0:64], True, True)], ["ebf", "abf"], wide_keys(3))
        sc.op("dve", lambda: V.tensor_tensor(
            out=WsTs[:, :, :].rearrange("p g (b t) -> p g b t", b=8),
            in0=W3[0:64, 0:64].rearrange("p (g t) -> p g t", g=8).unsqueeze(2).to_broadcast([64, 8, 8, 8]),
            in1=mblk[0:64, :].rearrange("p (b t) -> p b t", b=8).unsqueeze(1).to_broadcast([64, 8, 8, 8]),
            op=ALU.mult), wide_keys(3) + ["mblk"], ["WsTs"])
        sc.dma("sp", lambda: SP.dma_start(out=biasbc[:, :, :], in_=I.sgu_b.partition_broadcast(128)), "biasbc",
               writes=["biasbc"])

    xsb_ctr = [0]

    def norm_T(src, src_keys, rows, gcol, dst_fn, dst_keys):
        i = xsb_ctr[0] % 2
        xsb_ctr[0] += 1
        xb = xsb[i]
        xk = ("xsb", i)
        st, sk = stat()
        sc.op("dve", lambda: V.memset(st[:, 0:4], 0.0), writes=[sk])
        sc.op("act", lambda: S.activation(out=xb[0:rows, :], in_=src, func=AF.Square,
                                          accum_out=st[0:rows, 0:1]), list(src_keys), [sk, xk])
        sc.op("act", lambda: S.activation(out=st[0:rows, 1:2], in_=st[0:rows, 0:1], func=AF.Sqrt,
                                          scale=1.0 / D, bias=epsb[0:rows, 0:1]), ["epsb"], [sk], sreads=[sk])
        sc.op("dve", lambda: V.reciprocal(out=st[0:rows, 2:3], in_=st[0:rows, 1:2]), [sk], [sk])
        sc.op("dve", lambda: V.tensor_scalar_mul(out=xb[0:rows, :], in0=src, scalar1=st[0:rows, 2:3]),
              list(src_keys), [xk], sreads=[sk])
        tr_group([(TBw[:, k * 128:k * 128 + rows], xb[0:rows, k * 128:(k + 1) * 128], ident_bf[0:rows, 0:rows])
                  for k in range(16)], [xk, "ident_bf"], bk(4) + bk(5))
        src_ps = TBw[:, :].rearrange("p (k t) -> p k t", k=16)
        dst = dst_fn()
        dk = list(dst_keys)
        for k in range(16):
            if k < 8:
                sc.op("act", (lambda k=k: S.activation(out=dst[:, k, :], in_=src_ps[:, k, 0:rows], func=AF.Copy,
                                                       scale=pp[:, gcol + k:gcol + k + 1])),
                      bk(4) + ["pp"], dk)
            else:
                sc.op("dve", (lambda k=k: V.tensor_scalar_mul(out=dst[:, k, :], in0=src_ps[:, k, 0:rows],
                                                              scalar1=pp[:, gcol + k:gcol + k + 1])),
                      bk(5) + ["pp"], dk)

    def ln_stats(src0, src1, rows, src_keys):
        st, sk = stat()

        sk2 = (sk, "b")
        sc.op("dve", lambda: V.bn_stats(out=st[0:rows, 0:6], in_=src0), list(src_keys), [sk])
        sc.op("dve", lambda: V.bn_stats(out=st[0:rows, 16:22], in_=src1), list(src_keys), [sk2])
        sc.op("dve", lambda: V.bn_aggr(out=st[0:rows, 6:8], in_=st[0:rows, 0:6]), [], [sk], sreads=[sk])
        sc.op("dve", lambda: V.bn_aggr(out=st[0:rows, 8:10], in_=st[0:rows, 16:22]), [], [sk2], sreads=[sk2])
        sc.op("dve", lambda: V.tensor_tensor(out=st[0:rows, 10:12], in0=st[0:rows, 6:8], in1=st[0:rows, 8:10],
                                             op=ALU.add), [], [sk], sreads=[sk, sk2])
        sc.op("dve", lambda: V.tensor_scalar_mul(out=st[0:rows, 12:14], in0=st[0:rows, 10:12], scalar1=0.5),
              [], [sk], sreads=[sk])
        sc.op("dve", lambda: V.tensor_tensor(out=st[0:rows, 10:11], in0=st[0:rows, 6:7], in1=st[0:rows, 8:9],
                                             op=ALU.subtract), [], [sk], sreads=[sk])
        sc.op("dve", lambda: V.tensor_tensor(out=st[0:rows, 11:12], in0=st[0:rows, 10:11], in1=st[0:rows, 10:11],
                                             op=ALU.mult), [], [sk], sreads=[sk])
        sc.op("dve", lambda: V.scalar_tensor_tensor(out=st[0:rows, 13:14], in0=st[0:rows, 11:12], scalar=0.25,
                                                    in1=st[0:rows, 13:14], op0=ALU.mult, op1=ALU.add),
              [], [sk], sreads=[sk])
        sc.op("act", lambda: S.activation(out=st[0:rows, 14:15], in_=st[0:rows, 13:14], func=AF.Sqrt,
                                          scale=1.0, bias=epsb[0:rows, 0:1]), [sk, "epsb"], [sk])
        sc.op("dve", lambda: V.reciprocal(out=st[0:rows, 15:16], in_=st[0:rows, 14:15]), [sk], [sk])
        return st, sk

    def mem_kv():
        scr_reset()
        for mt in range(2):
            xst, xk = scratch(D)
            sc.dma("sp", (lambda mt=mt, xst=xst: SP.dma_start(out=xst[:, :], in_=I.mem[mt * 128:(mt + 1) * 128, :])),
                   "xst%d" % mt, writes=[xk])
            norm_T(xst[:, :], [xk], 128, 32, (lambda mt=mt: hT[:, :, mt * 128:(mt + 1) * 128]),
                   cells("hT", range(16), [mt]))
        if kstop <= 0.3:
            return
        kvst = [scratch(512) for _ in range(2)]
        for jb in range(8):
            slot, skey = wblock(I.w_kv, 0, 16, jb * 256)
            if kstop <= 0.5:
                sc.op("dve", lambda: V.tensor_copy(out=KT[:, 0, 0:256], in_=slot[:, 0, 0:256]), list(skey), ["KT"])
                return
            if jb < 4:
                def epi(cl, sn, o, okey, jb=jb):
                    c = 2 * jb + cl
                    sc.op("act", lambda: S.activation(out=KT[:, c, :], in_=o, func=AF.Copy), okey, ["KT"])
                proj_fm(slot, skey, 16, 2, hT, cells("hT", range(16), [0, 1]), [("p", 0, 256)], epi)
            if kstop <= 0.6:
                return
            ks, kk = kvst[jb % 2]
            ks3 = ks.rearrange("p (m c) -> p m c", m=2)

            def epi_tm(ti, rows, o, okeys, jb=jb, ks3=ks3, kk=kk):
                if jb < 4:
                    sc.op("dve", lambda: V.tensor_copy(out=ks3[:, ti, :], in_=o), okeys, [(kk, ti)])
                else:
                    sc.op("dve", lambda: V.tensor_copy(out=ks3[:, ti, :], in_=o), okeys, [(kk, ti)])
                    sc.op("act", lambda: S.activation(out=Vp[:, ti, (jb - 4) * 256:(jb - 3) * 256], in_=o,
                                                      func=AF.Copy), okeys, ["Vp"])
            import os as _os
            kd = _os.environ.get("KDBG", "")
            if kd == "noepi":
                epi_tm = lambda ti, rows, o, okeys: None
            proj_tm(slot, skey, 16, hT, lambda ti: cells("hT", range(16), [ti]), 0,
                    [(0, 0, 128)] if kd == "t0" else [(0, 0, 128), (1, 128, 128)], epi_tm)
            if kstop <= 0.7:
                return
            dst = I.mk if jb < 4 else I.mv
            c0 = (jb % 4) * 256
            import os as _os
            kdbg = _os.environ.get("KDBG", "")
            for m_ in range(2):
                if kdbg == "actq":
                    sc.dma("act", (lambda dst=dst, c0=c0, ks3=ks3, m_=m_: S.dma_start(
                        out=dst[m_ * 128:(m_ + 1) * 128, c0:c0 + 256], in_=ks3[:, m_, :])),
                        "kvst%d_%d" % (jb % 2, m_), reads=[(kk, m_)])
                elif kdbg == "pp":
                    sc.dma("sp", (lambda dst=dst, c0=c0, m_=m_: SP.dma_start(
                        out=dst[m_ * 128:(m_ + 1) * 128, c0:c0 + 128], in_=pp[:, :])),
                        "kvst%d_%d" % (jb % 2, m_), reads=["pp"])
                elif kdbg == "nodep":
                    sc.dma("sp", (lambda dst=dst, c0=c0, ks3=ks3, m_=m_: SP.dma_start(
                        out=dst[m_ * 128:(m_ + 1) * 128, c0:c0 + 256], in_=ks3[:, m_, :])),
                        "kvst%d_%d" % (jb % 2, m_), reads=[])
                else:
                    sc.dma("sp", (lambda dst=dst, c0=c0, ks3=ks3, m_=m_: SP.dma_start(
                        out=dst[m_ * 128:(m_ + 1) * 128, c0:c0 + 256], in_=ks3[:, m_, :])),
                        "kvst%d_%d" % (jb % 2, m_), reads=[(kk, m_)])
            if kstop <= 0.8:
                return

    def run_pass(p):
        pc0 = HO = H
        sc0 = H + TP
        hsegs = [("p", HO, TP), ("s", sc0, TS)]
        hkeys_all = cells("hT", range(16), range(5))

        scr_reset()
        xstage = [scratch(D), scratch(D)]
        tl = [("h", I.xh[:, :], 32, 0, 5)] if p == 0 else []
        for i in range(4):
            r0 = p * 512 + i * 128
            tl.append(("p", I.xp[r0:r0 + 128, :], 128, HO + 128 * i, i))
        tl.append(("s", I.xs[p * 64:(p + 1) * 64, :], 64, sc0, 4))
        for n, (kind, src, rows, col, ti) in enumerate(tl):
            xst, xk = xstage[n % 2]
            sc.dma("sp", (lambda xst=xst, src=src, rows=rows: SP.dma_start(out=xst[0:rows, :], in_=src)),
                   "xst%d" % (n % 2), writes=[xk])
            norm_T(xst[0:rows, :], [xk], rows, 0, (lambda col=col, rows=rows: hT[:, :, col:col + rows]),
                   cells("hT", range(16), [ti]))

        if kstop <= 2 + 10 * p:
            return
        scr_reset()
        sg, sgk = scratch(2 * (H + T))
        sg3 = sg.rearrange("p (c t) -> p c t", c=2)
        scst, scstk = scratch(1024)
        gsegs = ([("h", 0, H)] if p == 0 else []) + [("p", HO, TP), ("s", sc0, TS)]
        if p == 1:
            sc.op("dve", lambda: V.tensor_copy(out=cT[:, :, 0:H], in_=cT[:, :, TP:TP + H]),
                  cells("cT", range(8), [0]), cells("cT", range(8), [0]))
        import os as _os
        for g4 in range(2):
            if _os.environ.get("KDBG", "") in ("g0", "g0l") and g4 == 1:
                break
            s0 = p * 8 + g4 * 4
            sc.dma("sp", (lambda s0=s0: SP.dma_start(out=scst[0:120, :],
                                                     in_=I.scs[s0:s0 + 4].rearrange("b r f -> (b r) f"))),
                   "scst", writes=[scstk])
            if _os.environ.get("KDBG", "") == "g0l":
                break
            tr_group([(W3[:, j * 128:j * 128 + 120], scst[0:120, j * 128:(j + 1) * 128], ident_f[0:120, 0:120])
                      for j in range(8)], [scstk, "ident_f"], wide_keys(3))
            import os as _os
            for bb in range(4):
                if _os.environ.get("KDBG", "") == "nocsh":
                    break
                sc.dma("sp", (lambda s0=s0, bb=bb: SP.dma_start(out=I.css[s0 + bb, 0:22, :],
                                                               in_=scst[bb * 30 + 8:bb * 30 + 30, :])),
                       "csh%d" % bb, reads=[scstk])
            if _os.environ.get("KDBG", "") == "nocp":
                continue
            sc.op("dve", (lambda g4=g4: V.tensor_copy(
                out=cS[:, :, 4 * g4:4 * g4 + 4, 0:30],
                in_=W3[:, :].rearrange("p (j t) -> p j t", j=8)[:, :, 0:120].rearrange("p j (b r) -> p j b r", b=4))),
                wide_keys(3), cells("cS", range(8), [0]))
        for jb in range(4):
            slot, skey = wblock(I.w_in, 0, 16, 3072 + jb * 256)

            def epi_b(cl, sn, o, okey):
                lo = {"h": 0, "p": H, "s": H + TP}[sn]
                n = {"h": H, "p": TP, "s": TS}[sn]
                sc.op("act", lambda: S.activation(out=sg3[:, cl, lo:lo + n], in_=o, func=AF.Sigmoid),
                      okey, [(sgk, cl, sn)])
            proj_fm(slot, skey, 16, 2, hT, cells("hT", range(16), range(6)), gsegs, epi_b)
            slot, skey = wblock(I.w_in, 0, 16, 2048 + jb * 256)

            def epi_a(cl, sn, o, okey, jb=jb):
                j = 2 * jb + cl
                if sn == "h":
                    sc.op("dve", lambda: V.tensor_tensor(out=cT[:, j, 0:H], in0=o, in1=sg3[:, cl, 0:H], op=ALU.mult),
                          okey + [(sgk, cl, sn)], cells("cT", [j], [0]))
                elif sn == "p":
                    sc.op("dve", lambda: V.tensor_tensor(out=cT[:, j, H:H + TP], in0=o, in1=sg3[:, cl, H:H + TP],
                                                         op=ALU.mult),
                          okey + [(sgk, cl, sn)], cells("cT", [j], [0]))
                else:
                    sc.op("dve", lambda: V.tensor_tensor(
                        out=cS[:, j, :, 30:38], in0=o.rearrange("p (b t) -> p b t", b=8),
                        in1=sg3[:, cl, H + TP:H + T].rearrange("p (b t) -> p b t", b=8), op=ALU.mult),
                        okey + [(sgk, cl, sn)], cells("cS", [j], [0]))
            proj_fm(slot, skey, 16, 2, hT, cells("hT", range(16), range(6)), gsegs, epi_a)

        def conv_chunk(j):
            def conv_p():
                dc = mT_f[:, j, 0:TP]
                ins = V.tensor_scalar(out=dc, in0=cT[:, j, 2:2 + TP], scalar1=cw[:, j, 0:1],
                                      scalar2=pp[:, 96 + j:97 + j], op0=ALU.mult, op1=ALU.add)
                for t in range(1, 31):
                    ins = V.scalar_tensor_tensor(out=dc, in0=cT[:, j, 2 + t:2 + t + TP],
                                                 scalar=cw[:, j, t:t + 1], in1=dc, op0=ALU.mult, op1=ALU.add)
                return ins
            sc.op("dve", conv_p, cells("cT", [j], [0]) + ["cw", "pp"], cells("mT", [2 * j, 2 * j + 1], range(4)))

            dcs = mT_f[:, j, TP:T].rearrange("p (b t) -> p b t", b=8)
            ck = cells("mT", [2 * j, 2 * j + 1], [4])
            sc.op("dve", lambda: V.tensor_scalar(out=dcs, in0=cS[:, j, :, 0:8], scalar1=cw[:, j, 0:1],
                                                 scalar2=pp[:, 96 + j:97 + j], op0=ALU.mult, op1=ALU.add),
                  cells("cS", [j], [0]) + ["cw", "pp"], ck)
            for t in range(1, 31):
                sc.op("dve", (lambda t=t: V.scalar_tensor_tensor(out=dcs, in0=cS[:, j, :, t:t + 8],
                                                                 scalar=cw[:, j, t:t + 1], in1=dcs,
                                                                 op0=ALU.mult, op1=ALU.add)),
                      cells("cS", [j], [0]) + ["cw", "pp"], ck, sreads=ck)

        scr_reset()
        for jb in range(4):
            slot, skey = wblock(I.w_in, 0, 16, 4096 + jb * 256)

            def epi(cl, sn, o, okey, jb=jb):
                c = 2 * jb + cl
                if sn == "p":
                    sc.op("act", lambda: S.activation(out=qo[:, c, 0:TP], in_=o, func=AF.Copy), okey,
                          cells("qo", [c], range(4)))
                else:
                    sc.op("act", lambda: S.activation(out=qo[:, c, TP:T], in_=o, func=AF.Copy), okey,
                          cells("qo", [c], [4]))
            proj_fm(slot, skey, 16, 2, hT, hkeys_all, hsegs, epi)
            if jb in (1, 3):
                conv_chunk(jb // 2)

        p32, p32k = scratch(1024)
        pbf, pbfk = scratch(1024, BF16)
        pT, pTk = scratch(1024, BF16)

        def softmax(rows, heads, wk):
            st, sk = stat()
            for h in range(4):
                sc.op("dve", (lambda h=h: V.tensor_reduce(out=st[0:rows, h:h + 1], in_=heads[h],
                                                          axis=AX.X, op=ALU.max)), wk, [sk])
            sc.op("dve", lambda: V.tensor_scalar_mul(out=st[0:rows, 4:8], in0=st[0:rows, 0:4], scalar1=-1.0 / 16),
                  [], [sk], sreads=[sk])
            sc.op("dve", lambda: V.memset(st[0:rows, 8:12], 0.0), [], [sk], sreads=[sk])
            for h in range(4):
                sc.op("act", (lambda h=h: S.activation(out=p32[0:rows, h * 256:(h + 1) * 256],
                                                       in_=heads[h], func=AF.Exp,
                                                       bias=st[0:rows, 4 + h:5 + h], scale=1.0 / 16,
                                                       accum_out=st[0:rows, 8 + h:9 + h])),
                      wk + [sk], [sk, p32k])
            sc.op("dve", lambda: V.reciprocal(out=st[0:rows, 12:16], in_=st[0:rows, 8:12]), [sk], [sk])
            for h in range(4):
                sc.op("dve", (lambda h=h: V.tensor_scalar_mul(out=pbf[0:rows, h * 256:(h + 1) * 256],
                                                              in0=p32[0:rows, h * 256:(h + 1) * 256],
                                                              scalar1=st[0:rows, 12 + h:13 + h])),
                      [p32k], [pbfk], sreads=[sk])
            tr_group([(TB[:, j * 128:j * 128 + rows], pbf[0:rows, j * 128:(j + 1) * 128],
                       ident_bf[0:rows, 0:rows]) for j in range(8)], [pbfk, "ident_bf"], bk(4))
            sc.op("act", lambda: S.activation(
                out=pT.rearrange("p (j t) -> p j t", j=8)[:, :, 0:rows],
                in_=TB.rearrange("p (j t) -> p j t", j=8)[:, :, 0:rows], func=AF.Copy), bk(4), [pTk])

        for i in range(4):
            cols = slice(128 * i, 128 * i + 128)
            mms = []
            for h in range(4):
                for dc in range(2):
                    c = 2 * h + dc
                    mms.append((W3[:, h * 256:(h + 1) * 256], qo[:, c, cols], KT[:, c, :], dc == 0, dc == 1))
            mm_group(mms, cells("qo", range(8), [i]) + ["KT"], wide_keys(3))
            softmax(128, [W3[:, h * 256:(h + 1) * 256] for h in range(4)], wide_keys(3))
            mms = []
            for h in range(4):
                for dc in range(2):
                    c = 2 * h + dc
                    for mc in range(2):
                        mms.append((W[1][:, c * 128:(c + 1) * 128],
                                    Vp[:, mc, h * 256 + dc * 128:h * 256 + dc * 128 + 128],
                                    pT[:, (2 * h + mc) * 128:(2 * h + mc) * 128 + 128], mc == 0, mc == 1))
            mm_group(mms, [pTk, "Vp"], wide_keys(1))
            sc.op("dve", (lambda cols=cols: V.tensor_copy(out=qo[:, 8:16, cols],
                                                         in_=W[1][:, :].rearrange("p (c t) -> p c t", c=8))),
                  wide_keys(1), cells("qo", range(8, 16), [i]))

        shb = [2, 3, 6, 7]
        shk = bk(2) + bk(3) + bk(6) + bk(7)
        Ksb = [scratch(2048, BF16) for _ in range(2)]
        KsT = [scratch(2048, BF16) for _ in range(2)]
        Vs = [scratch(2048, BF16) for _ in range(2)]
        qmb = [scratch(512, BF16) for _ in range(2)]
        for b in range(8):
            seq = p * 8 + b
            kst, kstk = Ksb[b % 2]
            kst3 = kst.rearrange("p (m f) -> p m f", m=2)
            sc.dma("pool", (lambda seq=seq, kst3=kst3: G.dma_start(
                out=kst3, in_=I.ck[seq].rearrange("(m p) f -> p m f", p=128))), "ksb%d" % (b % 2), writes=[kstk])
            kt, ktk = KsT[b % 2]
            kt3 = kt.rearrange("p (c m) -> p c m", c=8)
            trs = []
            for c in range(8):
                for mc in range(2):
                    trs.append((TBw[:, c * 256 + mc * 128:c * 256 + mc * 128 + 128],
                                kst3[:, mc, c * 128:(c + 1) * 128], ident_bf[:, :]))
            tr_group(trs, [kstk, "ident_bf"], bk(4) + bk(5))
            sc.op("act", (lambda kt3=kt3: S.activation(
                out=kt3[:, 0:4, :], in_=TBw[:, 0:1024].rearrange("p (c m) -> p c m", c=4), func=AF.Copy)),
                bk(4), [(ktk, 0)])
            sc.op("dve", (lambda kt3=kt3: V.tensor_copy(
                out=kt3[:, 4:8, :], in_=TBw[:, 1024:2048].rearrange("p (c m) -> p c m", c=4))),
                bk(5), [(ktk, 1)])
            qm, qmk = qmb[b % 2]
            qm3 = qm.rearrange("p (c t) -> p c t", c=8)
            sc.op("dve", (lambda qm3=qm3, b=b: V.tensor_tensor(
                out=qm3, in0=qo[:, 0:8, TP:T],
                in1=maskq[:, b, :].unsqueeze(1).to_broadcast([128, 8, 64]), op=ALU.mult)),
                cells("qo", range(8), [4]) + ["maskq"], [qmk])
            mms = []
            for h in range(4):
                for dc in range(2):
                    c = 2 * h + dc
                    mms.append((bank(shb[h])[0:64, 0:256], qm3[:, c, :], kt3[:, c, :],
                                b == 0 and dc == 0, b == 7 and dc == 1))
            mm_group(mms, [qmk, (ktk, 0), (ktk, 1)], shk)
        softmax(64, [bank(shb[h])[0:64, 0:256] for h in range(4)], shk)
        pT3 = pT.rearrange("p (j t) -> p j t", j=8)
        for b in range(8):
            seq = p * 8 + b
            vs_, vsk = Vs[b % 2]
            vs3 = vs_.rearrange("p (m f) -> p m f", m=2)
            sc.dma("pool", (lambda seq=seq, vs3=vs3: G.dma_start(
                out=vs3, in_=I.cv[seq].rearrange("(m p) f -> p m f", p=128))), "vs%d" % (b % 2), writes=[vsk])
            mms = []
            for h in range(4):
                for dc in range(2):
                    c = 2 * h + dc
                    for mc in range(2):
                        mms.append((bank(0)[:, c * 64 + 8 * b:c * 64 + 8 * b + 8],
                                    vs3[:, mc, h * 256 + dc * 128:h * 256 + dc * 128 + 128],
                                    pT3[:, 2 * h + mc, 8 * b:8 * b + 8], mc == 0, mc == 1))
            mm_group(mms, [vsk, pTk], bk(0))
        sc.op("dve", lambda: V.tensor_copy(out=qo[:, 8:16, TP:T],
                                           in_=bank(0).rearrange("p (c t) -> p c t", c=8)),
              bk(0), cells("qo", range(8, 16), [4]))

        if kstop <= 3 + 10 * p:
            return
        scr_reset()
        for jb in range(4):
            slot, skey = wblock(I.w_in, 0, 16, jb * 256)

            def epi(cl, sn, o, okey, jb=jb):
                c = 2 * jb + cl
                if sn == "p":
                    sc.op("act", lambda: S.activation(out=uT[:, c, 0:TP], in_=o, func=AF.Gelu_apprx_tanh),
                          okey, cells("uT", [c], range(4)))
                else:
                    sc.op("act", lambda: S.activation(out=uT[:, c, TP:T], in_=o, func=AF.Gelu_apprx_tanh),
                          okey, cells("uT", [c], [4]))
            proj_fm(slot, skey, 16, 2, hT, hkeys_all, hsegs, epi)
            if jb in (1, 3):
                conv_chunk(2 + jb // 2)
        sc.dma("sp", lambda: SP.dma_start(out=bc[:, 0:1024], in_=I.sgu_ln_g.partition_broadcast(128)), "bc0",
               writes=["bc"])
        sc.dma("sp", lambda: SP.dma_start(out=bc[:, 1024:2048], in_=I.sgu_ln_b.partition_broadcast(128)), "bc1",
               writes=["bc"])
        gv = [scratch(1024) for _ in range(5)]
        vb = [scratch(1024, BF16) for _ in range(2)]
        tmpm, tmpk = scratch(1024)
        vtiles = [(i, HO + 128 * i, 128) for i in range(4)] + [(4, sc0, 64)]
        for jb in range(4):
            slot, skey = wblock(I.w_in, 0, 16, 1024 + jb * 256)

            def epi_tm(ti, rows, o, okeys, jb=jb):
                g_, gk = gv[ti]
                sc.op("act", lambda: S.activation(out=g_[0:rows, jb * 256:(jb + 1) * 256], in_=o,
                                                  func=AF.Gelu_apprx_tanh), okeys, [(gk, jb)])
            proj_tm(slot, skey, 16, hT, lambda ti: cells("hT", range(16), [ti]), 0, vtiles, epi_tm)
            conv_chunk(4 + jb)
        def sgu_tile(ti, tc0, rows):
            g_, gk = gv[ti]
            gkeys = [(gk, j) for j in range(4)]
            st, sk = ln_stats(g_[0:rows, 0:512], g_[0:rows, 512:1024], rows, gkeys)
            sc.op("dve", lambda: V.tensor_scalar(out=g_[0:rows, :], in0=g_[0:rows, :], scalar1=st[0:rows, 12:13],
                                                 scalar2=st[0:rows, 15:16], op0=ALU.subtract, op1=ALU.mult),
                  gkeys, gkeys, sreads=[sk])
            sc.op("dve", lambda: V.tensor_tensor(out=g_[0:rows, :], in0=g_[0:rows, :], in1=bc[0:rows, 0:1024],
                                                 op=ALU.mult), gkeys + ["bc"], gkeys)
            sc.op("dve", lambda: V.tensor_tensor(out=g_[0:rows, :], in0=g_[0:rows, :], in1=bc[0:rows, 1024:2048],
                                                 op=ALU.add), gkeys + ["bc"], gkeys)
            v_, vk = vb[ti % 2]
            sc.op("act", lambda: S.activation(out=v_[0:rows, :], in_=g_[0:rows, :], func=AF.Copy), gkeys, [vk])
            if ti == 4:
                sc.dma("sp", lambda: SP.dma_start(out=I.cvs[p * 64:(p + 1) * 64, :], in_=g_[0:64, :]), "cvs",
                       reads=gkeys)
                mms = [(W3[:, g * 64:(g + 1) * 64], v_[0:64, g * 128:(g + 1) * 128], WsTs[0:64, g, :], True, True)
                       for g in range(8)]
                mm_group(mms, [vk, "WsTs"], wide_keys(3))
                sc.op("dve", lambda: V.tensor_tensor(
                    out=tmpm[:, 0:512].rearrange("p (g b t) -> p g b t", g=8, b=8),
                    in0=W3[:, 0:512].rearrange("p (g b t) -> p g b t", g=8, b=8),
                    in1=biasbc[:, :, 0:8].unsqueeze(2).to_broadcast([128, 8, 8, 8]), op=ALU.add),
                    wide_keys(3) + ["biasbc"], [tmpk])
                sc.op("dve", lambda: V.tensor_tensor(out=uT[:, :, TP:T], in0=uT[:, :, TP:T],
                                                     in1=tmpm[:, 0:512].rearrange("p (g t) -> p g t", g=8),
                                                     op=ALU.mult),
                      [tmpk] + cells("uT", range(8), [4]), cells("uT", range(8), [4]))
            else:
                mms = [(W3[:, g * 128:(g + 1) * 128], v_[:, g * 128:(g + 1) * 128], WsT[:, g, :], True, True)
                       for g in range(8)]
                mm_group(mms, [vk, "WsT"], wide_keys(3))
                sc.op("dve", lambda: V.tensor_tensor(out=tmpm.rearrange("p (g t) -> p g t", g=8),
                                                     in0=W3[:, :].rearrange("p (g t) -> p g t", g=8),
                                                     in1=biasbc[:, :, :], op=ALU.add),
                      wide_keys(3) + ["biasbc"], [tmpk])
                cs_ = slice(128 * ti, 128 * ti + 128)
                sc.op("dve", (lambda cs_=cs_: V.tensor_tensor(out=uT[:, :, cs_], in0=uT[:, :, cs_],
                                                             in1=tmpm.rearrange("p (g t) -> p g t", g=8),
                                                             op=ALU.mult)),
                      [tmpk] + cells("uT", range(8), [ti]), cells("uT", range(8), [ti]))

        for (ti, tc0, rows) in vtiles:
            sgu_tile(ti, tc0, rows)
        if kstop <= 4 + 10 * p:
            return

        scr_reset()
        nb = [scratch(1024, BF16) for _ in range(2)]
        cso, csok = scratch(1024)
        cnew, cnk = scratch(512)
        sc.op("dve", lambda: V.tensor_copy(out=cnew.rearrange("p (j b t) -> p j b t", j=8, b=8),
                                           in_=cS[:, :, :, 30:38]), cells("cS", range(8), [0]), [cnk])
        tr_group([(W3[0:64, j * 128:(j + 1) * 128], cnew[:, j * 64:(j + 1) * 64], ident_f[:, :]) for j in range(8)],
                 [cnk, "ident_f"], wide_keys(3))
        sc.op("act", lambda: S.activation(out=cso[0:64, :], in_=W3[0:64, :], func=AF.Copy), wide_keys(3), [csok])
        for b in range(8):
            sc.dma("sp", (lambda b=b: SP.dma_start(out=I.css[p * 8 + b, 22:30, :], in_=cso[8 * b:8 * b + 8, :])),
                   "cso", reads=[csok])
        if p == 1:
            csp_s, cspk = scratch(1024)
            tr_group([(W3[0:32, j * 128:(j + 1) * 128], cT[:, j, TP:TP + H], ident_f[:, :]) for j in range(8)],
                     cells("cT", range(8), [0]) + ["ident_f"], wide_keys(3))
            sc.op("act", lambda: S.activation(out=csp_s[0:32, :], in_=W3[0:32, :], func=AF.Copy),
                  wide_keys(3), [cspk])
            sc.dma("sp", lambda: SP.dma_start(out=I.csp[:, :], in_=csp_s[2:32, :]), "csp", reads=[cspk])
        def ln_tile(ti, tcol, rows):
            tr_group([(W3[0:rows, j * 128:(j + 1) * 128], mT_f[:, j, tcol:tcol + rows], ident_f[:, :])
                      for j in range(8)], cells("mT", range(16), [ti]) + ["ident_f"], wide_keys(3))
            st, sk = ln_stats(W3[0:rows, 0:512], W3[0:rows, 512:1024], rows, wide_keys(3))
            n_, nk = nb[ti % 2]
            sc.op("dve", lambda: V.tensor_scalar(out=n_[0:rows, :], in0=W3[0:rows, :], scalar1=st[0:rows, 12:13],
                                                 scalar2=st[0:rows, 15:16], op0=ALU.subtract, op1=ALU.mult),
                  wide_keys(3), [nk], sreads=[sk])
            tr_group([(TB[:, j * 128:j * 128 + rows], n_[0:rows, j * 128:(j + 1) * 128], ident_bf[0:rows, 0:rows])
                      for j in range(8)], [nk, "ident_bf"], bk(4))
            for j in range(8):
                sc.op("act", (lambda j=j: S.activation(out=cbT[:, j, tcol:tcol + rows],
                                                       in_=TB[:, j * 128:j * 128 + rows], func=AF.Silu,
                                                       scale=pp[:, 104 + j:105 + j], bias=pp[:, 112 + j:113 + j])),
                      bk(4) + ["pp"], cells("cbT", [j], [ti]))

        for (ti, tcol, rows) in [(i, 128 * i, 128) for i in range(4)] + [(4, TP, 64)]:
            ln_tile(ti, tcol, rows)
        if kstop <= 5 + 10 * p:
            return

        scr_reset()
        sgg = [scratch(2 * T) for _ in range(2)]
        macc, mak = scratch(2 * T)
        tmpg, tgk = scratch(2 * T)
        macc3 = macc.rearrange("p (c t) -> p c t", c=2)
        tmpg3 = tmpg.rearrange("p (c t) -> p c t", c=2)
        branches = [(I.w_a, uT, "uT", range(8)), (I.w_b, cbT, "cbT", range(8)), (I.w_c, qo, "qo", range(8, 16))]
        psegs = [("p", 0, TP), ("s", TP, TS)]
        for jg in range(8):
            for br in range(3):
                slot, skey = wblock(I.w_in, 0, 16, 5120 + br * 2048 + jg * 256)
                sg_, sgk_ = sgg[br % 2]
                sgx = sg_.rearrange("p (c t) -> p c t", c=2)

                def epi_g(cl, sn, o, okey, br=br, jg=jg, sgx=sgx, sgk_=sgk_):
                    lo, n = (0, TP) if sn == "p" else (TP, TS)
                    col = 48 + br * 16 + 2 * jg + cl
                    sc.op("act", lambda: S.activation(out=sgx[:, cl, lo:lo + n], in_=o, func=AF.Sigmoid,
                                                      bias=pp[:, col:col + 1], scale=1.0),
                          okey + ["pp"], [(sgk_, cl, sn)])
                proj_fm(slot, skey, 16, 2, hT, hkeys_all, hsegs, epi_g)
                wsrc, act_t, aname, achunks = branches[br]
                slot, skey = wblock(wsrc, 0, 8, jg * 256)
                rhs_t = act_t if br < 2 else qo[:, 8:16, :]

                def epi_y(cl, sn, o, okey, br=br, jg=jg, sgx=sgx, sgk_=sgk_):
                    lo, n = (0, TP) if sn == "p" else (TP, TS)
                    c = 2 * jg + cl
                    tl_ = range(4) if sn == "p" else [4]
                    if br == 0:
                        sc.op("dve", lambda: V.tensor_tensor(out=macc3[:, cl, lo:lo + n], in0=o,
                                                             in1=sgx[:, cl, lo:lo + n], op=ALU.mult),
                              okey + [(sgk_, cl, sn)], [(mak, cl, sn)])
                    else:
                        sc.op("dve", lambda: V.tensor_tensor(out=tmpg3[:, cl, lo:lo + n], in0=o,
                                                             in1=sgx[:, cl, lo:lo + n], op=ALU.mult),
                              okey + [(sgk_, cl, sn)], [(tgk, cl, sn)])
                        if br == 1:
                            sc.op("dve", lambda: V.tensor_tensor(out=macc3[:, cl, lo:lo + n],
                                                                 in0=macc3[:, cl, lo:lo + n],
                                                                 in1=tmpg3[:, cl, lo:lo + n], op=ALU.add),
                                  [(mak, cl, sn)], [(mak, cl, sn)], sreads=[(tgk, cl, sn)])
                        else:
                            sc.op("dve", lambda: V.tensor_tensor(out=mT[:, c, lo:lo + n],
                                                                 in0=macc3[:, cl, lo:lo + n],
                                                                 in1=tmpg3[:, cl, lo:lo + n], op=ALU.add),
                                  [(mak, cl, sn)], cells("mT", [c], tl_), sreads=[(tgk, cl, sn)])
                proj_fm(slot, skey, 8, 2, rhs_t, cells(aname, achunks, range(5)), psegs, epi_y)

        if kstop <= 6 + 10 * p:
            return
        scr_reset()
        xt_list = [(i, 128 * i, 128) for i in range(4)] + [(4, TP, 64)]
        for (ti, tcol, rows) in xt_list:
            src = I.xp[p * 512 + ti * 128:p * 512 + ti * 128 + 128, :] if ti < 4 else I.xs[p * 64:(p + 1) * 64, :]
            sc.dma("sp", (lambda ti=ti, rows=rows, src=src: SP.dma_start(out=xres[0:rows, ti, :], in_=src)),
                   "xres%d" % ti, writes=[("xres", ti, n) for n in range(8)])
        for m_ in range(4):
            def epi_o(ti, rows, o, okeys, m_=m_):
                xk_ = [("xres", ti, 2 * m_), ("xres", ti, 2 * m_ + 1)]
                sc.op("dve", lambda: V.tensor_tensor(out=xres[0:rows, ti, m_ * 512:(m_ + 1) * 512],
                                                     in0=xres[0:rows, ti, m_ * 512:(m_ + 1) * 512], in1=o,
                                                     op=ALU.add),
                      okeys + xk_, xk_)
            proj_tm512(I.w_o, 0, m_ * 512, mT, lambda ti: cells("mT", range(16), [ti]), xt_list, epi_o)

        for (ti, tcol, rows) in xt_list:
            norm_T(xres[0:rows, ti, :], [("xres", ti, n) for n in range(8)], rows, 16,
                   (lambda tcol=tcol, rows=rows: hT[:, :, H + tcol:H + tcol + rows]), cells("hT", range(16), [ti]))

        if kstop <= 7 + 10 * p:
            return
        rl_ctr = [0]
        for r in range(4):
            for jb in range(8):
                slot, skey = wblock(I.w_up, 0, 16, r * 2048 + jb * 256)

                def epi_u(cl, sn, o, okey, jb=jb):
                    c = 2 * jb + cl
                    lo, n = (0, TP) if sn == "p" else (TP, TS)
                    tl_ = range(4) if sn == "p" else [4]
                    ri = rl_ctr[0] % 2
                    rl_ctr[0] += 1
                    rbuf = rlb[ri]
                    rk = [("rl", ri)]
                    sc.op("act", lambda: S.activation(out=rbuf[:, lo:lo + n], in_=o, func=AF.Relu), okey, rk)
                    sc.op("dve", lambda: V.tensor_tensor(out=qo[:, c, lo:lo + n], in0=rbuf[:, lo:lo + n],
                                                         in1=rbuf[:, lo:lo + n], op=ALU.mult),
                          rk, cells("qo", [c], tl_))
                proj_fm(slot, skey, 16, 2, hT, hkeys_all, hsegs, epi_u)
            for m_ in range(4):
                def epi_d(ti, rows, o, okeys, m_=m_):
                    xk_ = [("xres", ti, 2 * m_), ("xres", ti, 2 * m_ + 1)]
                    sc.op("dve", lambda: V.tensor_tensor(out=xres[0:rows, ti, m_ * 512:(m_ + 1) * 512],
                                                         in0=xres[0:rows, ti, m_ * 512:(m_ + 1) * 512], in1=o,
                                                         op=ALU.add),
                          okeys + xk_, xk_)
                proj_tm512(I.w_down, r * 2048, m_ * 512, qo, lambda ti: cells("qo", range(16), [ti]), xt_list, epi_d)

        sc.dma("sp", lambda: SP.dma_start(out=bc[:, :], in_=I.fin_g.partition_broadcast(128)), "bc0", writes=["bc"])
        def fin_tile(ti, tcol, rows):
            xk = [("xres", ti, n) for n in range(8)]
            i = xsb_ctr[0] % 2
            xsb_ctr[0] += 1
            xb = xsb[i]
            st, sk = stat()
            sc.op("dve", lambda: V.memset(st[:, 0:4], 0.0), writes=[sk])
            sc.op("act", lambda: S.activation(out=xb[0:rows, :], in_=xres[0:rows, ti, :], func=AF.Square,
                                              accum_out=st[0:rows, 0:1]), xk, [sk, ("xsb", i)])
            sc.op("act", lambda: S.activation(out=st[0:rows, 1:2], in_=st[0:rows, 0:1], func=AF.Sqrt,
                                              scale=1.0 / D, bias=epsb[0:rows, 0:1]), ["epsb"], [sk], sreads=[sk])
            sc.op("dve", lambda: V.reciprocal(out=st[0:rows, 2:3], in_=st[0:rows, 1:2]), [sk], [sk])
            sc.op("dve", lambda: V.scalar_tensor_tensor(out=xres[0:rows, ti, :], in0=xres[0:rows, ti, :],
                                                        scalar=st[0:rows, 2:3], in1=bc[0:rows, :],
                                                        op0=ALU.mult, op1=ALU.mult), xk + ["bc"], xk, sreads=[sk])
            dst = I.yp[p * 512 + ti * 128:p * 512 + ti * 128 + 128, :] if ti < 4 else I.ys[p * 64:(p + 1) * 64, :]
            sc.dma("sp", (lambda dst=dst, ti=ti, rows=rows: SP.dma_start(out=dst, in_=xres[0:rows, ti, :])),
                   "xres%d" % ti, reads=xk)
        for (ti, tcol, rows) in xt_list:
            fin_tile(ti, tcol, rows)

    with es:
        setup()
        if kstop > 0:
            mem_kv()
        for p in range(2):
            if kstop >= 2 + 10 * p:
                run_pass(p)
        if kstop < 99:
            dbg = nc.dram_tensor("dbg", [128, 2048], F32, kind="ExternalOutput").ap()
            dbt = bc
            sc.barrier()
            import os as _os
            sc.op("dve", lambda: V.memset(dbt[:, 0:2048], 0.0), [], ["dbt"])
        if kstop < 99 and not _os.environ.get("KNODBG"):
            sc.op("dve", lambda: V.tensor_copy(out=dbt[:, 0:64], in_=stats[:, 0:64]), [], ["dbt"])
            sc.op("dve", lambda: V.tensor_copy(out=dbt[:, 64:192], in_=hT[:, 0, 0:128]), [], ["dbt"])
            if kstop >= 3:
                sc.op("dve", lambda: V.tensor_copy(out=dbt[:, 1024:1264].rearrange("p (b r) -> p b r", b=8),
                                                   in_=cS[:, 3, :, 0:30]), [], ["dbt"])
                sc.op("dve", lambda: V.tensor_copy(out=dbt[:, 1264:1504].rearrange("p (b r) -> p b r", b=8),
                                                   in_=cS[:, 7, :, 0:30]), [], ["dbt"])
            sc.op("dve", lambda: V.tensor_copy(out=dbt[:, 192:320], in_=wring[0][:, 0, 0:128]), [], ["dbt"])
            sc.op("dve", lambda: V.tensor_copy(out=dbt[:, 320:448], in_=KT[:, 0, 0:128]), [], ["dbt"])
            sc.op("dve", lambda: V.tensor_copy(out=dbt[:, 448:576], in_=xres_flat[:, 4096:4096 + 128]), [], ["dbt"])
            sc.op("dve", lambda: V.tensor_copy(out=dbt[:, 576:696], in_=pp[:, 0:120]), [], ["dbt"])
            sc.op("dve", lambda: V.tensor_copy(out=dbt[:, 704:832], in_=xsb[1][:, 0:128]), [], ["dbt"])
            sc.op("dve", lambda: V.tensor_copy(out=dbt[:, 192:320], in_=ident_bf[:, 0:128]), [], ["dbt"])
            sc.op("dve", lambda: V.tensor_copy(out=dbt[:, 320:448], in_=TBw[:, 0:128]), [], ["dbt"])
            sc.op("dve", lambda: V.tensor_copy(out=dbt[:, 832:864], in_=cw[:, 1, 0:32]), [], ["dbt"])
            sc.op("dve", lambda: V.tensor_copy(out=dbt[0:64, 864:928], in_=WsTs[:, 1, 0:64]), [], ["dbt"])
            sc.op("dve", lambda: V.tensor_copy(out=dbt[:, 928:1024], in_=ident_f[:, 0:96]), [], ["dbt"])
        if kstop < 99:
            sc.barrier()
            sc.dma("sp", lambda: SP.dma_start(out=dbg[:, :], in_=dbt[:, 0:2048]), "dbg")
        block = es.enter_context(nc.Block())
        sc.emit(block)
    nc._io_names = (list(I._t), )
    return nc, sc


_CACHE = {}


def kernel(x_prompt, x_sample, mem_prompt, cache_mem_k, cache_mem_v, state_conv,
           attn_norm_g, w_in, b_gate, sgu_ln_g, sgu_ln_b, sgu_w, sgu_b, w_a_out,
           conv_w, conv_b, conv_ln_g, conv_ln_b, w_b_out, mem_norm_g, w_mem_kv,
           w_c_out, w_o, mlp_norm_g, w_up, w_down, final_norm_g):
    f = lambda a: np.ascontiguousarray(np.asarray(a, dtype=np.float32))
    if "nc" not in _CACHE:
        _CACHE["nc"] = build_program()[0]
    nc = _CACHE["nc"]
    x_prompt = f(x_prompt); x_sample = f(x_sample); mem_prompt = f(mem_prompt)
    ckf = f(cache_mem_k)[0].reshape(128, 256, 1024)
    cvf = f(cache_mem_v)[0].reshape(128, 256, 1024)
    scf = f(state_conv)[0]
    shared = {
        "attn_g": f(attn_norm_g)[0].reshape(16, 128), "mlp_g": f(mlp_norm_g)[0].reshape(16, 128),
        "mem_g": f(mem_norm_g)[0].reshape(16, 128), "w_in": f(w_in)[0], "b_gate": f(b_gate)[0].reshape(48, 128),
        "sgu_ln_g": f(sgu_ln_g)[0], "sgu_ln_b": f(sgu_ln_b)[0], "sgu_w": f(sgu_w)[0], "sgu_b": f(sgu_b)[0],
        "w_a": f(w_a_out)[0], "conv_w": f(conv_w)[0], "conv_b": f(conv_b)[0].reshape(8, 128),
        "conv_ln_g": f(conv_ln_g)[0].reshape(8, 128), "conv_ln_b": f(conv_ln_b)[0].reshape(8, 128),
        "w_b": f(w_b_out)[0], "w_kv": f(w_mem_kv)[0], "w_c": f(w_c_out)[0], "w_o": f(w_o)[0],
        "w_up": f(w_up)[0], "w_down": f(w_down)[0], "fin_g": f(final_norm_g),
    }
    in_maps = []
    for c in range(8):
        b, half = c // 2, c % 2
        m = dict(shared)
        m["xp"] = np.ascontiguousarray(x_prompt[b, half * 1024:(half + 1) * 1024])
        m["xh"] = (np.ascontiguousarray(x_prompt[b, 1024 - H:1024]) if half == 1
                   else np.zeros((H, D), np.float32))
        m["xs"] = np.ascontiguousarray(x_sample[c * 16:(c + 1) * 16].reshape(128, D))
        m["mem"] = np.ascontiguousarray(mem_prompt[b])
        m["ck"] = np.ascontiguousarray(ckf[c * 16:(c + 1) * 16])
        m["cv"] = np.ascontiguousarray(cvf[c * 16:(c + 1) * 16])
        m["scs"] = np.ascontiguousarray(scf[c * 16:(c + 1) * 16])
        in_maps.append(m)
    res = run_bass_kernel_spmd(nc, in_maps, core_ids=list(range(8)))
    R = res.results
    y_prompt = np.zeros((4, 2048, D), np.float32)
    y_sample = np.zeros((128, 8, D), np.float32)
    nk = np.zeros((1, 4, 256, 4, 256), np.float32)
    nv = np.zeros((1, 4, 256, 4, 256), np.float32)
    ncp = np.zeros((1, 4, 30, 1024), np.float32)
    ncs = np.zeros((1, 128, 30, 1024), np.float32)
    ncv = np.zeros((1, 128, 8, 1024), np.float32)
    for c in range(8):
        b, half = c // 2, c % 2
        y_prompt[b, half * 1024:(half + 1) * 1024] = R[c]["yp"]
        y_sample[c * 16:(c + 1) * 16] = R[c]["ys"].reshape(16, 8, D)
        if half == 0:
            nk[0, b] = R[c]["mk"].reshape(256, 4, 256)
            nv[0, b] = R[c]["mv"].reshape(256, 4, 256)
        else:
            ncp[0, b] = R[c]["csp"]
        ncs[0, c * 16:(c + 1) * 16] = R[c]["css"]
        ncv[0, c * 16:(c + 1) * 16] = R[c]["cvs"].reshape(16, 8, 1024)
    return (y_prompt, y_sample, nk, nv, ncp, ncs, ncv)
```

```python
import numpy as np
from contextlib import ExitStack
import concourse.bass as bass
import concourse.mybir as mybir
from concourse.bass_utils import run_bass_kernel_spmd

F32 = mybir.dt.float32
BF16 = mybir.dt.bfloat16
AF = mybir.ActivationFunctionType
ALU = mybir.AluOpType
AX = mybir.AxisListType

D = 2048
NIN = 11264
TP = 512
TS = 64
T = TP + TS
H = 32
EPS = 1e-6
NSLOT = 3
SAME_SYNC = False


class Sched:
    def __init__(self, nc, es):
        self.nc = nc
        self.engs = {"pe": nc.tensor, "act": nc.scalar, "dve": nc.vector,
                     "pool": nc.gpsimd, "sp": nc.sync}
        self.names = list(self.engs)
        self.sem = {e: es.enter_context(nc.semaphore("s_" + e)) for e in self.names}
        self.cnt = {e: 0 for e in self.names}
        self.q = {e: [] for e in self.names}
        self.known = {e: {} for e in self.names}
        self.snap = {e: [None] for e in self.names}
        self.dsem = {}
        self.dsnap = {}
        self.last_w = {}
        self.readers = {}
        self.pending = {e: [] for e in self.names}
        self.es = es
        self.nwaits = 0

    def _learn(self, e, ev):
        kn = self.known[e]
        k = (ev[0], ev[1])
        if kn.get(k, 0) < ev[2]:
            kn[k] = ev[2]
        other = self.snap[ev[1]][ev[2]] if ev[0] == "e" else self.dsnap[(ev[1], ev[2])]
        for kk, v in other.items():
            if kn.get(kk, 0) < v:
                kn[kk] = v

    def _deps(self, e, reads, writes, is_dma, use_pending=True, sreads=()):
        deps = {}
        sdeps = {}
        for k in sreads:
            w = self.last_w.get(k)
            if w and w[0] == "e" and w[1] == e:
                sdeps[(w[0], w[1])] = max(sdeps.get((w[0], w[1]), 0), w[2])

        def add(ev):
            k = (ev[0], ev[1])
            if deps.get(k, 0) < ev[2]:
                deps[k] = ev[2]
        if use_pending:
            for ev in self.pending[e]:
                add(ev)
            self.pending[e] = []
        for k in reads:
            w = self.last_w.get(k)
            if w:
                add(w)
        for k in writes:
            w = self.last_w.get(k)
            if w:
                add(w)
            for rk, rv in self.readers.get(k, {}).items():
                add((rk[0], rk[1], rv))
        waits = []
        for k_, v_ in sdeps.items():
            if self.known[e].get(k_, 0) < v_:
                waits.append((k_[0], k_[1], v_))
        for (ty, nm), v in deps.items():
            if ty == "e" and nm == e and not is_dma and (e == "pe" or not SAME_SYNC):
                continue
            if self.known[e].get((ty, nm), 0) >= v:
                continue
            waits.append((ty, nm, v))
        for ev in waits:
            self._learn(e, ev)
        self.nwaits += len(waits)
        return waits

    def _record(self, ev, reads, writes):
        for k in writes:
            self.last_w[k] = ev
            self.readers[k] = {}
        for k in reads:
            r = self.readers.setdefault(k, {})
            kk = (ev[0], ev[1])
            if r.get(kk, 0) < ev[2]:
                r[kk] = ev[2]

    def op(self, e, fn, reads=(), writes=(), sreads=()):
        reads = list(reads) + list(sreads)
        bankr = [k for k in reads if isinstance(k, tuple) and k[0] == "bank"]
        if bankr:
            reads = [k for k in reads if not (isinstance(k, tuple) and k[0] == "bank")]
            writes = list(writes) + bankr
        waits = self._deps(e, reads, writes, False, sreads=sreads)
        self.cnt[e] += 1
        n = self.cnt[e]
        s = dict(self.known[e])
        s[("e", e)] = max(s.get(("e", e), 0), n - 1)
        self.snap[e].append(s)
        self._record(("e", e, n), reads, writes)
        self.q[e].append((waits, fn, None))

    def dma(self, qn, fn, semname, reads=(), writes=(), use_pending=True):
        waits = self._deps(qn, reads, writes, True, use_pending)
        if semname not in self.dsem:
            self.dsem[semname] = [self.es.enter_context(self.nc.semaphore("d_" + semname)), 0]
        h = self.dsem[semname]
        h[1] += 16
        ev = ("d", semname, h[1])
        self.dsnap[(semname, h[1])] = dict(self.known[qn])
        self._record(ev, reads, writes)
        self.q[qn].append((waits, fn, h[0]))

    def barrier(self):
        evs = [("e", f, self.cnt[f]) for f in ("pe", "act", "dve", "pool") if self.cnt[f] > 0]
        evs += [("d", nm, h[1]) for nm, h in self.dsem.items() if nm not in ("w0a", "w1a", "w2a", "w0b", "w1b", "w2b")]
        for e in self.names:
            self.pending[e] = list(evs)

    def emit(self, block):
        def run(e):
            eng = self.engs[e]
            for waits, fn, dsem in self.q[e]:
                for (ty, nm, v) in waits:
                    eng.wait_ge(self.sem[nm] if ty == "e" else self.dsem[nm][0], v)
                ins = fn()
                if dsem is None:
                    ins.then_inc(self.sem[e], 1)
                else:
                    ins.then_inc(dsem, 16)
            if e == "sp":
                for f in ("pe", "act", "dve", "pool"):
                    if self.cnt[f]:
                        eng.wait_ge(self.sem[f], self.cnt[f])
                for nm, h in self.dsem.items():
                    eng.wait_ge(h[0], h[1])

        @block.tensor
        def _(x):
            run("pe")

        @block.scalar
        def _(x):
            run("act")

        @block.vector
        def _(x):
            run("dve")

        @block.gpsimd
        def _(x):
            run("pool")

        @block.sync
        def _(x):
            run("sp")


def build_program(kstop=99, io_names=None):
    nc = bass.Bass("TRN2", target_bir_lowering=False)
    es = ExitStack()
    es.enter_context(nc.allow_non_contiguous_dma(reason="small strided parameter loads"))

    IN_SHAPES = {
        "xp": [1024, D], "xh": [H, D], "xs": [128, D], "mem": [256, D],
        "ck": [16, 256, 1024], "cv": [16, 256, 1024], "scs": [16, 30, 1024],
        "attn_g": [16, 128], "mlp_g": [16, 128], "mem_g": [16, 128], "w_in": [D, NIN], "b_gate": [48, 128],
        "sgu_ln_g": [1024], "sgu_ln_b": [1024], "sgu_w": [8, 128, 128], "sgu_b": [8, 128],
        "w_a": [1024, D], "conv_w": [31, 1024], "conv_b": [8, 128], "conv_ln_g": [8, 128], "conv_ln_b": [8, 128],
        "w_b": [1024, D], "w_kv": [D, D], "w_c": [1024, D], "w_o": [D, D], "w_up": [D, 4 * D],
        "w_down": [4 * D, D], "fin_g": [D],
    }
    OUT_SHAPES = {"yp": [1024, D], "ys": [128, D], "mk": [256, 1024], "mv": [256, 1024],
                  "csp": [30, 1024], "css": [16, 30, 1024], "cvs": [128, 1024]}

    class _IO:
        def __init__(self):
            self._t = {}

        def __getattr__(self, name):
            t = self.__dict__["_t"]
            if name not in t:
                if name in IN_SHAPES:
                    t[name] = nc.dram_tensor(name, IN_SHAPES[name], F32, kind="ExternalInput").ap()
                elif name in OUT_SHAPES:
                    t[name] = nc.dram_tensor(name, OUT_SHAPES[name], F32, kind="ExternalOutput").ap()
                else:
                    raise AttributeError(name)
            return t[name]
    I = _IO()
    if io_names is None and kstop >= 99:
        io_names = list(IN_SHAPES) + list(OUT_SHAPES)
    for n_ in list(IN_SHAPES) + list(OUT_SHAPES):
        if io_names is not None and n_ in io_names:
            getattr(I, n_)

    def sb(name, shape, dt=F32):
        return es.enter_context(nc.sbuf_tensor(name, shape, dt))

    def ps(name, shape, dt=F32):
        return es.enter_context(nc.psum_tensor(name, shape, dt))

    hT = sb("hT", [128, 16, H + T], BF16)
    xres = sb("xres", [128, 5, D])
    wring = [sb("wr%d" % i, [128, 16, 256], BF16) for i in range(NSLOT)]
    qo = sb("qo", [128, 16, T], BF16)
    uT = sb("uT", [128, 8, T], BF16)
    cT = sb("cT", [128, 8, H + TP])
    cS = sb("cS", [128, 8, 8, 38])
    cbT = sb("cbT", [128, 8, T], BF16)
    mT = sb("mT", [128, 16, T], BF16)
    KT = sb("KT", [128, 8, 256], BF16)
    Vp = sb("Vp", [128, 2, 1024], BF16)
    bc = sb("bc", [128, D])
    xsb = [sb("xsb%d" % i, [128, D], BF16) for i in range(2)]
    ident_bf = sb("ident_bf", [128, 128], BF16)
    ident_f = sb("ident_f", [128, 128])
    trimask = sb("trimask", [128, 128])
    pp = sb("pp", [128, 128])
    cw = sb("cw", [128, 8, 32])
    WsT = sb("WsT", [128, 8, 128], BF16)
    WsTs = sb("WsTs", [64, 8, 64], BF16)
    biasbc = sb("biasbc", [128, 8, 128])
    maskq = sb("maskq", [128, 8, 64], BF16)
    epsb = sb("epsb", [128, 1])
    stats = sb("stats", [128, 1024])
    rlb = [sb("rl%d" % i, [128, T]) for i in range(2)]

    W0_ = ps("W0", [128, 1024]); W1_ = ps("W1", [128, 1024])
    TBw = ps("TBw", [128, 2048], BF16)
    W3_ = ps("W3", [128, 1024])
    W = [W0_, W1_, None, W3_]

    sc = Sched(nc, es)
    V, S, G, PE, SP = nc.vector, nc.scalar, nc.gpsimd, nc.tensor, nc.sync

    def bk(i):
        return [("bank", i)]

    def bank(i):
        assert i in (0, 1, 2, 3, 6, 7)
        return W[i // 2][:, (i % 2) * 512:(i % 2) * 512 + 512]

    TB = TBw[:, 0:1024]
    W3 = W[3]
    mT_f = mT[:, :, :].rearrange("p c t -> p (c t)").bitcast(F32).rearrange("p (c t) -> p c t", c=8)

    def cells(name, chunks, tiles):
        return [(name, c, t) for c in chunks for t in tiles]

    ALLT = range(5)
    stat_ctr = [0]

    def stat(n=16):
        i = stat_ctr[0] % 32
        stat_ctr[0] += 1
        return stats[:, i * 32:i * 32 + 32], ("stat", i)

    scr_off = [0]
    scr_gen = [0]
    xres_flat = xres[:, :, :].rearrange("p a b -> p (a b)")

    def scr_reset(barrier=True, start=0):
        if barrier:
            sc.barrier()
        scr_off[0] = start
        scr_gen[0] += 1

    def scratch(nelem, dt=F32):
        n32 = nelem if dt == F32 else (nelem + 1) // 2
        n32 = (n32 + 7) // 8 * 8
        o = scr_off[0]
        assert o + n32 <= 5 * D, "scratch overflow"
        scr_off[0] += n32
        ap = xres_flat[:, o:o + n32]
        if dt != F32:
            ap = ap.bitcast(dt)[:, 0:nelem]
        else:
            ap = ap[:, 0:nelem]
        return ap, ("scr", scr_gen[0], o)

    def mm_group(mms, reads, writes):
        def fn():
            ins = None
            for (o, l, r, st, sp_) in mms:
                ins = PE.matmul(o, lhsT=l, rhs=r, start=st, stop=sp_)
            return ins
        sc.op("pe", fn, reads, writes)

    def tr_group(trs, reads, writes):
        def fn():
            ins = None
            for (o, i_, idn) in trs:
                ins = PE.transpose(out=o, in_=i_, identity=idn)
            return ins
        sc.op("pe", fn, reads, writes)

    wcnt = [0]

    def wblock(wdram, r0, kcn, c0, ncol=256):
        s = wcnt[0] % NSLOT
        wcnt[0] += 1
        slot = wring[s]
        if ncol != 256:
            assert kcn * ncol <= 16 * 256
            slot = slot[:, :, :].rearrange("p a b -> p (a b)")[:, 0:kcn * ncol].rearrange("p (k n) -> p k n", k=kcn)
        src = wdram[r0:r0 + 128 * kcn, c0:c0 + ncol].rearrange("(k p) n -> p k n", p=128)
        hk = max(1, kcn // 2)
        keys = []
        for hi, (k0, k1) in enumerate([(0, hk), (hk, kcn)]):
            if k1 <= k0:
                continue
            key = ("wslot", s, hi)
            keys.append(key)
            sc.dma("pool", (lambda k0=k0, k1=k1: G.dma_start(out=slot[:, k0:k1, 0:ncol], in_=src[:, k0:k1, :])),
                   "w%d%s" % (s, "ab"[hi]), writes=[key], use_pending=False)
        return slot, keys

    acc_ctr = [0]
    ss_ctr = [0]

    def proj_fm(slot, skey, kcn, nchunks, rhs_t, rhs_keys, segs, epilogue):
        for cl in range(nchunks):
            outs = []
            for grp in ("p", "small"):
                mms = []
                wrs = []
                for (sn, c0, n) in segs:
                    if (sn == "p") != (grp == "p"):
                        continue
                    if sn == "p":
                        b = acc_ctr[0] % 3
                        acc_ctr[0] += 1
                        o = bank(b)[:, 0:n]
                        wr = bk(b)
                    else:
                        off = 0 if sn == "s" else 64
                        o = bank(3)[:, off:off + n]
                        wr = bk(3)
                    outs.append((sn, o, wr))
                    wrs += wr
                    for k in range(kcn):
                        mms.append((o, slot[:, k, cl * 128:(cl + 1) * 128], rhs_t[:, k, c0:c0 + n],
                                    k == 0, k == kcn - 1))
                if mms:
                    mm_group(mms, list(skey) + list(rhs_keys), list(dict.fromkeys(wrs)))
            for (sn, o, wr) in outs:
                epilogue(cl, sn, o, wr)

    tm_ctr = [0]

    def proj_tm(slot, skey, kcn, lhs_t, lhs_keys_fn, col0, tiles, epilogue, ncol=256):
        par = tm_ctr[0] % 2
        tm_ctr[0] += 1
        for (ti, tc0, rows) in tiles:
            bi = ti if ti < 4 else 6
            o = bank(bi)[0:rows, 0:ncol]
            wr = bk(bi)
            mms = [(o, lhs_t[:, k, tc0:tc0 + rows], slot[:, k, 0:ncol], k == 0, k == kcn - 1)
                   for k in range(kcn)]
            mm_group(mms, list(skey) + lhs_keys_fn(ti), wr)
            epilogue(ti, rows, o, wr)

    def proj_tm512(wdram, r0, c0, lhs_t, lhs_keys_fn, tiles, epilogue):
        blocks = [wblock(wdram, r0 + 1024 * hb, 8, c0, ncol=512) for hb in range(2)]
        outs = {}
        for hb, (slot, skey) in enumerate(blocks):
            for (ti, tc0, rows) in tiles:
                bi = ti if ti < 4 else 6
                o = bank(bi)[0:rows, 0:512]
                outs[ti] = (o, bk(bi), rows)
                mms = [(o, lhs_t[:, 8 * hb + k, tc0:tc0 + rows], slot[:, k, 0:512], hb == 0 and k == 0,
                        hb == 1 and k == 7) for k in range(8)]
                mm_group(mms, list(skey) + lhs_keys_fn(ti), bk(bi))
        for (ti, tc0, rows) in tiles:
            o, wr, rows = outs[ti]
            epilogue(ti, rows, o, wr)

    def wide_keys(i):
        return bk(2 * i) + bk(2 * i + 1)

    def setup():
        sc.op("pool", lambda: G.memset(ident_bf[:], 1.0), writes=["ident_bf"], sreads=["ident_bf"])
        sc.op("pool", lambda: G.affine_select(out=ident_bf[:], in_=ident_bf[:], pattern=[[-1, 128]],
                                              compare_op=ALU.is_equal, fill=0.0, base=0, channel_multiplier=1),
              writes=["ident_bf"], sreads=["ident_bf"])
        sc.op("pool", lambda: G.memset(ident_f[:], 1.0), writes=["ident_f"], sreads=["ident_f"])
        sc.op("pool", lambda: G.affine_select(out=ident_f[:], in_=ident_f[:], pattern=[[-1, 128]],
                                              compare_op=ALU.is_equal, fill=0.0, base=0, channel_multiplier=1),
              writes=["ident_f"], sreads=["ident_f"])
        sc.op("pool", lambda: G.memset(trimask[:], 1.0), writes=["trimask"], sreads=["trimask"])
        sc.op("pool", lambda: G.affine_select(out=trimask[:], in_=trimask[:], pattern=[[1, 128]],
                                              compare_op=ALU.is_ge, fill=0.0, base=0, channel_multiplier=-1),
              writes=["trimask"], sreads=["trimask"])
        sc.op("pool", lambda: G.memset(epsb[:], EPS), writes=["epsb"])
        sc.op("pool", lambda: G.memset(stats[:], 0.0), writes=[("stat", i) for i in range(64)])
        sc.op("pool", lambda: G.memset(maskq[:], 0.0), writes=["maskq"], sreads=["maskq"])
        for b in range(8):
            sc.op("pool", (lambda b=b: G.memset(maskq[:, b, 8 * b:8 * b + 8], 1.0)), writes=["maskq"], sreads=["maskq"])

        prm_t, _ = scratch(128)
        rows = [(I.attn_g, 0, 16), (I.mlp_g, 16, 16), (I.mem_g, 32, 16), (I.b_gate, 48, 48),
                (I.conv_b, 96, 8), (I.conv_ln_g, 104, 8), (I.conv_ln_b, 112, 8)]
        for i, (src, r0, n) in enumerate(rows):
            sc.dma("sp", (lambda src=src, r0=r0, n=n: SP.dma_start(out=prm_t[r0:r0 + n, :], in_=src[:, :])),
                   "prm%d" % i, writes=["prm"])
        tr_group([(W3[:, 0:120], prm_t[0:120, :], ident_f[0:120, 0:120])], ["prm", "ident_f"], wide_keys(3))
        sc.op("dve", lambda: V.tensor_copy(out=pp[:, 0:120], in_=W3[:, 0:120]), wide_keys(3), ["pp"])
        cwst, _ = scratch(1024)
        sc.dma("sp", lambda: SP.dma_start(out=cwst[0:31, :], in_=I.conv_w[:, :]), "cwst", writes=["cwst"])
        tr_group([(W3[:, j * 32:j * 32 + 31], cwst[0:31, j * 128:(j + 1) * 128], ident_f[0:31, 0:31])
                  for j in range(8)], ["cwst", "ident_f"], wide_keys(3))
        sc.op("dve", lambda: V.tensor_copy(out=cw[:, :, 0:31],
                                           in_=W3[:, 0:256].rearrange("p (j t) -> p j t", j=8)[:, :, 0:31]),
              wide_keys(3), ["cw"])
        wnat, _ = scratch(1024)
        wn3 = wnat.rearrange("p (g s) -> p g s", g=8)
        sc.dma("sp", lambda: SP.dma_start(out=wn3, in_=I.sgu_w.rearrange("g t s -> t g s")), "wnat", writes=["wnat"])
        tr_group([(W3[:, g * 128:(g + 1) * 128], wn3[:, g, :], ident_f[:, :]) for g in range(8)],
                 ["wnat", "ident_f"], wide_keys(3))
        sc.op("dve", lambda: V.tensor_tensor(out=WsT[:, :, :], in0=W3[:, :].rearrange("p (g t) -> p g t", g=8),
                                             in1=trimask[:, :].unsqueeze(1).to_broadcast([128, 8, 128]), op=ALU.mult),
              wide_keys(3) + ["trimask"], ["WsT"])
        a32, _ = scratch(64)
        abf, _ = scratch(64, BF16)
        ebf, _ = scratch(64, BF16)
        mblk, _ = scratch(64)
        for g in range(8):
            sc.dma("sp", (lambda g=g: SP.dma_start(out=a32[0:8, g * 8:(g + 1) * 8],
                                                   in_=I.sgu_w[g, 0:8, 0:8].rearrange("t s -> s t"))),
                   "wssg%d" % g, writes=[("a32", g)])
        sc.op("dve", lambda: V.tensor_copy(out=abf[0:8, :], in_=a32[0:8, :]), [("a32", g) for g in range(8)], ["abf"])
        sc.op("pool", lambda: G.memset(ebf[0:8, :], 1.0), writes=["ebf"], sreads=["ebf"])
        sc.op("pool", lambda: G.affine_select(out=ebf[0:8, :].rearrange("p (b j) -> p b j", b=8),
                                              in_=ebf[0:8, :].rearrange("p (b j) -> p b j", b=8),
                                              pattern=[[0, 8], [1, 8]], compare_op=ALU.is_equal, fill=0.0,
                                              base=0, channel_multiplier=-1), writes=["ebf"], sreads=["ebf"])
        sc.op("pool", lambda: G.memset(mblk[0:64, :], 1.0), writes=["mblk"], sreads=["mblk"])
        sc.op("pool", lambda: G.affine_select(out=mblk[0:64, :].rearrange("p (b t) -> p b t", b=8),
                                              in_=mblk[0:64, :].rearrange("p (b t) -> p b t", b=8),
                                              pattern=[[8, 8], [1, 8]], compare_op=ALU.is_ge, fill=0.0,
                                              base=0, channel_multiplier=-1), writes=["mblk"], sreads=["mblk"])
        sc.op("pool", lambda: G.affine_select(out=mblk[0:64, :].rearrange("p (b t) -> p b t", b=8),
                                              in_=mblk[0:64, :].rearrange("p (b t) -> p b t", b=8),
                                              pattern=[[-8, 8], [0, 8]], compare_op=ALU.is_ge, fill=0.0,
                                              base=0, channel_multiplier=1), writes=["mblk"], sreads=["mblk"])
        mm_group([(W3[0:64, 0:64], ebf[0:8, 0:64], abf[0:8, 0:64], True, True)], ["ebf", "abf"], wide_keys(3))
        sc.op("dve", lambda: V.tensor_tensor(
            out=WsTs[:, :, :].rearrange("p g (b t) -> p g b t", b=8),
            in0=W3[0:64, 0:64].rearrange("p (g t) -> p g t", g=8).unsqueeze(2).to_broadcast([64, 8, 8, 8]),
            in1=mblk[0:64, :].rearrange("p (b t) -> p b t", b=8).unsqueeze(1).to_broadcast([64, 8, 8, 8]),
            op=ALU.mult), wide_keys(3) + ["mblk"], ["WsTs"])
        sc.dma("sp", lambda: SP.dma_start(out=biasbc[:, :, :], in_=I.sgu_b.partition_broadcast(128)), "biasbc",
               writes=["biasbc"])

    xsb_ctr = [0]

    def norm_T(src, src_keys, rows, gcol, dst_fn, dst_keys):
        i = xsb_ctr[0] % 2
        xsb_ctr[0] += 1
        xb = xsb[i]
        xk = ("xsb", i)
        st, sk = stat()
        sc.op("dve", lambda: V.memset(st[:, 0:4], 0.0), writes=[sk])
        sc.op("act", lambda: S.activation(out=xb[0:rows, :], in_=src, func=AF.Square,
                                          accum_out=st[0:rows, 0:1]), list(src_keys), [sk, xk])
        sc.op("act", lambda: S.activation(out=st[0:rows, 1:2], in_=st[0:rows, 0:1], func=AF.Sqrt,
                                          scale=1.0 / D, bias=epsb[0:rows, 0:1]), ["epsb"], [sk], sreads=[sk])
        sc.op("dve", lambda: V.reciprocal(out=st[0:rows, 2:3], in_=st[0:rows, 1:2]), [sk], [sk])
        sc.op("dve", lambda: V.tensor_scalar_mul(out=xb[0:rows, :], in0=src, scalar1=st[0:rows, 2:3]),
              list(src_keys), [xk], sreads=[sk])
        tr_group([(TBw[:, k * 128:k * 128 + rows], xb[0:rows, k * 128:(k + 1) * 128], ident_bf[0:rows, 0:rows])
                  for k in range(16)], [xk, "ident_bf"], bk(4) + bk(5))
        src_ps = TBw[:, :].rearrange("p (k t) -> p k t", k=16)
        dst = dst_fn()
        dk = list(dst_keys)
        for k in range(16):
            if k < 8:
                sc.op("act", (lambda k=k: S.activation(out=dst[:, k, :], in_=src_ps[:, k, 0:rows], func=AF.Copy,
                                                       scale=pp[:, gcol + k:gcol + k + 1])),
                      bk(4) + ["pp"], dk)
            else:
                sc.op("dve", (lambda k=k: V.tensor_scalar_mul(out=dst[:, k, :], in0=src_ps[:, k, 0:rows],
                                                              scalar1=pp[:, gcol + k:gcol + k + 1])),
                      bk(5) + ["pp"], dk)

    def ln_stats(src0, src1, rows, src_keys):
        st, sk = stat()

        sk2 = (sk, "b")
        sc.op("dve", lambda: V.bn_stats(out=st[0:rows, 0:6], in_=src0), list(src_keys), [sk])
        sc.op("dve", lambda: V.bn_stats(out=st[0:rows, 16:22], in_=src1), list(src_keys), [sk2])
        sc.op("dve", lambda: V.bn_aggr(out=st[0:rows, 6:8], in_=st[0:rows, 0:6]), [], [sk], sreads=[sk])
        sc.op("dve", lambda: V.bn_aggr(out=st[0:rows, 8:10], in_=st[0:rows, 16:22]), [], [sk2], sreads=[sk2])
        sc.op("dve", lambda: V.tensor_tensor(out=st[0:rows, 10:12], in0=st[0:rows, 6:8], in1=st[0:rows, 8:10],
                                             op=ALU.add), [], [sk], sreads=[sk, sk2])
        sc.op("dve", lambda: V.tensor_scalar_mul(out=st[0:rows, 12:14], in0=st[0:rows, 10:12], scalar1=0.5),
              [], [sk], sreads=[sk])
        sc.op("dve", lambda: V.tensor_tensor(out=st[0:rows, 10:11], in0=st[0:rows, 6:7], in1=st[0:rows, 8:9],
                                             op=ALU.subtract), [], [sk], sreads=[sk])
        sc.op("dve", lambda: V.tensor_tensor(out=st[0:rows, 11:12], in0=st[0:rows, 10:11], in1=st[0:rows, 10:11],
                                             op=ALU.mult), [], [sk], sreads=[sk])
        sc.op("dve", lambda: V.scalar_tensor_tensor(out=st[0:rows, 13:14], in0=st[0:rows, 11:12], scalar=0.25,
                                                    in1=st[0:rows, 13:14], op0=ALU.mult, op1=ALU.add),
              [], [sk], sreads=[sk])
        sc.op("act", lambda: S.activation(out=st[0:rows, 14:15], in_=st[0:rows, 13:14], func=AF.Sqrt,
                                          scale=1.0, bias=epsb[0:rows, 0:1]), [sk, "epsb"], [sk])
        sc.op("dve", lambda: V.reciprocal(out=st[0:rows, 15:16], in_=st[0:rows, 14:15]), [sk], [sk])
        return st, sk

    def mem_kv():
        scr_reset()
        for mt in range(2):
            xst, xk = scratch(D)
            sc.dma("sp", (lambda mt=mt, xst=xst: SP.dma_start(out=xst[:, :], in_=I.mem[mt * 128:(mt + 1) * 128, :])),
                   "xst%d" % mt, writes=[xk])
            norm_T(xst[:, :], [xk], 128, 32, (lambda mt=mt: hT[:, :, mt * 128:(mt + 1) * 128]),
                   cells("hT", range(16), [mt]))
        if kstop <= 0.3:
            return
        kvst = [scratch(512) for _ in range(2)]
        for jb in range(8):
            slot, skey = wblock(I.w_kv, 0, 16, jb * 256)
            if kstop <= 0.5:
                sc.op("dve", lambda: V.tensor_copy(out=KT[:, 0, 0:256], in_=slot[:, 0, 0:256]), list(skey), ["KT"])
                return
            if jb < 4:
                def epi(cl, sn, o, okey, jb=jb):
                    c = 2 * jb + cl
                    sc.op("act", lambda: S.activation(out=KT[:, c, :], in_=o, func=AF.Copy), okey, ["KT"])
                proj_fm(slot, skey, 16, 2, hT, cells("hT", range(16), [0, 1]), [("p", 0, 256)], epi)
            if kstop <= 0.6:
                return
            ks, kk = kvst[jb % 2]
            ks3 = ks.rearrange("p (m c) -> p m c", m=2)

            def epi_tm(ti, rows, o, okeys, jb=jb, ks3=ks3, kk=kk):
                if jb < 4:
                    sc.op("dve", lambda: V.tensor_copy(out=ks3[:, ti, :], in_=o), okeys, [(kk, ti)])
                else:
                    sc.op("dve", lambda: V.tensor_copy(out=ks3[:, ti, :], in_=o), okeys, [(kk, ti)])
                    sc.op("act", lambda: S.activation(out=Vp[:, ti, (jb - 4) * 256:(jb - 3) * 256], in_=o,
                                                      func=AF.Copy), okeys, ["Vp"])
            import os as _os
            kd = _os.environ.get("KDBG", "")
            if kd == "noepi":
                epi_tm = lambda ti, rows, o, okeys: None
            proj_tm(slot, skey, 16, hT, lambda ti: cells("hT", range(16), [ti]), 0,
                    [(0, 0, 128)] if kd == "t0" else [(0, 0, 128), (1, 128, 128)], epi_tm)
            if kstop <= 0.7:
                return
            dst = I.mk if jb < 4 else I.mv
            c0 = (jb % 4) * 256
            import os as _os
            kdbg = _os.environ.get("KDBG", "")
            for m_ in range(2):
                if kdbg == "actq":
                    sc.dma("act", (lambda dst=dst, c0=c0, ks3=ks3, m_=m_: S.dma_start(
                        out=dst[m_ * 128:(m_ + 1) * 128, c0:c0 + 256], in_=ks3[:, m_, :])),
                        "kvst%d_%d" % (jb % 2, m_), reads=[(kk, m_)])
                elif kdbg == "pp":
                    sc.dma("sp", (lambda dst=dst, c0=c0, m_=m_: SP.dma_start(
                        out=dst[m_ * 128:(m_ + 1) * 128, c0:c0 + 128], in_=pp[:, :])),
                        "kvst%d_%d" % (jb % 2, m_), reads=["pp"])
                elif kdbg == "nodep":
                    sc.dma("sp", (lambda dst=dst, c0=c0, ks3=ks3, m_=m_: SP.dma_start(
                        out=dst[m_ * 128:(m_ + 1) * 128, c0:c0 + 256], in_=ks3[:, m_, :])),
                        "kvst%d_%d" % (jb % 2, m_), reads=[])
                else:
                    sc.dma("sp", (lambda dst=dst, c0=c0, ks3=ks3, m_=m_: SP.dma_start(
                        out=dst[m_ * 128:(m_ + 1) * 128, c0:c0 + 256], in_=ks3[:, m_, :])),
                        "kvst%d_%d" % (jb % 2, m_), reads=[(kk, m_)])
            if kstop <= 0.8:
                return

    def run_pass(p):
        pc0 = HO = H
        sc0 = H + TP
        hsegs = [("p", HO, TP), ("s", sc0, TS)]
        hkeys_all = cells("hT", range(16), range(5))

        scr_reset()
        xstage = [scratch(D), scratch(D)]
        tl = [("h", I.xh[:, :], 32, 0, 5)] if p == 0 else []
        for i in range(4):
            r0 = p * 512 + i * 128
            tl.append(("p", I.xp[r0:r0 + 128, :], 128, HO + 128 * i, i))
        tl.append(("s", I.xs[p * 64:(p + 1) * 64, :], 64, sc0, 4))
        for n, (kind, src, rows, col, ti) in enumerate(tl):
            xst, xk = xstage[n % 2]
            sc.dma("sp", (lambda xst=xst, src=src, rows=rows: SP.dma_start(out=xst[0:rows, :], in_=src)),
                   "xst%d" % (n % 2), writes=[xk])
            norm_T(xst[0:rows, :], [xk], rows, 0, (lambda col=col, rows=rows: hT[:, :, col:col + rows]),
                   cells("hT", range(16), [ti]))

        if kstop <= 2 + 10 * p:
            return
        scr_reset(barrier=False, start=2 * D)
        sg, sgk = scratch(2 * (H + T))
        sg3 = sg.rearrange("p (c t) -> p c t", c=2)
        scst, scstk = scratch(1024)
        gsegs = ([("h", 0, H)] if p == 0 else []) + [("p", HO, TP), ("s", sc0, TS)]
        if p == 1:
            sc.op("dve", lambda: V.tensor_copy(out=cT[:, :, 0:H], in_=cT[:, :, TP:TP + H]),
                  cells("cT", range(8), [0]), cells("cT", range(8), [0]))
        import os as _os
        for g4 in range(2):
            if _os.environ.get("KDBG", "") in ("g0", "g0l") and g4 == 1:
                break
            s0 = p * 8 + g4 * 4
            sc.dma("sp", (lambda s0=s0: SP.dma_start(out=scst[0:120, :],
                                                     in_=I.scs[s0:s0 + 4].rearrange("b r f -> (b r) f"))),
                   "scst", writes=[scstk])
            if _os.environ.get("KDBG", "") == "g0l":
                break
            tr_group([(W3[:, j * 128:j * 128 + 120], scst[0:120, j * 128:(j + 1) * 128], ident_f[0:120, 0:120])
                      for j in range(8)], [scstk, "ident_f"], wide_keys(3))
            import os as _os
            for bb in range(4):
                if _os.environ.get("KDBG", "") == "nocsh":
                    break
                sc.dma("sp", (lambda s0=s0, bb=bb: SP.dma_start(out=I.css[s0 + bb, 0:22, :],
                                                               in_=scst[bb * 30 + 8:bb * 30 + 30, :])),
                       "csh%d" % bb, reads=[scstk])
            if _os.environ.get("KDBG", "") == "nocp":
                continue
            sc.op("dve", (lambda g4=g4: V.tensor_copy(
                out=cS[:, :, 4 * g4:4 * g4 + 4, 0:30],
                in_=W3[:, :].rearrange("p (j t) -> p j t", j=8)[:, :, 0:120].rearrange("p j (b r) -> p j b r", b=4))),
                wide_keys(3), cells("cS", range(8), [0]))
        for jb in range(4):
            slot, skey = wblock(I.w_in, 0, 16, 3072 + jb * 256)

            def epi_b(cl, sn, o, okey):
                lo = {"h": 0, "p": H, "s": H + TP}[sn]
                n = {"h": H, "p": TP, "s": TS}[sn]
                sc.op("act", lambda: S.activation(out=sg3[:, cl, lo:lo + n], in_=o, func=AF.Sigmoid),
                      okey, [(sgk, cl, sn)])
            proj_fm(slot, skey, 16, 2, hT, cells("hT", range(16), range(6)), gsegs, epi_b)
            slot, skey = wblock(I.w_in, 0, 16, 2048 + jb * 256)

            def epi_a(cl, sn, o, okey, jb=jb):
                j = 2 * jb + cl
                if sn == "h":
                    sc.op("dve", lambda: V.tensor_tensor(out=cT[:, j, 0:H], in0=o, in1=sg3[:, cl, 0:H], op=ALU.mult),
                          okey + [(sgk, cl, sn)], cells("cT", [j], [0]))
                elif sn == "p":
                    sc.op("dve", lambda: V.tensor_tensor(out=cT[:, j, H:H + TP], in0=o, in1=sg3[:, cl, H:H + TP],
                                                         op=ALU.mult),
                          okey + [(sgk, cl, sn)], cells("cT", [j], [0]))
                else:
                    sc.op("dve", lambda: V.tensor_tensor(
                        out=cS[:, j, :, 30:38], in0=o.rearrange("p (b t) -> p b t", b=8),
                        in1=sg3[:, cl, H + TP:H + T].rearrange("p (b t) -> p b t", b=8), op=ALU.mult),
                        okey + [(sgk, cl, sn)], cells("cS", [j], [0]))
            proj_fm(slot, skey, 16, 2, hT, cells("hT", range(16), range(6)), gsegs, epi_a)

        def conv_chunk(j):
            def conv_p():
                dc = mT_f[:, j, 0:TP]
                ins = V.tensor_scalar(out=dc, in0=cT[:, j, 2:2 + TP], scalar1=cw[:, j, 0:1],
                                      scalar2=pp[:, 96 + j:97 + j], op0=ALU.mult, op1=ALU.add)
                for t in range(1, 31):
                    ins = V.scalar_tensor_tensor(out=dc, in0=cT[:, j, 2 + t:2 + t + TP],
                                                 scalar=cw[:, j, t:t + 1], in1=dc, op0=ALU.mult, op1=ALU.add)
                return ins
            sc.op("dve", conv_p, cells("cT", [j], [0]) + ["cw", "pp"], cells("mT", [2 * j, 2 * j + 1], range(4)))

            dcs = mT_f[:, j, TP:T].rearrange("p (b t) -> p b t", b=8)
            ck = cells("mT", [2 * j, 2 * j + 1], [4])
            sc.op("dve", lambda: V.tensor_scalar(out=dcs, in0=cS[:, j, :, 0:8], scalar1=cw[:, j, 0:1],
                                                 scalar2=pp[:, 96 + j:97 + j], op0=ALU.mult, op1=ALU.add),
                  cells("cS", [j], [0]) + ["cw", "pp"], ck)
            for t in range(1, 31):
                sc.op("dve", (lambda t=t: V.scalar_tensor_tensor(out=dcs, in0=cS[:, j, :, t:t + 8],
                                                                 scalar=cw[:, j, t:t + 1], in1=dcs,
                                                                 op0=ALU.mult, op1=ALU.add)),
                      cells("cS", [j], [0]) + ["cw", "pp"], ck, sreads=ck)

        scr_reset()
        for jb in range(4):
            slot, skey = wblock(I.w_in, 0, 16, 4096 + jb * 256)

            def epi(cl, sn, o, okey, jb=jb):
                c = 2 * jb + cl
                if sn == "p":
                    sc.op("act", lambda: S.activation(out=qo[:, c, 0:TP], in_=o, func=AF.Copy), okey,
                          cells("qo", [c], range(4)))
                else:
                    sc.op("act", lambda: S.activation(out=qo[:, c, TP:T], in_=o, func=AF.Copy), okey,
                          cells("qo", [c], [4]))
            proj_fm(slot, skey, 16, 2, hT, hkeys_all, hsegs, epi)
            if jb in (1, 3):
                conv_chunk(jb // 2)

        p32, p32k = scratch(1024)
        pbf, pbfk = scratch(1024, BF16)
        pT, pTk = scratch(1024, BF16)

        def softmax(rows, heads, wk):
            st, sk = stat()
            for h in range(4):
                sc.op("dve", (lambda h=h: V.tensor_reduce(out=st[0:rows, h:h + 1], in_=heads[h],
                                                          axis=AX.X, op=ALU.max)), wk, [sk])
            sc.op("dve", lambda: V.tensor_scalar_mul(out=st[0:rows, 4:8], in0=st[0:rows, 0:4], scalar1=-1.0 / 16),
                  [], [sk], sreads=[sk])
            sc.op("dve", lambda: V.memset(st[0:rows, 8:12], 0.0), [], [sk], sreads=[sk])
            for h in range(4):
                sc.op("act", (lambda h=h: S.activation(out=p32[0:rows, h * 256:(h + 1) * 256],
                                                       in_=heads[h], func=AF.Exp,
                                                       bias=st[0:rows, 4 + h:5 + h], scale=1.0 / 16,
                                                       accum_out=st[0:rows, 8 + h:9 + h])),
                      wk + [sk], [sk, p32k])
            sc.op("dve", lambda: V.reciprocal(out=st[0:rows, 12:16], in_=st[0:rows, 8:12]), [sk], [sk])
            for h in range(4):
                sc.op("dve", (lambda h=h: V.tensor_scalar_mul(out=pbf[0:rows, h * 256:(h + 1) * 256],
                                                              in0=p32[0:rows, h * 256:(h + 1) * 256],
                                                              scalar1=st[0:rows, 12 + h:13 + h])),
                      [p32k], [pbfk], sreads=[sk])
            tr_group([(TB[:, j * 128:j * 128 + rows], pbf[0:rows, j * 128:(j + 1) * 128],
                       ident_bf[0:rows, 0:rows]) for j in range(8)], [pbfk, "ident_bf"], bk(4))
            sc.op("act", lambda: S.activation(
                out=pT.rearrange("p (j t) -> p j t", j=8)[:, :, 0:rows],
                in_=TB.rearrange("p (j t) -> p j t", j=8)[:, :, 0:rows], func=AF.Copy), bk(4), [pTk])

        for i in range(4):
            cols = slice(128 * i, 128 * i + 128)
            mms = []
            for h in range(4):
                for dc in range(2):
                    c = 2 * h + dc
                    mms.append((W3[:, h * 256:(h + 1) * 256], qo[:, c, cols], KT[:, c, :], dc == 0, dc == 1))
            mm_group(mms, cells("qo", range(8), [i]) + ["KT"], wide_keys(3))
            softmax(128, [W3[:, h * 256:(h + 1) * 256] for h in range(4)], wide_keys(3))
            mms = []
            for h in range(4):
                for dc in range(2):
                    c = 2 * h + dc
                    for mc in range(2):
                        mms.append((W[1][:, c * 128:(c + 1) * 128],
                                    Vp[:, mc, h * 256 + dc * 128:h * 256 + dc * 128 + 128],
                                    pT[:, (2 * h + mc) * 128:(2 * h + mc) * 128 + 128], mc == 0, mc == 1))
            mm_group(mms, [pTk, "Vp"], wide_keys(1))
            sc.op("dve", (lambda cols=cols: V.tensor_copy(out=qo[:, 8:16, cols],
                                                         in_=W[1][:, :].rearrange("p (c t) -> p c t", c=8))),
                  wide_keys(1), cells("qo", range(8, 16), [i]))

        shb = [2, 3, 6, 7]
        shk = bk(2) + bk(3) + bk(6) + bk(7)
        Ksb = [scratch(2048, BF16) for _ in range(2)]
        KsT = [scratch(2048, BF16) for _ in range(2)]
        Vs = [scratch(2048, BF16) for _ in range(2)]
        qmb = [scratch(512, BF16) for _ in range(2)]
        for b in range(8):
            seq = p * 8 + b
            kst, kstk = Ksb[b % 2]
            kst3 = kst.rearrange("p (m f) -> p m f", m=2)
            sc.dma("pool", (lambda seq=seq, kst3=kst3: G.dma_start(
                out=kst3, in_=I.ck[seq].rearrange("(m p) f -> p m f", p=128))), "ksb%d" % (b % 2), writes=[kstk])
            kt, ktk = KsT[b % 2]
            kt3 = kt.rearrange("p (c m) -> p c m", c=8)
            trs = []
            for c in range(8):
                for mc in range(2):
                    trs.append((TBw[:, c * 256 + mc * 128:c * 256 + mc * 128 + 128],
                                kst3[:, mc, c * 128:(c + 1) * 128], ident_bf[:, :]))
            tr_group(trs, [kstk, "ident_bf"], bk(4) + bk(5))
            sc.op("act", (lambda kt3=kt3: S.activation(
                out=kt3[:, 0:4, :], in_=TBw[:, 0:1024].rearrange("p (c m) -> p c m", c=4), func=AF.Copy)),
                bk(4), [(ktk, 0)])
            sc.op("dve", (lambda kt3=kt3: V.tensor_copy(
                out=kt3[:, 4:8, :], in_=TBw[:, 1024:2048].rearrange("p (c m) -> p c m", c=4))),
                bk(5), [(ktk, 1)])
            qm, qmk = qmb[b % 2]
            qm3 = qm.rearrange("p (c t) -> p c t", c=8)
            sc.op("dve", (lambda qm3=qm3, b=b: V.tensor_tensor(
                out=qm3, in0=qo[:, 0:8, TP:T],
                in1=maskq[:, b, :].unsqueeze(1).to_broadcast([128, 8, 64]), op=ALU.mult)),
                cells("qo", range(8), [4]) + ["maskq"], [qmk])
            mms = []
            for h in range(4):
                for dc in range(2):
                    c = 2 * h + dc
                    mms.append((bank(shb[h])[0:64, 0:256], qm3[:, c, :], kt3[:, c, :],
                                b == 0 and dc == 0, b == 7 and dc == 1))
            mm_group(mms, [qmk, (ktk, 0), (ktk, 1)], shk)
        softmax(64, [bank(shb[h])[0:64, 0:256] for h in range(4)], shk)
        pT3 = pT.rearrange("p (j t) -> p j t", j=8)
        for b in range(8):
            seq = p * 8 + b
            vs_, vsk = Vs[b % 2]
            vs3 = vs_.rearrange("p (m f) -> p m f", m=2)
            sc.dma("pool", (lambda seq=seq, vs3=vs3: G.dma_start(
                out=vs3, in_=I.cv[seq].rearrange("(m p) f -> p m f", p=128))), "vs%d" % (b % 2), writes=[vsk])
            mms = []
            for h in range(4):
                for dc in range(2):
                    c = 2 * h + dc
                    for mc in range(2):
                        mms.append((bank(0)[:, c * 64 + 8 * b:c * 64 + 8 * b + 8],
                                    vs3[:, mc, h * 256 + dc * 128:h * 256 + dc * 128 + 128],
                                    pT3[:, 2 * h + mc, 8 * b:8 * b + 8], mc == 0, mc == 1))
            mm_group(mms, [vsk, pTk], bk(0))
        sc.op("dve", lambda: V.tensor_copy(out=qo[:, 8:16, TP:T],
                                           in_=bank(0).rearrange("p (c t) -> p c t", c=8)),
              bk(0), cells("qo", range(8, 16), [4]))

        if kstop <= 3 + 10 * p:
            return
        scr_reset()
        for jb in range(4):
            slot, skey = wblock(I.w_in, 0, 16, jb * 256)

            def epi(cl, sn, o, okey, jb=jb):
                c = 2 * jb + cl
                if sn == "p":
                    sc.op("act", lambda: S.activation(out=uT[:, c, 0:TP], in_=o, func=AF.Gelu_apprx_tanh),
                          okey, cells("uT", [c], range(4)))
                else:
                    sc.op("act", lambda: S.activation(out=uT[:, c, TP:T], in_=o, func=AF.Gelu_apprx_tanh),
                          okey, cells("uT", [c], [4]))
            proj_fm(slot, skey, 16, 2, hT, hkeys_all, hsegs, epi)
            if jb in (1, 3):
                conv_chunk(2 + jb // 2)
        sc.dma("sp", lambda: SP.dma_start(out=bc[:, 0:1024], in_=I.sgu_ln_g.partition_broadcast(128)), "bc0",
               writes=["bc"])
        sc.dma("sp", lambda: SP.dma_start(out=bc[:, 1024:2048], in_=I.sgu_ln_b.partition_broadcast(128)), "bc1",
               writes=["bc"])
        gv = [scratch(1024) for _ in range(5)]
        vb = [scratch(1024, BF16) for _ in range(2)]
        tmpm, tmpk = scratch(1024)
        vtiles = [(i, HO + 128 * i, 128) for i in range(4)] + [(4, sc0, 64)]
        for jb in range(4):
            slot, skey = wblock(I.w_in, 0, 16, 1024 + jb * 256)

            def epi_tm(ti, rows, o, okeys, jb=jb):
                g_, gk = gv[ti]
                sc.op("act", lambda: S.activation(out=g_[0:rows, jb * 256:(jb + 1) * 256], in_=o,
                                                  func=AF.Gelu_apprx_tanh), okeys, [(gk, jb)])
            proj_tm(slot, skey, 16, hT, lambda ti: cells("hT", range(16), [ti]), 0, vtiles, epi_tm)
            conv_chunk(4 + jb)
        def sgu_tile(ti, tc0, rows):
            g_, gk = gv[ti]
            gkeys = [(gk, j) for j in range(4)]
            st, sk = ln_stats(g_[0:rows, 0:512], g_[0:rows, 512:1024], rows, gkeys)
            sc.op("dve", lambda: V.tensor_scalar(out=g_[0:rows, :], in0=g_[0:rows, :], scalar1=st[0:rows, 12:13],
                                                 scalar2=st[0:rows, 15:16], op0=ALU.subtract, op1=ALU.mult),
                  gkeys, gkeys, sreads=[sk])
            sc.op("dve", lambda: V.tensor_tensor(out=g_[0:rows, :], in0=g_[0:rows, :], in1=bc[0:rows, 0:1024],
                                                 op=ALU.mult), gkeys + ["bc"], gkeys)
            sc.op("dve", lambda: V.tensor_tensor(out=g_[0:rows, :], in0=g_[0:rows, :], in1=bc[0:rows, 1024:2048],
                                                 op=ALU.add), gkeys + ["bc"], gkeys)
            v_, vk = vb[ti % 2]
            sc.op("act", lambda: S.activation(out=v_[0:rows, :], in_=g_[0:rows, :], func=AF.Copy), gkeys, [vk])
            if ti == 4:
                sc.dma("sp", lambda: SP.dma_start(out=I.cvs[p * 64:(p + 1) * 64, :], in_=g_[0:64, :]), "cvs",
                       reads=gkeys)
                mms = [(W3[:, g * 64:(g + 1) * 64], v_[0:64, g * 128:(g + 1) * 128], WsTs[0:64, g, :], True, True)
                       for g in range(8)]
                mm_group(mms, [vk, "WsTs"], wide_keys(3))
                sc.op("dve", lambda: V.tensor_tensor(
                    out=tmpm[:, 0:512].rearrange("p (g b t) -> p g b t", g=8, b=8),
                    in0=W3[:, 0:512].rearrange("p (g b t) -> p g b t", g=8, b=8),
                    in1=biasbc[:, :, 0:8].unsqueeze(2).to_broadcast([128, 8, 8, 8]), op=ALU.add),
                    wide_keys(3) + ["biasbc"], [tmpk])
                sc.op("dve", lambda: V.tensor_tensor(out=uT[:, :, TP:T], in0=uT[:, :, TP:T],
                                                     in1=tmpm[:, 0:512].rearrange("p (g t) -> p g t", g=8),
                                                     op=ALU.mult),
                      [tmpk] + cells("uT", range(8), [4]), cells("uT", range(8), [4]))
            else:
                mms = [(W3[:, g * 128:(g + 1) * 128], v_[:, g * 128:(g + 1) * 128], WsT[:, g, :], True, True)
                       for g in range(8)]
                mm_group(mms, [vk, "WsT"], wide_keys(3))
                sc.op("dve", lambda: V.tensor_tensor(out=tmpm.rearrange("p (g t) -> p g t", g=8),
                                                     in0=W3[:, :].rearrange("p (g t) -> p g t", g=8),
                                                     in1=biasbc[:, :, :], op=ALU.add),
                      wide_keys(3) + ["biasbc"], [tmpk])
                cs_ = slice(128 * ti, 128 * ti + 128)
                sc.op("dve", (lambda cs_=cs_: V.tensor_tensor(out=uT[:, :, cs_], in0=uT[:, :, cs_],
                                                             in1=tmpm.rearrange("p (g t) -> p g t", g=8),
                                                             op=ALU.mult)),
                      [tmpk] + cells("uT", range(8), [ti]), cells("uT", range(8), [ti]))

        for (ti, tc0, rows) in vtiles:
            sgu_tile(ti, tc0, rows)
        if kstop <= 4 + 10 * p:
            return

        scr_reset()
        nb = [scratch(1024, BF16) for _ in range(2)]
        cso, csok = scratch(1024)
        cnew, cnk = scratch(512)
        sc.op("dve", lambda: V.tensor_copy(out=cnew.rearrange("p (j b t) -> p j b t", j=8, b=8),
                                           in_=cS[:, :, :, 30:38]), cells("cS", range(8), [0]), [cnk])
        tr_group([(W3[0:64, j * 128:(j + 1) * 128], cnew[:, j * 64:(j + 1) * 64], ident_f[:, :]) for j in range(8)],
                 [cnk, "ident_f"], wide_keys(3))
        sc.op("act", lambda: S.activation(out=cso[0:64, :], in_=W3[0:64, :], func=AF.Copy), wide_keys(3), [csok])
        for b in range(8):
            sc.dma("sp", (lambda b=b: SP.dma_start(out=I.css[p * 8 + b, 22:30, :], in_=cso[8 * b:8 * b + 8, :])),
                   "cso", reads=[csok])
        if p == 1:
            csp_s, cspk = scratch(1024)
            tr_group([(W3[0:32, j * 128:(j + 1) * 128], cT[:, j, TP:TP + H], ident_f[:, :]) for j in range(8)],
                     cells("cT", range(8), [0]) + ["ident_f"], wide_keys(3))
            sc.op("act", lambda: S.activation(out=csp_s[0:32, :], in_=W3[0:32, :], func=AF.Copy),
                  wide_keys(3), [cspk])
            sc.dma("sp", lambda: SP.dma_start(out=I.csp[:, :], in_=csp_s[2:32, :]), "csp", reads=[cspk])
        def ln_tile(ti, tcol, rows):
            tr_group([(W3[0:rows, j * 128:(j + 1) * 128], mT_f[:, j, tcol:tcol + rows], ident_f[:, :])
                      for j in range(8)], cells("mT", range(16), [ti]) + ["ident_f"], wide_keys(3))
            st, sk = ln_stats(W3[0:rows, 0:512], W3[0:rows, 512:1024], rows, wide_keys(3))
            n_, nk = nb[ti % 2]
            sc.op("dve", lambda: V.tensor_scalar(out=n_[0:rows, :], in0=W3[0:rows, :], scalar1=st[0:rows, 12:13],
                                                 scalar2=st[0:rows, 15:16], op0=ALU.subtract, op1=ALU.mult),
                  wide_keys(3), [nk], sreads=[sk])
            tr_group([(TB[:, j * 128:j * 128 + rows], n_[0:rows, j * 128:(j + 1) * 128], ident_bf[0:rows, 0:rows])
                      for j in range(8)], [nk, "ident_bf"], bk(4))
            for j in range(8):
                sc.op("act", (lambda j=j: S.activation(out=cbT[:, j, tcol:tcol + rows],
                                                       in_=TB[:, j * 128:j * 128 + rows], func=AF.Silu,
                                                       scale=pp[:, 104 + j:105 + j], bias=pp[:, 112 + j:113 + j])),
                      bk(4) + ["pp"], cells("cbT", [j], [ti]))

        for (ti, tcol, rows) in [(i, 128 * i, 128) for i in range(4)] + [(4, TP, 64)]:
            ln_tile(ti, tcol, rows)
        if kstop <= 5 + 10 * p:
            return

        scr_reset(barrier=False, start=3584)
        sgg = [scratch(2 * T) for _ in range(2)]
        macc, mak = scratch(2 * T)
        tmpg, tgk = scratch(2 * T)
        macc3 = macc.rearrange("p (c t) -> p c t", c=2)
        tmpg3 = tmpg.rearrange("p (c t) -> p c t", c=2)
        branches = [(I.w_a, uT, "uT", range(8)), (I.w_b, cbT, "cbT", range(8)), (I.w_c, qo, "qo", range(8, 16))]
        psegs = [("p", 0, TP), ("s", TP, TS)]
        for jg in range(8):
            for br in range(3):
                slot, skey = wblock(I.w_in, 0, 16, 5120 + br * 2048 + jg * 256)
                sg_, sgk_ = sgg[br % 2]
                sgx = sg_.rearrange("p (c t) -> p c t", c=2)

                def epi_g(cl, sn, o, okey, br=br, jg=jg, sgx=sgx, sgk_=sgk_):
                    lo, n = (0, TP) if sn == "p" else (TP, TS)
                    col = 48 + br * 16 + 2 * jg + cl
                    sc.op("act", lambda: S.activation(out=sgx[:, cl, lo:lo + n], in_=o, func=AF.Sigmoid,
                                                      bias=pp[:, col:col + 1], scale=1.0),
                          okey + ["pp"], [(sgk_, cl, sn)])
                proj_fm(slot, skey, 16, 2, hT, hkeys_all, hsegs, epi_g)
                wsrc, act_t, aname, achunks = branches[br]
                slot, skey = wblock(wsrc, 0, 8, jg * 256)
                rhs_t = act_t if br < 2 else qo[:, 8:16, :]

                def epi_y(cl, sn, o, okey, br=br, jg=jg, sgx=sgx, sgk_=sgk_):
                    lo, n = (0, TP) if sn == "p" else (TP, TS)
                    c = 2 * jg + cl
                    tl_ = range(4) if sn == "p" else [4]
                    if br == 0:
                        sc.op("dve", lambda: V.tensor_tensor(out=macc3[:, cl, lo:lo + n], in0=o,
                                                             in1=sgx[:, cl, lo:lo + n], op=ALU.mult),
                              okey + [(sgk_, cl, sn)], [(mak, cl, sn)])
                    else:
                        sc.op("dve", lambda: V.tensor_tensor(out=tmpg3[:, cl, lo:lo + n], in0=o,
                                                             in1=sgx[:, cl, lo:lo + n], op=ALU.mult),
                              okey + [(sgk_, cl, sn)], [(tgk, cl, sn)])
                        if br == 1:
                            sc.op("dve", lambda: V.tensor_tensor(out=macc3[:, cl, lo:lo + n],
                                                                 in0=macc3[:, cl, lo:lo + n],
                                                                 in1=tmpg3[:, cl, lo:lo + n], op=ALU.add),
                                  [(mak, cl, sn)], [(mak, cl, sn)], sreads=[(tgk, cl, sn)])
                        else:
                            sc.op("dve", lambda: V.tensor_tensor(out=mT[:, c, lo:lo + n],
                                                                 in0=macc3[:, cl, lo:lo + n],
                                                                 in1=tmpg3[:, cl, lo:lo + n], op=ALU.add),
                                  [(mak, cl, sn)], cells("mT", [c], tl_), sreads=[(tgk, cl, sn)])
                proj_fm(slot, skey, 8, 2, rhs_t, cells(aname, achunks, range(5)), psegs, epi_y)

        if kstop <= 6 + 10 * p:
            return
        scr_reset()
        xt_list = [(i, 128 * i, 128) for i in range(4)] + [(4, TP, 64)]
        for (ti, tcol, rows) in xt_list:
            src = I.xp[p * 512 + ti * 128:p * 512 + ti * 128 + 128, :] if ti < 4 else I.xs[p * 64:(p + 1) * 64, :]
            sc.dma("sp", (lambda ti=ti, rows=rows, src=src: SP.dma_start(out=xres[0:rows, ti, :], in_=src)),
                   "xres%d" % ti, writes=[("xres", ti, n) for n in range(8)])
        for m_ in range(4):
            def epi_o(ti, rows, o, okeys, m_=m_):
                xk_ = [("xres", ti, 2 * m_), ("xres", ti, 2 * m_ + 1)]
                sc.op("dve", lambda: V.tensor_tensor(out=xres[0:rows, ti, m_ * 512:(m_ + 1) * 512],
                                                     in0=xres[0:rows, ti, m_ * 512:(m_ + 1) * 512], in1=o,
                                                     op=ALU.add),
                      okeys + xk_, xk_)
            proj_tm512(I.w_o, 0, m_ * 512, mT, lambda ti: cells("mT", range(16), [ti]), xt_list, epi_o)

        for (ti, tcol, rows) in xt_list:
            norm_T(xres[0:rows, ti, :], [("xres", ti, n) for n in range(8)], rows, 16,
                   (lambda tcol=tcol, rows=rows: hT[:, :, H + tcol:H + tcol + rows]), cells("hT", range(16), [ti]))

        if kstop <= 7 + 10 * p:
            return
        rl_ctr = [0]
        for r in range(4):
            for jb in range(8):
                slot, skey = wblock(I.w_up, 0, 16, r * 2048 + jb * 256)

                def epi_u(cl, sn, o, okey, jb=jb):
                    c = 2 * jb + cl
                    lo, n = (0, TP) if sn == "p" else (TP, TS)
                    tl_ = range(4) if sn == "p" else [4]
                    ri = rl_ctr[0] % 2
                    rl_ctr[0] += 1
                    rbuf = rlb[ri]
                    rk = [("rl", ri)]
                    sc.op("act", lambda: S.activation(out=rbuf[:, lo:lo + n], in_=o, func=AF.Relu), okey, rk)
                    sc.op("dve", lambda: V.tensor_tensor(out=qo[:, c, lo:lo + n], in0=rbuf[:, lo:lo + n],
                                                         in1=rbuf[:, lo:lo + n], op=ALU.mult),
                          rk, cells("qo", [c], tl_))
                proj_fm(slot, skey, 16, 2, hT, hkeys_all, hsegs, epi_u)
            for m_ in range(4):
                def epi_d(ti, rows, o, okeys, m_=m_):
                    xk_ = [("xres", ti, 2 * m_), ("xres", ti, 2 * m_ + 1)]
                    sc.op("dve", lambda: V.tensor_tensor(out=xres[0:rows, ti, m_ * 512:(m_ + 1) * 512],
                                                         in0=xres[0:rows, ti, m_ * 512:(m_ + 1) * 512], in1=o,
                                                         op=ALU.add),
                          okeys + xk_, xk_)
                proj_tm512(I.w_down, r * 2048, m_ * 512, qo, lambda ti: cells("qo", range(16), [ti]), xt_list, epi_d)

        sc.dma("sp", lambda: SP.dma_start(out=bc[:, :], in_=I.fin_g.partition_broadcast(128)), "bc0", writes=["bc"])
        def fin_tile(ti, tcol, rows):
            xk = [("xres", ti, n) for n in range(8)]
            i = xsb_ctr[0] % 2
            xsb_ctr[0] += 1
            xb = xsb[i]
            st, sk = stat()
            sc.op("dve", lambda: V.memset(st[:, 0:4], 0.0), writes=[sk])
            sc.op("act", lambda: S.activation(out=xb[0:rows, :], in_=xres[0:rows, ti, :], func=AF.Square,
                                              accum_out=st[0:rows, 0:1]), xk, [sk, ("xsb", i)])
            sc.op("act", lambda: S.activation(out=st[0:rows, 1:2], in_=st[0:rows, 0:1], func=AF.Sqrt,
                                              scale=1.0 / D, bias=epsb[0:rows, 0:1]), ["epsb"], [sk], sreads=[sk])
            sc.op("dve", lambda: V.reciprocal(out=st[0:rows, 2:3], in_=st[0:rows, 1:2]), [sk], [sk])
            sc.op("dve", lambda: V.scalar_tensor_tensor(out=xres[0:rows, ti, :], in0=xres[0:rows, ti, :],
                                                        scalar=st[0:rows, 2:3], in1=bc[0:rows, :],
                                                        op0=ALU.mult, op1=ALU.mult), xk + ["bc"], xk, sreads=[sk])
            dst = I.yp[p * 512 + ti * 128:p * 512 + ti * 128 + 128, :] if ti < 4 else I.ys[p * 64:(p + 1) * 64, :]
            sc.dma("sp", (lambda dst=dst, ti=ti, rows=rows: SP.dma_start(out=dst, in_=xres[0:rows, ti, :])),
                   "xres%d" % ti, reads=xk)
        for (ti, tcol, rows) in xt_list:
            fin_tile(ti, tcol, rows)

    with es:
        setup()
        if kstop > 0:
            mem_kv()
        for p in range(2):
            if kstop >= 2 + 10 * p:
                run_pass(p)
        if kstop < 99:
            dbg = nc.dram_tensor("dbg", [128, 2048], F32, kind="ExternalOutput").ap()
            dbt = bc
            sc.barrier()
            import os as _os
            sc.op("dve", lambda: V.memset(dbt[:, 0:2048], 0.0), [], ["dbt"])
        if kstop < 99 and not _os.environ.get("KNODBG"):
            sc.op("dve", lambda: V.tensor_copy(out=dbt[:, 0:64], in_=stats[:, 0:64]), [], ["dbt"])
            sc.op("dve", lambda: V.tensor_copy(out=dbt[:, 64:192], in_=hT[:, 0, 0:128]), [], ["dbt"])
            if kstop >= 3:
                sc.op("dve", lambda: V.tensor_copy(out=dbt[:, 1024:1264].rearrange("p (b r) -> p b r", b=8),
                                                   in_=cS[:, 3, :, 0:30]), [], ["dbt"])
                sc.op("dve", lambda: V.tensor_copy(out=dbt[:, 1264:1504].rearrange("p (b r) -> p b r", b=8),
                                                   in_=cS[:, 7, :, 0:30]), [], ["dbt"])
            sc.op("dve", lambda: V.tensor_copy(out=dbt[:, 192:320], in_=wring[0][:, 0, 0:128]), [], ["dbt"])
            sc.op("dve", lambda: V.tensor_copy(out=dbt[:, 320:448], in_=KT[:, 0, 0:128]), [], ["dbt"])
            sc.op("dve", lambda: V.tensor_copy(out=dbt[:, 448:576], in_=xres_flat[:, 4096:4096 + 128]), [], ["dbt"])
            sc.op("dve", lambda: V.tensor_copy(out=dbt[:, 576:696], in_=pp[:, 0:120]), [], ["dbt"])
            sc.op("dve", lambda: V.tensor_copy(out=dbt[:, 704:832], in_=xsb[1][:, 0:128]), [], ["dbt"])
            sc.op("dve", lambda: V.tensor_copy(out=dbt[:, 192:320], in_=ident_bf[:, 0:128]), [], ["dbt"])
            sc.op("dve", lambda: V.tensor_copy(out=dbt[:, 320:448], in_=TBw[:, 0:128]), [], ["dbt"])
            sc.op("dve", lambda: V.tensor_copy(out=dbt[:, 832:864], in_=cw[:, 1, 0:32]), [], ["dbt"])
            sc.op("dve", lambda: V.tensor_copy(out=dbt[0:64, 864:928], in_=WsTs[:, 1, 0:64]), [], ["dbt"])
            sc.op("dve", lambda: V.tensor_copy(out=dbt[:, 928:1024], in_=ident_f[:, 0:96]), [], ["dbt"])
        if kstop < 99:
            sc.barrier()
            sc.dma("sp", lambda: SP.dma_start(out=dbg[:, :], in_=dbt[:, 0:2048]), "dbg")
        block = es.enter_context(nc.Block())
        sc.emit(block)
    nc._io_names = (list(I._t), )
    return nc, sc


_CACHE = {}


def kernel(x_prompt, x_sample, mem_prompt, cache_mem_k, cache_mem_v, state_conv,
           attn_norm_g, w_in, b_gate, sgu_ln_g, sgu_ln_b, sgu_w, sgu_b, w_a_out,
           conv_w, conv_b, conv_ln_g, conv_ln_b, w_b_out, mem_norm_g, w_mem_kv,
           w_c_out, w_o, mlp_norm_g, w_up, w_down, final_norm_g):
    f = lambda a: np.ascontiguousarray(np.asarray(a, dtype=np.float32))
    if "nc" not in _CACHE:
        _CACHE["nc"] = build_program()[0]
    nc = _CACHE["nc"]
    x_prompt = f(x_prompt); x_sample = f(x_sample); mem_prompt = f(mem_prompt)
    ckf = f(cache_mem_k)[0].reshape(128, 256, 1024)
    cvf = f(cache_mem_v)[0].reshape(128, 256, 1024)
    scf = f(state_conv)[0]
    shared = {
        "attn_g": f(attn_norm_g)[0].reshape(16, 128), "mlp_g": f(mlp_norm_g)[0].reshape(16, 128),
        "mem_g": f(mem_norm_g)[0].reshape(16, 128), "w_in": f(w_in)[0], "b_gate": f(b_gate)[0].reshape(48, 128),
        "sgu_ln_g": f(sgu_ln_g)[0], "sgu_ln_b": f(sgu_ln_b)[0], "sgu_w": f(sgu_w)[0], "sgu_b": f(sgu_b)[0],
        "w_a": f(w_a_out)[0], "conv_w": f(conv_w)[0], "conv_b": f(conv_b)[0].reshape(8, 128),
        "conv_ln_g": f(conv_ln_g)[0].reshape(8, 128), "conv_ln_b": f(conv_ln_b)[0].reshape(8, 128),
        "w_b": f(w_b_out)[0], "w_kv": f(w_mem_kv)[0], "w_c": f(w_c_out)[0], "w_o": f(w_o)[0],
        "w_up": f(w_up)[0], "w_down": f(w_down)[0], "fin_g": f(final_norm_g),
    }
    in_maps = []
    for c in range(8):
        b, half = c // 2, c % 2
        m = dict(shared)
        m["xp"] = np.ascontiguousarray(x_prompt[b, half * 1024:(half + 1) * 1024])
        m["xh"] = (np.ascontiguousarray(x_prompt[b, 1024 - H:1024]) if half == 1
                   else np.zeros((H, D), np.float32))
        m["xs"] = np.ascontiguousarray(x_sample[c * 16:(c + 1) * 16].reshape(128, D))
        m["mem"] = np.ascontiguousarray(mem_prompt[b])
        m["ck"] = np.ascontiguousarray(ckf[c * 16:(c + 1) * 16])
        m["cv"] = np.ascontiguousarray(cvf[c * 16:(c + 1) * 16])
        m["scs"] = np.ascontiguousarray(scf[c * 16:(c + 1) * 16])
        in_maps.append(m)
    res = run_bass_kernel_spmd(nc, in_maps, core_ids=list(range(8)))
    R = res.results
    y_prompt = np.zeros((4, 2048, D), np.float32)
    y_sample = np.zeros((128, 8, D), np.float32)
    nk = np.zeros((1, 4, 256, 4, 256), np.float32)
    nv = np.zeros((1, 4, 256, 4, 256), np.float32)
    ncp = np.zeros((1, 4, 30, 1024), np.float32)
    ncs = np.zeros((1, 128, 30, 1024), np.float32)
    ncv = np.zeros((1, 128, 8, 1024), np.float32)
    for c in range(8):
        b, half = c // 2, c % 2
        y_prompt[b, half * 1024:(half + 1) * 1024] = R[c]["yp"]
        y_sample[c * 16:(c + 1) * 16] = R[c]["ys"].reshape(16, 8, D)
        if half == 0:
            nk[0, b] = R[c]["mk"].reshape(256, 4, 256)
            nv[0, b] = R[c]["mv"].reshape(256, 4, 256)
        else:
            ncp[0, b] = R[c]["csp"]
        ncs[0, c * 16:(c + 1) * 16] = R[c]["css"]
        ncv[0, c * 16:(c + 1) * 16] = R[c]["cvs"].reshape(16, 8, 1024)
    return (y_prompt, y_sample, nk, nv, ncp, ncs, ncv)
```
